# Optimizing a Trainium2 kernel written in Bass

```python
import math
import jax, jax.numpy as jnp
from jax import lax
import numpy as np

D_MODEL = 1024
BATCH = 8
SEQ = 2048
DEPTH = 1
DEC_BATCH = 128
DEC_SEQ = 4
PAST_LEN = 16384
PAGE_SIZE = 128

A_HEADS = 4
A_HEAD_DIM = D_MODEL // (2 * A_HEADS)
A_WIDTH = A_HEADS * A_HEAD_DIM
CHUNK = 128
SSM_WIDTH = D_MODEL - A_WIDTH
SSM_GROUP = 16
SSM_GROUPS = SSM_WIDTH // SSM_GROUP
SSM_STATE = 64
DT_MIN = 1e-3
DT_MAX = 1e-1
D_FF = 11 * D_MODEL // 4
CONV_W = 3
EPS = 1e-6

kernel_name = 'hymba_sgu_s5_convffn_step'


def _rmsnorm(x, g):
    xf = x.astype(jnp.float32)
    xf = xf * lax.rsqrt(jnp.mean(xf * xf, axis=-1, keepdims=True) + EPS)
    return xf.astype(x.dtype) * g


def _chunk_sgu(u, v, w_s, b_s):
    bt, L, H, Dh = v.shape
    w = jnp.tril(w_s)
    if L <= CHUNK:
        s = jnp.einsum('hts,bshd->bthd', w[:, :L, :L], v) + jnp.transpose(b_s[:, :L])[None, :, :, None]
    else:
        n = -(-L // CHUNK)
        vp = jnp.pad(v, ((0, 0), (0, n * CHUNK - L), (0, 0), (0, 0))).reshape(bt, n, CHUNK, H, Dh)
        s = jnp.einsum('hts,bnshd->bnthd', w, vp) + jnp.transpose(b_s)[None, None, :, :, None]
        s = s.reshape(bt, n * CHUNK, H, Dh)[:, :L]
    return u * s


def _cplx_combine(e1, e2):
    a1r, a1i, b1r, b1i = e1
    a2r, a2i, b2r, b2i = e2
    return (a2r * a1r - a2i * a1i,
            a2r * a1i + a2i * a1r,
            a2r * b1r - a2i * b1i + b2r,
            a2r * b1i + a2i * b1r + b2i)


def _ssm_scan(xs, h0_re, h0_im, lam_re, lam_im, log_dt, b_re, b_im, c_re, c_im, d_skip):
    bt, L, _ = xs.shape
    f32 = jnp.float32
    xf = xs.astype(f32).reshape(bt, L, SSM_GROUPS, SSM_GROUP)
    lr = lam_re.astype(f32)
    li = lam_im.astype(f32)
    dt = jnp.exp(log_dt.astype(f32))[:, None]
    mag = jnp.exp(lr * dt)
    ar = mag * jnp.cos(li * dt)
    ai = mag * jnp.sin(li * dt)
    den = lr * lr + li * li
    fr = ((ar - 1.0) * lr + ai * li) / den
    fi = (ai * lr - (ar - 1.0) * li) / den
    bre = b_re.astype(f32)
    bim = b_im.astype(f32)
    bbr = fr[..., None] * bre - fi[..., None] * bim
    bbi = fr[..., None] * bim + fi[..., None] * bre
    ur = jnp.einsum('blgc,gpc->blgp', xf, bbr)
    ui = jnp.einsum('blgc,gpc->blgp', xf, bbi)
    h0r = h0_re.astype(f32)
    h0i = h0_im.astype(f32)
    ur = ur.at[:, 0].add(ar * h0r - ai * h0i)
    ui = ui.at[:, 0].add(ar * h0i + ai * h0r)
    a_r = jnp.broadcast_to(ar, ur.shape)
    a_i = jnp.broadcast_to(ai, ui.shape)
    _, _, hr, hi = lax.associative_scan(_cplx_combine, (a_r, a_i, ur, ui), axis=1)
    y = (jnp.einsum('blgp,gcp->blgc', hr, c_re.astype(f32))
         - jnp.einsum('blgp,gcp->blgc', hi, c_im.astype(f32))
         + d_skip.astype(f32).reshape(SSM_GROUPS, SSM_GROUP) * xf)
    return (y.reshape(bt, L, SSM_WIDTH).astype(xs.dtype),
            hr[:, -1].astype(h0_re.dtype), hi[:, -1].astype(h0_im.dtype))


def _layer(x, h0_re, h0_im, conv_buf, p):
    bt, L, _ = x.shape
    n1 = _rmsnorm(x, p['g_mix'])
    proj = n1 @ p['w_in']
    u, v, xs = jnp.split(proj, [A_WIDTH, 2 * A_WIDTH], axis=-1)
    u = u.reshape(bt, L, A_HEADS, A_HEAD_DIM)
    v = _rmsnorm(v.reshape(bt, L, A_HEADS, A_HEAD_DIM), p['g_v'])
    a_out = _chunk_sgu(u, v, p['w_s'], p['b_s']).reshape(bt, L, A_WIDTH)
    ys, hr, hi = _ssm_scan(xs, h0_re, h0_im, p['lam_re'], p['lam_im'], p['log_dt'],
                           p['b_re'], p['b_im'], p['c_re'], p['c_im'], p['d_skip'])
    g = jax.nn.gelu(ys)
    b_out = g * jax.nn.sigmoid(g @ p['w_glu'] + p['b_glu'])
    mix = jnp.concatenate([_rmsnorm(a_out, p['g_out_a']), _rmsnorm(b_out, p['g_out_b'])], axis=-1)
    h = x + mix @ p['w_out']
    n2 = _rmsnorm(h, p['g_ffn'])
    up = n2 @ p['w_ffn_in']
    padded = jnp.concatenate([conv_buf.astype(up.dtype), up], axis=1)
    conv = p['conv_b'] + sum(p['conv_w'][k] * padded[:, k:k + L] for k in range(CONV_W))
    gate, val = jnp.split(conv, 2, axis=-1)
    h = h + (jax.nn.gelu(gate) * val) @ p['w_ffn_out']
    y = _rmsnorm(h, p['g_final'])
    return y, v, hr, hi, padded[:, -(CONV_W - 1):]


def setup_inputs(seed: int = 0) -> dict:
    key = jax.random.key(seed)
    ks = jax.random.split(key, 32)
    nrm = lambda k, s, sc: jax.random.normal(k, s, jnp.float32) * sc
    d_in = 2 * A_WIDTH + SSM_WIDTH
    lam_im = jnp.float32(math.pi) * jnp.arange(SSM_STATE, dtype=jnp.float32)[None, :] + nrm(ks[11], (SSM_GROUPS, SSM_STATE), 0.01)
    return {
        'x_prompt': nrm(ks[0], (BATCH, SEQ, D_MODEL), 1.0),
        'x_sample': nrm(ks[1], (DEC_BATCH, DEC_SEQ, D_MODEL), 1.0),
        'state_ssm_re': nrm(ks[2], (DEC_BATCH, SSM_GROUPS, SSM_STATE), 0.1),
        'state_ssm_im': nrm(ks[3], (DEC_BATCH, SSM_GROUPS, SSM_STATE), 0.1),
        'state_conv': nrm(ks[4], (DEC_BATCH, CONV_W - 1, 2 * D_FF), 1.0),
        'g_mix': 1.0 + nrm(ks[5], (D_MODEL,), 0.01),
        'w_in': nrm(ks[6], (D_MODEL, d_in), D_MODEL ** -0.5),
        'g_v': 1.0 + nrm(ks[7], (A_HEADS, A_HEAD_DIM), 0.01),
        'w_s': nrm(ks[8], (A_HEADS, CHUNK, CHUNK), CHUNK ** -0.5),
        'b_s': 1.0 + nrm(ks[9], (A_HEADS, CHUNK), 0.01),
        'lam_re': -0.5 + nrm(ks[10], (SSM_GROUPS, SSM_STATE), 0.01),
        'lam_im': lam_im,
        'log_dt': jax.random.uniform(ks[12], (SSM_GROUPS,), jnp.float32, math.log(DT_MIN), math.log(DT_MAX)),
        'b_re': nrm(ks[13], (SSM_GROUPS, SSM_STATE, SSM_GROUP), (2 * SSM_GROUP) ** -0.5),
        'b_im': nrm(ks[14], (SSM_GROUPS, SSM_STATE, SSM_GROUP), (2 * SSM_GROUP) ** -0.5),
        'c_re': nrm(ks[15], (SSM_GROUPS, SSM_GROUP, SSM_STATE), (2 * SSM_STATE) ** -0.5),
        'c_im': nrm(ks[16], (SSM_GROUPS, SSM_GROUP, SSM_STATE), (2 * SSM_STATE) ** -0.5),
        'd_skip': nrm(ks[17], (SSM_WIDTH,), 1.0),
        'w_glu': nrm(ks[18], (SSM_WIDTH, SSM_WIDTH), SSM_WIDTH ** -0.5),
        'b_glu': nrm(ks[19], (SSM_WIDTH,), 0.01),
        'g_out_a': 1.0 + nrm(ks[20], (A_WIDTH,), 0.01),
        'g_out_b': 1.0 + nrm(ks[21], (SSM_WIDTH,), 0.01),
        'w_out': nrm(ks[22], (A_WIDTH + SSM_WIDTH, D_MODEL), (A_WIDTH + SSM_WIDTH) ** -0.5),
        'g_ffn': 1.0 + nrm(ks[23], (D_MODEL,), 0.01),
        'w_ffn_in': nrm(ks[24], (D_MODEL, 2 * D_FF), D_MODEL ** -0.5),
        'conv_w': nrm(ks[25], (CONV_W, 2 * D_FF), CONV_W ** -0.5),
        'conv_b': nrm(ks[26], (2 * D_FF,), 0.01),
        'w_ffn_out': nrm(ks[27], (D_FF, D_MODEL), D_FF ** -0.5),
        'g_final': 1.0 + nrm(ks[28], (D_MODEL,), 0.01),
    }


def reference(x_prompt, x_sample, state_ssm_re, state_ssm_im, state_conv,
              g_mix, w_in, g_v, w_s, b_s, lam_re, lam_im, log_dt, b_re, b_im, c_re, c_im,
              d_skip, w_glu, b_glu, g_out_a, g_out_b, w_out, g_ffn, w_ffn_in, conv_w, conv_b,
              w_ffn_out, g_final):
    p = dict(g_mix=g_mix, w_in=w_in, g_v=g_v, w_s=w_s, b_s=b_s, lam_re=lam_re, lam_im=lam_im,
             log_dt=log_dt, b_re=b_re, b_im=b_im, c_re=c_re, c_im=c_im, d_skip=d_skip,
             w_glu=w_glu, b_glu=b_glu, g_out_a=g_out_a, g_out_b=g_out_b, w_out=w_out,
             g_ffn=g_ffn, w_ffn_in=w_ffn_in, conv_w=conv_w, conv_b=conv_b,
             w_ffn_out=w_ffn_out, g_final=g_final)
    h = x_prompt
    s_re = jnp.zeros((BATCH, SSM_GROUPS, SSM_STATE), state_ssm_re.dtype)
    s_im = jnp.zeros((BATCH, SSM_GROUPS, SSM_STATE), state_ssm_im.dtype)
    c_buf = jnp.zeros((BATCH, CONV_W - 1, 2 * D_FF), state_conv.dtype)
    for _ in range(DEPTH):
        h, _v_p, s_re, s_im, c_buf = _layer(h, s_re, s_im, c_buf, p)
    y_prompt, ssm_re_prompt, ssm_im_prompt, conv_prompt = h, s_re, s_im, c_buf
    h = x_sample
    s_re, s_im, c_buf = state_ssm_re, state_ssm_im, state_conv
    for _ in range(DEPTH):
        h, v_sample, s_re, s_im, c_buf = _layer(h, s_re, s_im, c_buf, p)
    y_sample = h
    return (y_prompt, y_sample, v_sample, ssm_re_prompt, ssm_im_prompt, conv_prompt, s_re, s_im, c_buf)
```

```python
import contextlib
import math
import numpy as np
import concourse.bass as bass
import concourse.mybir as mybir
from concourse.bass_utils import run_bass_kernel_spmd

F32 = mybir.dt.float32
BF16 = mybir.dt.bfloat16
I32 = mybir.dt.int32
AF = mybir.ActivationFunctionType
ALU = mybir.AluOpType
AX = mybir.AxisListType

NCORES = 8
import os as _os
F_POW = bool(int(_os.environ.get("F_POW", "0")))
F_SCANPOOL = bool(int(_os.environ.get("F_SCANPOOL", "0")))
F_CONVOLD = bool(int(_os.environ.get("F_CONVOLD", "0")))
F_TSACT = bool(int(_os.environ.get("F_TSACT", "0")))
D = 1024
DFF = 2816
EPS = 1e-6
TWO_PI = 2.0 * math.pi


class Buf:
    __slots__ = ("name", "last_w", "readers")

    def __init__(self, name=""):
        self.name = name
        self.last_w = None
        self.readers = []


class Op:
    __slots__ = ("eng", "kind", "emit", "deps", "marked", "seq", "dsem", "dval", "idx")

    def __init__(self, eng, kind, emit):
        self.eng = eng
        self.kind = kind
        self.emit = emit
        self.deps = []
        self.marked = False
        self.seq = 0
        self.dsem = None
        self.dval = 0


ENGS = ("pe", "act", "dve", "pool", "sp")


class Prog:
    def __init__(self, nc, n_dma_sems=12):
        self.nc = nc
        self.ops = {e: [] for e in ENGS}
        self.all_ops = []
        self.n_dma_sems = n_dma_sems
        self.dma_count = {e: 0 for e in ENGS}
        self.out_dmas = []
        self.since_barrier = []
        self.last_on_sem = {}

    def _add(self, op, reads, writes):
        deps = []
        for b in reads:
            if b.last_w is not None:
                deps.append(b.last_w)
        for b in writes:
            if b.last_w is not None:
                deps.append(b.last_w)
            deps.extend(b.readers)
        seen = set()
        for d in deps:
            if d is op or id(d) in seen:
                continue
            seen.add(id(d))
            if op.eng == "pe" and d.eng == "pe" and d.kind == "c" and op.kind == "c":
                continue
            op.deps.append(d)
            d.marked = True
        for b in reads:
            if op.kind == "c":
                b.readers = [r for r in b.readers if not (r.kind == "c" and r.eng == op.eng)]
            b.readers.append(op)
        for b in writes:
            b.last_w = op
            b.readers = []
        op.idx = len(self.all_ops)
        self.all_ops.append(op)
        self.ops[op.eng].append(op)
        self.since_barrier.append(op)
        return op

    def op(self, eng, emit, reads=(), writes=()):
        return self._add(Op(eng, "c", emit), reads, writes)

    def dma(self, queue, emit, reads=(), writes=(), is_output=False):
        o = Op(queue, "d", emit)
        k = self.dma_count[queue]
        self.dma_count[queue] = k + 1
        o.dsem = k % self.n_dma_sems
        o.dval = 16 * (k // self.n_dma_sems + 1)
        prev = self.last_on_sem.get((queue, o.dsem))
        if prev is not None:
            o.deps.append(prev)
        self.last_on_sem[(queue, o.dsem)] = o
        self._add(o, reads, writes)
        if is_output:
            self.out_dmas.append(o)
        return o

    def barrier(self, exclude=()):
        lasts = []
        for e in ENGS:
            lastc = None
            for o in self.ops[e]:
                if o.kind == "c":
                    lastc = o
            if lastc is not None:
                lasts.append(lastc)
        dmas = [o for o in self.since_barrier if o.kind == "d" and id(o) not in exclude]
        self.since_barrier = []
        for e in ("pe", "act", "dve", "pool", "sp"):
            o = Op(e, "c", lambda eng: eng.nop())
            for d in lasts + dmas:
                o.deps.append(d)
                d.marked = True
            o.idx = len(self.all_ops)
            self.all_ops.append(o)
            self.ops[e].append(o)

    def build(self):
        nc = self.nc
        with contextlib.ExitStack() as st:
            esem = {}
            for e in ("pe", "act", "dve", "pool", "sp"):
                esem[e] = st.enter_context(nc.semaphore("s_" + e))
            dsem = {}
            for q in ENGS:
                if self.dma_count[q] > 0:
                    dsem[q] = [
                        st.enter_context(nc.semaphore("d_%s_%d" % (q, i)))
                        for i in range(min(self.n_dma_sems, self.dma_count[q]))
                    ]
            for e in ENGS:
                c = 0
                for o in self.ops[e]:
                    if o.kind == "c" and o.marked:
                        c += 1
                        o.seq = c
            import os
            if os.environ.get("KDEBUG"):
                for e in ENGS:
                    print("ENG", e, "ops", len(self.ops[e]), "marked", max([o.seq for o in self.ops[e]] + [0]), "dmas", self.dma_count[e])
            block = st.enter_context(nc.Block())

            def run(ename, eng):
                waited = {}
                for o in self.ops[ename]:
                    for d in o.deps:
                        if d.kind == "c":
                            key = ("c", d.eng)
                            val = d.seq
                            sem = esem[d.eng]
                        else:
                            key = ("d", d.eng, d.dsem)
                            val = d.dval
                            sem = dsem[d.eng][d.dsem]
                        if waited.get(key, 0) >= val:
                            continue
                        waited[key] = val
                        eng.wait_ge(sem, val)
                    ins = o.emit(eng)
                    if o.kind == "c":
                        if o.marked:
                            ins.then_inc(esem[ename], 1)
                    else:
                        ins.then_inc(dsem[ename][o.dsem], 16)
                fin = {}
                for o in self.out_dmas:
                    if o.eng == ename:
                        fin[o.dsem] = max(fin.get(o.dsem, 0), o.dval)
                for s, v in fin.items():
                    eng.wait_ge(dsem[ename][s], v)

            @block.sync
            def _(e):
                run("sp", e)

            @block.tensor
            def _(e):
                run("pe", e)

            @block.scalar
            def _(e):
                run("act", e)

            @block.vector
            def _(e):
                run("dve", e)

            @block.gpsimd
            def _(e):
                run("pool", e)


def build_nc():
    nc = bass.Bass("TRN2", target_bir_lowering=False)
    P = Prog(nc)

    def din(name, shape):
        return nc.dram_tensor(name, list(shape), F32, kind="ExternalInput").ap()

    def dout(name, shape):
        return nc.dram_tensor(name, list(shape), F32, kind="ExternalOutput").ap()

    xp = din("xp", [2048, D]); xs = din("xs", [64, D])
    sre = din("sre", [16, 32, 64]); sim = din("sim", [16, 32, 64]); scv = din("scv", [32, 2 * DFF])
    g_mix = din("g_mix", [D]); w_in = din("w_in", [D, 1536]); g_v = din("g_v", [512])
    w_s = din("w_s", [4, 128, 128]); b_s = din("b_s", [4, 128])
    lam_re = din("lam_re", [32, 64]); lam_im = din("lam_im", [32, 64]); log_dt = din("log_dt", [32])
    b_re = din("b_re", [32, 64, 16]); b_im = din("b_im", [32, 64, 16])
    c_re = din("c_re", [512, 64]); c_im = din("c_im", [512, 64]); d_skip = din("d_skip", [512])
    w_glu = din("w_glu", [512, 512]); b_glu = din("b_glu", [512])
    g_out_a = din("g_out_a", [512]); g_out_b = din("g_out_b", [512])
    w_out = din("w_out", [D, D]); g_ffn = din("g_ffn", [D]); w_ffn_in = din("w_ffn_in", [D, 2 * DFF])
    conv_w = din("conv_w", [3, 2 * DFF]); conv_b = din("conv_b", [2 * DFF])
    w_ffn_out = din("w_ffn_out", [DFF, D]); g_final = din("g_final", [D])
    c_ident = din("c_ident", [128, 128]); c_mtoep = din("c_mtoep", [128, 128]); c_msgu = din("c_msgu", [128, 128])
    c_nvec = din("c_nvec", [128, 32]); c_sigma = din("c_sigma", [128, 1]); c_kvec = din("c_kvec", [128, 64])

    yp = dout("yp", [2048, D]); ys = dout("ys", [64, D]); vs = dout("vs", [64, 512])
    srep = dout("srep", [32, 64]); simp = dout("simp", [32, 64]); cvp = dout("cvp", [2, 2 * DFF])
    sres = dout("sres", [16, 32, 64]); sims = dout("sims", [16, 32, 64]); cvs = dout("cvs", [32, 2 * DFF])
    h1d = nc.dram_tensor("h1d", [2112, D], F32).ap()
    zscr = nc.dram_tensor("zscr", [2112, 512], BF16).ap()
    b_zscr = Buf()

    ES = contextlib.ExitStack()
    with ES:
        def sb(name, shape, dt=F32, stack=ES):
            return stack.enter_context(nc.sbuf_tensor(name, list(shape), dt))

        banks = [ES.enter_context(nc.psum_tensor("bank%d" % i, [128, 512], F32)) for i in range(8)]
        bbufs = [Buf("bank%d" % i) for i in range(8)]
        bctr = [0]

        def nb():
            i = bctr[0] % 8
            bctr[0] += 1
            return banks[i], bbufs[i]

        def mm(out, lhsT, rhs, start, stop, reads, writes):
            P.op("pe", lambda e: e.matmul(out, lhsT, rhs, start=start, stop=stop), reads, writes)

        def tr(out, in_, ident, reads, writes):
            P.op("pe", lambda e: e.transpose(out, in_, ident), reads, writes)

        def act(out, in_, func, reads, writes, **kw):
            P.op("act", lambda e: e.activation(out=out, in_=in_, func=func, **kw), reads, writes)

        def tt(eng, out, in0, in1, op, reads, writes):
            P.op(eng, lambda e: e.tensor_tensor(out=out, in0=in0, in1=in1, op=op), reads, writes)

        def ts(eng, out, in0, s1, s2, op0, op1, reads, writes):
            if s2 is None:
                P.op(eng, lambda e: e.tensor_scalar(out=out, in0=in0, scalar1=s1, scalar2=None, op0=op0), reads, writes)
            else:
                P.op(eng, lambda e: e.tensor_scalar(out=out, in0=in0, scalar1=s1, scalar2=s2, op0=op0, op1=op1), reads, writes)

        def stt(out, in0, scalar, in1, op0, op1, reads, writes):
            P.op("dve", lambda e: e.scalar_tensor_tensor(out=out, in0=in0, scalar=scalar, in1=in1, op0=op0, op1=op1), reads, writes)

        def cp(eng, out, in_, reads, writes):
            if eng == "act":
                P.op(eng, lambda e: e.activation(out=out, in_=in_, func=AF.Copy), reads, writes)
            else:
                P.op(eng, lambda e: e.tensor_copy(out=out, in_=in_), reads, writes)

        def dma(q, out, in_, reads, writes, is_output=False, **kw):
            P.dma(q, lambda e: e.dma_start(out=out, in_=in_, **kw), reads, writes, is_output=is_output)

        identf = sb("identf", [128, 128]); b_identf = Buf()
        identb = sb("identb", [128, 128], BF16); b_identb = Buf()
        onesb = sb("onesb", [128, 128], BF16); b_ones = Buf()
        epst = sb("epst", [128, 1]); b_eps = Buf()
        mhalf = sb("mhalf", [128, 512]); b_mhalf = Buf()
        sigma = sb("sigma", [128, 1]); b_sigma = Buf()
        ARR8 = sb("ARR8", [128, 2, 32]); AII8 = sb("AII8", [128, 2, 32]); b_A8 = Buf()
        AR4 = sb("AR4", [128, 32]); AI4 = sb("AI4", [128, 32]); b_A4 = Buf()
        ST = [sb("ST%d" % i, [128, 3, 32]) for i in range(2)]
        b_ST = [Buf(), Buf()]
        SM = contextlib.ExitStack()
        SM.__enter__()
        Toep = sb("Toep", [128, 32, 128], BF16, SM); b_Toep = Buf()
        MinX = sb("MinX", [128, 32, 192], BF16, SM); b_MinX = Buf()
        Mout = sb("Mout", [128, 32, 128], BF16, SM); b_Mout = Buf()
        w_in_sb = sb("w_in_sb", [128, 8, 1536], BF16, SM); b_win = Buf()
        w_glu_sb = sb("w_glu_sb", [128, 4, 512], BF16, SM); b_wglu = Buf()
        w_out_sb = sb("w_out_sb", [128, 8, 1024], BF16, SM); b_wout = Buf()
        TC = sb("TC", [128, 32, 64], F32, SM); TSs = sb("TSs", [128, 32, 64], F32, SM); b_tab = Buf()
        MG8 = sb("MG8", [128, 32], F32, SM); b_mg8 = Buf()
        C1 = sb("C1", [128, 32], F32, SM); C2 = sb("C2", [128, 32], F32, SM); b_C = Buf()
        th8c = sb("th8c", [128, 32], F32, SM); b_th8c = Buf()
        wiv = w_in.rearrange("(a p) n -> p a n", p=128)
        n_pref0 = len(P.all_ops)
        for a in range(8):
            dma("pool", w_in_sb[:, a, 0:1024], wiv[:, a, 0:1024], [], [b_win])
        dma("pool", w_glu_sb[:], w_glu.rearrange("(a p) n -> p a n", p=128), [], [b_wglu])
        wov = w_out.rearrange("(a p) n -> p a n", p=128)
        P.op("pool", lambda e: e.memset(C1[:], 0.0), [], [b_C])
        P.op("pool", lambda e: e.memset(C2[:], 0.0), [], [b_C])

        dma("sp", identf[:], c_ident, [], [b_identf])
        dma("sp", sigma[:], c_sigma, [], [b_sigma])
        cp("dve", identb[:], identf[:], [b_identf], [b_identb])
        P.op("pool", lambda e: e.memset(onesb[:], 1.0), [], [b_ones])
        P.op("pool", lambda e: e.memset(epst[:], EPS), [], [b_eps])
        P.op("pool", lambda e: e.memset(mhalf[:], -0.5), [], [b_mhalf])
        P.op("pool", lambda e: e.memset(ST[0][:], 0.0), [], [b_ST[0]])

        def rstd(out, ssum, n, pt, width, reads, writes, tmp, b_tmp):
            if F_POW:
                ts("dve", tmp, ssum, 1.0 / n, EPS, ALU.mult, ALU.add, reads, [b_tmp])
                tt("pool", out, tmp, mhalf[0:pt, 0:width], ALU.pow, [b_tmp, b_mhalf], writes)
                return
            act(tmp, ssum, AF.Sqrt, list(reads) + [b_eps], [b_tmp], scale=1.0 / n, bias=epst[0:pt, :])
            P.op("dve", lambda e: e.reciprocal(out, tmp), [b_tmp], writes)

        with contextlib.ExitStack() as S0:
            def sb0(name, shape, dt=F32):
                return sb(name, shape, dt, S0)

            wxs = sb0("wxs", [128, 8, 512], BF16); b_wxs = Buf()
            dma("pool", wxs[:], wiv[:, :, 1024:1536], [], [b_wxs])
            for a in range(8):
                dma("pool", w_out_sb[:, a, :], wov[:, a, :], [], [b_wout])
            n_pref1 = len(P.all_ops)
            cp("act", w_in_sb[:, :, 1024:1536].rearrange("p a (c g) -> p a c g", g=32),
               wxs[:].rearrange("p a (g c) -> p a c g", c=16), [b_wxs], [b_win])
            mtoep = sb0("mtoep", [128, 128]); b_mtoep = Buf()
            nvec = sb0("nvec", [128, 32]); b_nvec = Buf()
            dma("sp", mtoep[:], c_mtoep, [], [b_mtoep])
            dma("sp", nvec[:], c_nvec, [], [b_nvec])
            L2 = sb0("L2", [32, 256]); b_L2 = Buf()
            dma("sp", L2[:, 0:64], lam_re, [], [b_L2]); dma("sp", L2[:, 64:128], lam_re, [], [b_L2])
            dma("sp", L2[:, 128:192], lam_im, [], [b_L2]); dma("sp", L2[:, 192:256], lam_im, [], [b_L2])
            lr = sb0("lr", [128, 32]); li = sb0("li", [128, 32]); b_l = Buf()
            bk, bb = nb()
            tr(bk[:, 0:32], L2[:, 0:128], identf[0:32, 0:32], [b_L2, b_identf], [bb])
            tr(bk[:, 32:64], L2[:, 128:256], identf[0:32, 0:32], [b_L2, b_identf], [bb])
            cp("dve", lr[:], bk[:, 0:32], [bb], [b_l]); cp("dve", li[:], bk[:, 32:64], [bb], [b_l])
            dtb = sb0("dtb", [128, 32]); b_dt = Buf()
            dma("sp", dtb[:], log_dt.partition_broadcast(128), [], [b_dt])
            act(dtb[:], dtb[:], AF.Exp, [b_dt], [b_dt])
            rho = sb0("rho", [128, 32]); th = sb0("th", [128, 32]); b_rt = Buf()
            tt("dve", rho[:], lr[:], dtb[:], ALU.mult, [b_l, b_dt], [b_rt])
            tt("dve", th[:], li[:], dtb[:], ALU.mult, [b_l, b_dt], [b_rt])
            NQ = 32
            shp = [128, 32, NQ]
            ARG = sb0("ARG", shp); RHO = sb0("RHO", shp); b_arg = Buf()
            nv_b = nvec[:].unsqueeze(1).broadcast_to(shp)
            tt("dve", ARG[:], th[:].unsqueeze(2).broadcast_to(shp), nv_b, ALU.mult, [b_rt, b_nvec], [b_arg])
            tt("dve", RHO[:], rho[:].unsqueeze(2).broadcast_to(shp), nv_b, ALU.mult, [b_rt, b_nvec], [b_arg])
            mag = RHO; b_mag = Buf()
            act(mag[:], RHO[:], AF.Exp, [b_arg], [b_mag, b_arg])
            ki = sb0("ki", shp, I32); kf = sb0("kf", shp); rr = sb0("rr", shp); mk = sb0("mk", shp); b_red = Buf()
            sinv = sb0("sinv", shp); cosv = sb0("cosv", shp); b_sc = Buf()

            def sin_of(outt, shift):
                ts("dve", rr[:], ARG[:], shift, None, ALU.add, None, [b_arg], [b_red])
                ts("dve", kf[:], rr[:], 1.0 / TWO_PI, None, ALU.mult, None, [b_red], [b_red])
                cp("dve", ki[:], kf[:], [b_red], [b_red])
                cp("dve", kf[:], ki[:], [b_red], [b_red])
                stt(rr[:], kf[:], -TWO_PI, rr[:], ALU.mult, ALU.add, [b_red], [b_red])
                ts("dve", mk[:], rr[:], math.pi, None, ALU.is_gt, None, [b_red], [b_red])
                stt(rr[:], mk[:], -TWO_PI, rr[:], ALU.mult, ALU.add, [b_red], [b_red])
                ts("dve", mk[:], rr[:], -math.pi, None, ALU.is_lt, None, [b_red], [b_red])
                stt(rr[:], mk[:], TWO_PI, rr[:], ALU.mult, ALU.add, [b_red], [b_red])
                ts("dve", rr[:], rr[:], math.pi, -math.pi, ALU.min, ALU.max, [b_red], [b_red])
                act(outt[:], rr[:], AF.Sin, [b_red], [b_sc])

            sin_of(sinv, 0.0)
            sin_of(cosv, math.pi / 2)
            PA = cosv; PB = sinv; b_P = b_sc
            tt("dve", PA[:], mag[:], cosv[:], ALU.mult, [b_mag, b_sc], [b_P])
            tt("dve", PB[:], mag[:], sinv[:], ALU.mult, [b_mag, b_sc], [b_P])
            cp("dve", MG8[:], mag[:, :, 25], [b_mag], [b_mg8])
            den = sb0("den", [128, 32]); t0 = sb0("t0", [128, 32]); t1 = sb0("t1", [128, 32])
            fr = sb0("fr", [128, 32]); fi = sb0("fi", [128, 32]); am1 = sb0("am1", [128, 32]); b_f = Buf()
            tt("dve", den[:], lr[:], lr[:], ALU.mult, [b_l], [b_f])
            tt("dve", t0[:], li[:], li[:], ALU.mult, [b_l], [b_f])
            tt("dve", den[:], den[:], t0[:], ALU.add, [b_f], [b_f])
            P.op("dve", lambda e: e.reciprocal(den[:], den[:]), [b_f], [b_f])
            ts("dve", am1[:], PA[:, :, 27], -1.0, None, ALU.add, None, [b_P], [b_f])
            tt("dve", t0[:], am1[:], lr[:], ALU.mult, [b_f, b_l], [b_f])
            tt("dve", t1[:], PB[:, :, 27], li[:], ALU.mult, [b_P, b_l], [b_f])
            tt("dve", t0[:], t0[:], t1[:], ALU.add, [b_f], [b_f])
            tt("dve", fr[:], t0[:], den[:], ALU.mult, [b_f], [b_f])
            tt("dve", t0[:], PB[:, :, 27], lr[:], ALU.mult, [b_P, b_l], [b_f])
            tt("dve", t1[:], am1[:], li[:], ALU.mult, [b_f, b_l], [b_f])
            tt("dve", t0[:], t0[:], t1[:], ALU.subtract, [b_f], [b_f])
            tt("dve", fi[:], t0[:], den[:], ALU.mult, [b_f], [b_f])
            shp16 = [128, 32, 16]
            FRn = sb0("FRn", shp16); FIs = sb0("FIs", shp16); tq = sb0("tq", shp16); b_FR = Buf()
            frb = fr[:].unsqueeze(2).broadcast_to(shp16); fib = fi[:].unsqueeze(2).broadcast_to(shp16)
            tt("dve", FRn[:], PA[:, :, 0:16], frb, ALU.mult, [b_P, b_f], [b_FR])
            tt("dve", tq[:], PB[:, :, 0:16], fib, ALU.mult, [b_P, b_f], [b_FR])
            tt("dve", FRn[:], FRn[:], tq[:], ALU.subtract, [b_FR], [b_FR])
            tt("dve", FIs[:], PA[:, :, 0:16], fib, ALU.mult, [b_P, b_f], [b_FR])
            tt("dve", tq[:], PB[:, :, 0:16], frb, ALU.mult, [b_P, b_f], [b_FR])
            tt("dve", FIs[:], FIs[:], tq[:], ALU.add, [b_FR], [b_FR])
            ts("dve", FIs[:], FIs[:], sigma[:, 0:1], None, ALU.mult, None, [b_FR, b_sigma], [b_FR])
            BT1 = sb0("BT1", [128, 32, 16]); BT2 = sb0("BT2", [128, 32, 16]); b_BT = Buf()
            brv = b_re.rearrange("g p c -> p g c"); biv = b_im.rearrange("g p c -> p g c")
            dma("sp", BT1[0:64], brv, [], [b_BT]); dma("sp", BT1[64:128], biv, [], [b_BT])
            dma("sp", BT2[0:64], biv, [], [b_BT]); dma("sp", BT2[64:128], brv, [], [b_BT])
            shp4 = [128, 32, 8, 16]
            BBneg = sb0("BBneg", shp4); MinT = BBneg; b_BB = Buf()
            bt1b = BT1[:].unsqueeze(2).broadcast_to(shp4); bt2b = BT2[:].unsqueeze(2).broadcast_to(shp4)

            def build_bb(dst, q0, tbuf, b_tbuf):
                tt("dve", dst[:], FRn[:, :, q0:q0 + 8].unsqueeze(3).broadcast_to(shp4), bt1b, ALU.mult, [b_FR, b_BT], [b_BB])
                tt("dve", tbuf, FIs[:, :, q0:q0 + 8].unsqueeze(3).broadcast_to(shp4), bt2b, ALU.mult, [b_FR, b_BT], [b_tbuf])
                tt("dve", dst[:], dst[:], tbuf, ALU.subtract, [b_BB, b_tbuf], [b_BB])
            CT1 = sb0("CT1", [128, 32, 16]); CT2 = sb0("CT2", [128, 32, 16]); b_CT = Buf()
            CL = sb0("CL", [128, 4, 256]); b_CL = Buf()
            crv = c_re.rearrange("(r p) k -> p r k", p=128); civ = c_im.rearrange("(r p) k -> p r k", p=128)
            dma("sp", CL[:, :, 0:64], crv, [], [b_CL]); dma("sp", CL[:, :, 64:128], civ, [], [b_CL])
            dma("sp", CL[:, :, 128:192], civ, [], [b_CL]); dma("sp", CL[:, :, 192:256], crv, [], [b_CL])
            for half, dst in ((0, CT1), (1, CT2)):
                bk, bb = nb()
                for r in range(4):
                    tr(bk[:, r * 128:(r + 1) * 128], CL[:, r, half * 128:(half + 1) * 128], identf[:], [b_CL, b_identf], [bb])
                cp("dve", dst[:].rearrange("p g c -> p (g c)"), bk[:, 0:512], [bb], [b_CT])
            shp9 = [128, 32, 9, 16]
            EE = sb0("EE", shp9); te = sb0("te", shp9); PAs = sb0("PAs", [128, 32, 9]); b_EE = Buf(); b_te = Buf()
            ts("dve", PAs[:], PA[:, :, 16:25], sigma[:, 0:1], None, ALU.mult, None, [b_P, b_sigma], [b_EE])
            tt("dve", EE[:], PAs[:].unsqueeze(3).broadcast_to(shp9), CT1[:].unsqueeze(2).broadcast_to(shp9), ALU.mult, [b_EE, b_CT], [b_EE])
            tt("dve", te[:], PB[:, :, 16:25].unsqueeze(3).broadcast_to(shp9), CT2[:].unsqueeze(2).broadcast_to(shp9), ALU.mult, [b_P, b_CT], [b_te])
            tt("dve", EE[:], EE[:], te[:], ALU.subtract, [b_EE, b_te], [b_EE])
            build_bb(BBneg, 0, te[:, :, 0:8, :], b_te)
            cp("dve", Mout[:].rearrange("p g (j c) -> p g j c", c=16), EE[:, :, 1:9, :], [b_EE], [b_Mout])
            Drep = sb0("Drep", [128, 32]); b_Drep = Buf()
            dsv = d_skip.rearrange("(g c) -> c g", c=16)
            for i in range(8):
                dma("sp", Drep[16 * i:16 * i + 16, :], dsv, [], [b_Drep], allow_slow_non_contiguous=True)
            tmpT = te[:, 0:4, 0:8, :].rearrange("p g n c -> p g (n c)"); b_tmpT = b_te
            for gq in range(8):
                bk, bb = nb()
                for gl in range(4):
                    g = gq * 4 + gl
                    mm(bk[:, gl * 128:(gl + 1) * 128], BBneg[:, g].rearrange("p i c -> p (i c)"),
                       EE[:, g, 0:8, :].rearrange("p n c -> p (n c)"), True, True, [b_BB, b_EE], [bb])
                tt("dve", tmpT, bk[:, 0:512].rearrange("p (g c) -> p g c", g=4),
                   mtoep[:].unsqueeze(1).broadcast_to([128, 4, 128]), ALU.mult, [bb, b_mtoep], [b_tmpT])
                for gl in range(4):
                    g = gq * 4 + gl
                    stt(Toep[:, g, :], identf[:], Drep[:, g:g + 1], tmpT[:, gl, :], ALU.mult, ALU.add,
                        [b_identf, b_Drep, b_tmpT], [b_Toep])
            build_bb(MinT, 8, te[:, :, 0:8, :], b_te)
            for gq in range(8):
                bk, bb = nb()
                for gl in range(4):
                    g = gq * 4 + gl
                    tr(bk[:, gl * 128:(gl + 1) * 128], MinT[:, g].rearrange("p i c -> p (i c)"), identf[:], [b_BB, b_identf], [bb])
                bv = bk[:, 0:512].rearrange("p (g c) -> p g c", g=4)
                cp("dve", MinX[:, gq * 4:gq * 4 + 4, 0:128], bv, [bb], [b_MinX])
                cp("act", MinX[:, gq * 4:gq * 4 + 4, 128:192], bv[:, :, 0:64], [bb], [b_MinX])
            cp("dve", ARR8[:, 0, :], PA[:, :, 25], [b_P], [b_A8]); cp("dve", ARR8[:, 1, :], PA[:, :, 25], [b_P], [b_A8])
            ts("dve", AII8[:, 1, :], PB[:, :, 25], sigma[:, 0:1], None, ALU.mult, None, [b_P, b_sigma], [b_A8])
            ts("dve", AII8[:, 0, :], AII8[:, 1, :], -1.0, None, ALU.mult, None, [b_A8], [b_A8])
            cp("dve", AR4[:], PA[:, :, 26], [b_P], [b_A4])
            ts("dve", AI4[:], PB[:, :, 26], sigma[:, 0:1], -1.0, ALU.mult, ALU.mult, [b_P, b_sigma], [b_A4])
            ts("dve", th8c[:], th[:], 8.0, None, ALU.mult, None, [b_rt], [b_th8c])
            P.barrier(exclude=set(id(o) for o in P.all_ops[n_pref0:n_pref1]))

        with contextlib.ExitStack() as S1:
            shpk = [128, 32, 64]
            kvec = sb("kvec", [128, 64], F32, S1); b_kvec = Buf()
            dma("sp", kvec[:], c_kvec, [], [b_kvec])
            th8r = sb("th8r", [128, 32], F32, S1); b_th8r = Buf()
            kA = sb("kA", shpk, F32, S1); kR = sb("kR", shpk, F32, S1); kF = sb("kF", shpk, F32, S1)
            kI = sb("kI", shpk, I32, S1); kM = sb("kM", shpk, F32, S1); b_k = Buf()

            def reduce_pi(rr_, src, shift, kf_, ki_, mk_, rd, wr):
                ts("dve", rr_, src, shift, None, ALU.add, None, rd, wr)
                ts("dve", kf_, rr_, 1.0 / TWO_PI, None, ALU.mult, None, wr, wr)
                cp("dve", ki_, kf_, wr, wr)
                cp("dve", kf_, ki_, wr, wr)
                stt(rr_, kf_, -TWO_PI, rr_, ALU.mult, ALU.add, wr, wr)
                ts("dve", mk_, rr_, math.pi, None, ALU.is_gt, None, wr, wr)
                stt(rr_, mk_, -TWO_PI, rr_, ALU.mult, ALU.add, wr, wr)
                ts("dve", mk_, rr_, -math.pi, None, ALU.is_lt, None, wr, wr)
                stt(rr_, mk_, TWO_PI, rr_, ALU.mult, ALU.add, wr, wr)
                ts("dve", rr_, rr_, math.pi, -math.pi, ALU.min, ALU.max, wr, wr)

            reduce_pi(th8r[:], th8c[:], 0.0, kF[:, :, 0], kI[:, :, 0], kM[:, :, 0], [b_th8c], [b_th8r, b_k])
            tt("dve", kA[:], th8r[:].unsqueeze(2).broadcast_to(shpk), kvec[:].unsqueeze(1).broadcast_to(shpk), ALU.mult,
               [b_th8r, b_kvec], [b_k])
            reduce_pi(kR[:], kA[:], 0.0, kF[:], kI[:], kM[:], [b_k], [b_k])
            act(TSs[:], kR[:], AF.Sin, [b_k], [b_tab])
            ts("dve", TSs[:], TSs[:], sigma[:, 0:1], None, ALU.mult, None, [b_tab, b_sigma], [b_tab])
            reduce_pi(kR[:], kA[:], math.pi / 2, kF[:], kI[:], kM[:], [b_k], [b_k])
            act(TC[:], kR[:], AF.Sin, [b_k], [b_tab])
            P.barrier(exclude=set(id(o) for o in P.all_ops[n_pref0:n_pref1]))

        with contextlib.ExitStack() as SA:
            def sba(name, shape, dt=F32):
                return sb(name, shape, dt, SA)

            gmix_bc = sba("gmix_bc", [128, 1024]); b_gmix = Buf()
            dma("sp", gmix_bc[:], g_mix.partition_broadcast(128), [], [b_gmix])
            gv_bc = sba("gv_bc", [128, 512]); b_gv = Buf()
            goa_bc = sba("goa_bc", [128, 512]); b_goa = Buf()
            gob = sba("gob", [128, 4]); bglu = sba("bglu", [128, 4]); b_col = Buf()
            bsT = sba("bsT", [128, 4]); bsT_s = sba("bsT_s", [64, 4])
            def sa_late():
                dma("sp", gv_bc[:], g_v.partition_broadcast(128), [], [b_gv])
                dma("sp", goa_bc[:], g_out_a.partition_broadcast(128), [], [b_goa])
                dma("sp", msgu[:], c_msgu, [], [b_msgu])
                dma("sp", gob[:], g_out_b.rearrange("(m p) -> p m", p=128), [], [b_col], allow_slow_non_contiguous=True)
                dma("sp", bglu[:], b_glu.rearrange("(m p) -> p m", p=128), [], [b_col], allow_slow_non_contiguous=True)
                dma("sp", bsT[:], b_s.rearrange("h t -> t h"), [], [b_col], allow_slow_non_contiguous=True)
                for b in range(16):
                    dma("sp", bsT_s[4 * b:4 * b + 4, :], b_s[:, 0:4].rearrange("h t -> t h"), [], [b_col], allow_slow_non_contiguous=True)
                for m_ in range(4):
                    ts("dve", w_out_sb[:, 4 + m_, :], w_out_sb[:, 4 + m_, :], gob[:, m_:m_ + 1], None, ALU.mult, None, [b_wout, b_col], [b_wout])

            msgu = sba("msgu", [128, 128]); b_msgu = Buf()
            Msg = sba("Msg", [128, 4, 128], BF16); Msg_s = sba("Msg_s", [64, 4, 64], BF16); b_Msg = Buf()
            xa = [sba("xa%d" % i, [128, 1024]) for i in range(2)]; b_xa = [Buf(), Buf()]
            xh = [sba("xh%d" % i, [128, 1024]) for i in range(2)]; b_xh = [Buf(), Buf()]
            b_ss_s = [Buf() for _ in range(4)]; b_rs_s = [Buf() for _ in range(4)]; b_tm_s = [Buf() for _ in range(4)]
            b_xs_s = [Buf() for _ in range(4)]
            cnt_a = {"xh": 0}
            ss4 = sba("ss4", [128, 4]); rs4 = sba("rs4", [128, 4]); tm4 = sba("tm4", [128, 4]); b_ss = Buf(); b_rs = Buf(); b_tm4 = Buf()
            xn = [sba("xn%d" % i, [128, 1024], BF16) for i in range(2)]; b_xn = [Buf(), Buf()]
            n1T = sba("n1T", [128, 8, 512], BF16); b_n1T = Buf(); b_n1T_list = [Buf() for _ in range(4)]
            u_tm = [sba("u_tm%d" % i, [128, 512], BF16) for i in range(2)]; b_u = [Buf(), Buf()]
            vsq = sba("vsq", [128, 512], BF16); b_vsq = Buf()
            ssv = sba("ssv", [128, 4]); rv = sba("rv", [128, 4]); tmv = sba("tmv", [128, 4]); b_ssv = Buf(); b_rv = Buf(); b_tmv = Buf()
            vtmp = sba("vtmp", [128, 512]); b_vtmp = Buf()
            vnf_alias = True
            vn = [sba("vn%d" % i, [128, 512], BF16) for i in range(2)]; b_vn = [Buf(), Buf()]
            xs_tm = sba("xs_tm", [128, 4, 512], BF16); b_xstm = Buf()
            a_tms = [sba("a_tm%d" % i, [128, 512]) for i in range(2)]; b_atms = [Buf(), Buf()]
            vnf = a_tms[1]; b_vnf = b_atms[1]
            ssa = sba("ssa", [128, 1]); ra = sba("ra", [128, 1]); tma = sba("tma", [128, 1]); b_ssa = Buf(); b_ra = Buf(); b_tma = Buf()
            mixas = [sba("mixa%d" % i, [128, 512], BF16) for i in range(2)]; b_mixas = [Buf(), Buf()]
            mixTa = [sba("mixTa%d" % i, [128, 4, 512], BF16) for i in range(2)]; b_mixTa = [Buf(), Buf()]
            mixTb = sba("mixTb", [128, 4, 512], BF16); b_mixTb = Buf()
            Z = sba("Z", [64, 8, 512], BF16); b_Zs = [Buf() for _ in range(4)]
            X2 = sba("X2", [128, 32, 64], BF16); b_X2 = Buf()
            R1 = sba("R1", [128, 32, 64]); R2 = sba("R2", [128, 32, 64])
            b_R1 = [Buf() for _ in range(32)]; b_R2 = [Buf() for _ in range(32)]
            mA = sba("mA", [128, 8, 64]); mB = sba("mB", [128, 8, 64]); b_mA = Buf(); b_mB = Buf()
            cA = sba("cA", [128, 32]); cB = sba("cB", [128, 32]); b_cA = Buf(); b_cB = Buf()
            HPb = sba("HPb", [128, 32, 64], BF16); b_HPb = Buf()
            gTb = sba("gTb", [128, 4, 512], BF16); b_gTb = Buf()
            junk = gTb[:].rearrange("p a t -> p (a t)")[:, 0:1024]; b_junk = b_gTb
            sg = sba("sg", [128, 512]); b_sg = Buf()
            R1f = R1[:].rearrange("p g k -> p (g k)"); R2f = R2[:].rearrange("p g k -> p (g k)")
            allR1 = b_R1; allR2 = b_R2
            sqb = R1f.bitcast(BF16)[:, 0:2048].rearrange("p (a t) -> p a t", a=4); b_sqb = b_R1[0]
            rb_bc = R2f[:, 0:512]; tmb = R2f[:, 512:1024]; b_rb = b_R2[0]; b_tmb = b_R2[8]
            stS = sba("stS", [32, 128]); b_stS = Buf()
            rb4 = sba("rb4", [128, 4]); tb4 = sba("tb4", [128, 4]); b_rb4 = Buf(); b_tb4 = Buf()
            Hins = [R2f[0:16, 1024:2048].rearrange("p (g c) -> p g c", g=4), R1f[0:16, 1024:2048].rearrange("p (g c) -> p g c", g=4)]
            b_Hins = [b_R2[16], b_R1[16]]
            r3 = lambda t_: t_.rearrange("p (g b) -> p g b", b=16)
            V10 = r3(rb_bc); b_V10 = b_rb
            V20 = r3(R1f[:, 512:1024]); b_V20 = b_R1[8]
            Vend = r3(tmb); b_Vend = b_tmb
            Vt = r3(sg[:]); b_Vt = b_sg
            Houts = [vtmp[0:16, :].rearrange("p (g c) -> p g c", g=4), mA[0:16].rearrange("p a k -> p (a k)").rearrange("p (g c) -> p g c", g=4)]
            b_Houts = [b_vtmp, b_mA]

            def sa_late2():
                wsl = R1f[:, 0:512].rearrange("p (h s) -> p h s", h=4)
                wblk = R1f[0:64, 512:768].rearrange("p (h s) -> p h s", h=4)
                dma("sp", wsl, w_s.rearrange("h t s -> t h s"), [], b_R1[0:8])
                P.op("dve", lambda e: e.memset(wblk, 0.0), [], b_R1[8:12])
                for b in range(16):
                    dma("act", wblk[4 * b:4 * b + 4, :, 4 * b:4 * b + 4], w_s[:, 0:4, 0:4].rearrange("h t s -> t h s"), b_R1[8:12], b_R1[8:12],
                        allow_slow_non_contiguous=True)
                bk, bb = nb()
                for h in range(4):
                    tr(bk[:, h * 128:(h + 1) * 128], wsl[:, h, :], identf[:], b_R1[0:8] + [b_identf], [bb])
                tt("dve", Msg[:], bk[:, 0:512].rearrange("p (h t) -> p h t", h=4), msgu[:].unsqueeze(1).broadcast_to([128, 4, 128]),
                   ALU.mult, [bb, b_msgu], [b_Msg])
                bk, bb = nb()
                for h in range(4):
                    tr(bk[0:64, h * 64:(h + 1) * 64], wblk[:, h, :], identf[0:64, 0:64], b_R1[8:12] + [b_identf], [bb])
                tt("dve", Msg_s[:], bk[0:64, 0:256].rearrange("p (h t) -> p h t", h=4),
                   msgu[0:64, 0:64].unsqueeze(1).broadcast_to([64, 4, 64]), ALU.mult, [bb, b_msgu], [b_Msg])

            st_cur = [0]

            def mixer_tile(tok0, src, ntok, sample, h1row, par):
                NS = 1 if sample else 4
                PT = 64 if sample else 128
                NC = 16 if sample else 64
                mixTa_t = mixTa[par]; b_mixTa_t = b_mixTa[par]
                shared = {}

                def front(part):
                    b_n1T_s = b_n1T_list

                    def a2(s):
                        xs_ = xa[s % 2]; bxs_ = b_xa[s % 2]
                        if sample:
                            dma("sp", xs_[0:64, :], src, [], [bxs_])
                        else:
                            dma("sp", xs_[:, :], src[tok0 + 128 * s:tok0 + 128 * (s + 1), :], [], [bxs_])
                        act(junk[0:PT, :], xs_[0:PT, :], AF.Square, [bxs_], [b_junk, b_ss_s[s]], accum_out=ss4[0:PT, s:s + 1])
                        rstd(rs4[0:PT, s:s + 1], ss4[0:PT, s:s + 1], 1024.0, PT, 1, [b_ss_s[s]], [b_rs_s[s]], tm4[0:PT, s:s + 1], b_tm_s[s])

                    def a4(s):
                        k2 = s % 2
                        stt(xn[k2][0:PT, :], xa[k2][0:PT, :], rs4[0:PT, s:s + 1], gmix_bc[0:PT, :], ALU.mult, ALU.mult,
                            [b_xa[k2], b_rs_s[s], b_gmix], [b_xn[k2]])
                        bk, bb = nb()
                        bkb = bk[:].bitcast(BF16)
                        for dti in range(8):
                            tr(bkb[:, dti * PT:(dti + 1) * PT], xn[k2][0:PT, dti * 128:(dti + 1) * 128], identb[0:PT, 0:PT],
                               [b_xn[k2], b_identb], [bb])
                        cp("act", n1T[:, :, s * 128:s * 128 + PT], bkb[:, 0:8 * PT].rearrange("p (a t) -> p a t", a=8), [bb], [b_n1T_s[s]])

                    if part == 0:
                        a2(0)
                        for s in range(NS):
                            if s + 1 < NS:
                                a2(s + 1)
                            a4(s)
                        return

                    def stage1(s):
                        k2 = s % 2
                        tsl = slice(s * 128, s * 128 + PT)
                        bu, bbu = nb(); bv, bbv = nb(); bx, bbx = nb()
                        for (bkx, bbx_, c0) in ((bv, bbv, 512), (bu, bbu, 0), (bx, bbx, 1024)):
                            for dti in range(8):
                                mm(bkx[0:PT, :], n1T[:, dti, tsl], w_in_sb[:, dti, c0:c0 + 512], dti == 0, dti == 7,
                                   [b_n1T_s[s], b_win], [bbx_])
                        return (bv, bbv, bu, bbu, bx, bbx)

                    def stage1e(s, bu, bbu, bx, bbx):
                        k2 = s % 2
                        cp("act", u_tm[k2][0:PT, :], bu[0:PT, :], [bbu], [b_u[k2]])
                        cp("act", xs_tm[0:PT, s, :], bx[0:PT, :], [bbx], [b_xs_s[s]])
                        if sample:
                            dma("sp", zscr[2048:2112, :], xs_tm[0:64, 0, :], [b_xs_s[0]], [b_zscr])
                        else:
                            r0 = h1row + 128 * s
                            dma("sp", zscr[r0:r0 + 128, :], xs_tm[:, s, :], [b_xs_s[s]], [b_zscr])

                    def stage2(s, bv, bbv):
                        k2 = s % 2
                        tsl = slice(s * 128, s * 128 + PT)
                        act(vsq[0:PT, :], bv[0:PT, :], AF.Square, [bbv], [b_vsq])
                        P.op("dve", lambda e: e.reduce_sum(out=ssv[0:PT, :], in_=vsq[0:PT, :].rearrange("p (h d) -> p h d", h=4), axis=AX.X),
                             [b_vsq], [b_ssv])
                        rstd(rv[0:PT, :], ssv[0:PT, :], 128.0, PT, 4, [b_ssv], [b_rv], tmv[0:PT, :], b_tmv)
                        tt("dve", vtmp[0:PT, :].rearrange("p (h d) -> p h d", h=4), bv[0:PT, :].rearrange("p (h d) -> p h d", h=4),
                           rv[0:PT, :].unsqueeze(2).broadcast_to([PT, 4, 128]), ALU.mult, [bbv, b_rv], [b_vtmp])
                        if sample:
                            tt("dve", vnf[0:PT, :], vtmp[0:PT, :], gv_bc[0:PT, :], ALU.mult, [b_vtmp, b_gv], [b_vnf])
                            dma("sp", vs, vnf[0:PT, :], [b_vnf], [], is_output=True)
                            cp("dve", vn[k2][0:PT, :], vnf[0:PT, :], [b_vnf], [b_vn[k2]])
                        else:
                            tt("dve", vn[k2][0:PT, :], vtmp[0:PT, :], gv_bc[0:PT, :], ALU.mult, [b_vtmp, b_gv], [b_vn[k2]])
                        bs_, bbs = nb()
                        for h in range(4):
                            lhs = Msg_s[:, h, :] if sample else Msg[:, h, :]
                            mm(bs_[0:PT, h * 128:(h + 1) * 128], lhs, vn[k2][0:PT, h * 128:(h + 1) * 128], True, True,
                               [b_Msg, b_vn[k2]], [bbs])
                        bsb = bsT_s if sample else bsT
                        a_tm = a_tms[k2]; b_atm = b_atms[k2]
                        for h in range(4):
                            stt(a_tm[0:PT, h * 128:(h + 1) * 128], bs_[0:PT, h * 128:(h + 1) * 128], bsb[0:PT, h:h + 1],
                                u_tm[k2][0:PT, h * 128:(h + 1) * 128], ALU.add, ALU.mult, [bbs, b_col, b_u[k2]], [b_atm])

                    def stage2b(s):
                        k2 = s % 2
                        tsl = slice(s * 128, s * 128 + PT)
                        a_tm = a_tms[k2]; b_atm = b_atms[k2]; mixa = mixas[k2]; b_mixa = b_mixas[k2]
                        act(junk[0:PT, 0:512], a_tm[0:PT, :], AF.Square, [b_atm], [b_junk, b_ssa], accum_out=ssa[0:PT, :])
                        rstd(ra[0:PT, :], ssa[0:PT, :], 512.0, PT, 1, [b_ssa], [b_ra], tma[0:PT, :], b_tma)
                        stt(mixa[0:PT, :], a_tm[0:PT, :], ra[0:PT, 0:1], goa_bc[0:PT, :], ALU.mult, ALU.mult, [b_atm, b_ra, b_goa], [b_mixa])
                        bk, bb = nb()
                        bkb = bk[:].bitcast(BF16)
                        for m in range(4):
                            tr(bkb[:, m * PT:(m + 1) * PT], mixa[0:PT, m * 128:(m + 1) * 128], identb[0:PT, 0:PT], [b_mixa, b_identb], [bb])
                        cp("act", mixTa_t[:, :, tsl], bkb[:, 0:4 * PT].rearrange("p (a t) -> p a t", a=4), [bb], [b_mixTa_t])

                    pend = stage1(0)
                    stage1e(0, *pend[2:])
                    lag = None
                    for s in range(NS):
                        nxt = stage1(s + 1) if s + 1 < NS else None
                        stage2(s, pend[0], pend[1])
                        if lag is not None:
                            stage2b(lag)
                        if nxt is not None:
                            stage1e(s + 1, *nxt[2:])
                        lag = s
                        pend = nxt
                    stage2b(lag)

                def z_readback():
                    if sample:
                        zv = zscr[2048:2112, :].rearrange("(b t) c -> b t c", t=4)
                        dma("pool", Z[0:16, 0:4, :], zv, [b_zscr], [b_Zs[0]])
                        dma("pool", Z[0:16, 4:8, :], zv, [b_zscr], [b_Zs[0]])
                    else:
                        for s_ in range(4):
                            r0 = h1row + 128 * s_
                            dma("pool", Z[16 * s_:16 * s_ + 16, :, :], zscr[r0:r0 + 128, :].rearrange("(k i) c -> k i c", i=8), [b_zscr], [b_Zs[s_]])

                def ssm_front():
                    if sample:
                        bk1, bbk1 = nb(); bk2, bbk2 = nb()
                        for gq in range(8):
                            gs = slice(gq * 4, gq * 4 + 4)
                            Hin = Hins[gq % 2]; b_Hin = b_Hins[gq % 2]
                            dma("sp", Hin[:, :, 0:64], sre[:, gs, :], [], [b_Hin]); dma("sp", Hin[:, :, 64:128], sim[:, gs, :], [], [b_Hin])
                            dma("sp", Hin[:, :, 128:192], sim[:, gs, :], [], [b_Hin]); dma("sp", Hin[:, :, 192:256], sre[:, gs, :], [], [b_Hin])
                            for gl in range(4):
                                g = gq * 4 + gl
                                tr(bk1[:, g * 16:(g + 1) * 16], Hin[:, gl, 0:128], identf[0:16, 0:16], [b_Hin, b_identf], [bbk1])
                                tr(bk2[:, g * 16:(g + 1) * 16], Hin[:, gl, 128:256], identf[0:16, 0:16], [b_Hin, b_identf], [bbk2])
                        f2 = lambda v_: v_.rearrange("p g b -> p (g b)")
                        cp("act", f2(V10), bk1[:, 0:512], [bbk1], [b_V10])
                        cp("dve", f2(V20), bk2[:, 0:512], [bbk2], [b_V20])
                        cp("dve", HPb[:, :, 0:16], V10, [b_V10], [b_HPb])
                        shb = [128, 32, 16]
                        tt("dve", Vend, V10, AR4[:].unsqueeze(2).broadcast_to(shb), ALU.mult, [b_V10, b_A4], [b_Vend])
                        tt("dve", Vt, V20, AI4[:].unsqueeze(2).broadcast_to(shb), ALU.mult, [b_V20, b_A4], [b_Vt])
                        tt("dve", Vend, Vend, Vt, ALU.add, [b_Vend, b_Vt], [b_Vend])

                    for gh in range(2):
                        bk, bb = nb()
                        bkb = bk[:].bitcast(BF16)
                        for gl in range(16):
                            g = gh * 16 + gl
                            tr(bkb[:, gl * NC:(gl + 1) * NC], Z[:].rearrange("k i c -> k (i c)")[0:NC, g:4096:32], identb[0:NC, 0:NC],
                               b_Zs + [b_identb], [bb])
                        cp("act" if gh == 0 else "dve", X2[:, gh * 16:gh * 16 + 16, 0:NC],
                           bkb[:, 0:16 * NC].rearrange("p (g k) -> p g k", g=16), [bb], [b_X2])
                    if sample:
                        b1, bb1 = nb()
                        for g in range(32):
                            mm(b1[:, g * 16:(g + 1) * 16], MinX[64:128, g, 0:128], X2[64:128, g, 0:16], True, True, [b_MinX, b_X2], [bb1])
                    else:
                        GPB = 8
                        for gq in range(4):
                            gs = slice(gq * GPB, (gq + 1) * GPB)
                            b1, bb1 = nb(); b2, bb2 = nb()
                            for gl in range(GPB):
                                g = gq * GPB + gl
                                mm(b1[:, gl * 64:(gl + 1) * 64], MinX[:, g, 0:128], X2[:, g, 0:64], True, True, [b_MinX, b_X2], [bb1])
                                mm(b2[:, gl * 64:(gl + 1) * 64], MinX[:, g, 64:192], X2[:, g, 0:64], True, True, [b_MinX, b_X2], [bb2])
                            s1v = b1[:, 0:512].rearrange("p (g k) -> p g k", g=GPB)
                            s2v = b2[:, 0:512].rearrange("p (g k) -> p g k", g=GPB)
                            wr1 = b_R1[gq * GPB:(gq + 1) * GPB]; wr2 = b_R2[gq * GPB:(gq + 1) * GPB]
                            tt("dve", R1[:, gs, :], s1v, TC[:, gs, :], ALU.mult, [bb1, b_tab], wr1)
                            tt("dve", mA[:], s2v, TSs[:, gs, :], ALU.mult, [bb2, b_tab], [b_mA])
                            tt("pool", R1[:, gs, :], R1[:, gs, :], mA[:], ALU.add, wr1 + [b_mA], wr1)
                            tt("dve", R2[:, gs, :], s2v, TC[:, gs, :], ALU.mult, [bb2, b_tab], wr2)
                            tt("dve", mB[:], s1v, TSs[:, gs, :], ALU.mult, [bb1, b_tab], [b_mB])
                            tt("pool", R2[:, gs, :], R2[:, gs, :], mB[:], ALU.subtract, wr2 + [b_mB], wr2)
                    if sample:
                        shared["b1"] = (b1, bb1)

                def scan():
                    if sample:
                        f2 = lambda v_: v_.rearrange("p g b -> p (g b)")
                        b1, bb1 = shared["b1"]
                        tt("dve", f2(Vend), f2(Vend), b1[:, 0:512], ALU.add, [b_Vend, bb1], [b_Vend])
                        for gq in range(8):
                            bk, bb = nb()
                            for gl in range(4):
                                g = gq * 4 + gl
                                tr(bk[0:16, gl * 128:(gl + 1) * 128], Vend[:, g, :], identf[:], [b_Vend, b_identf], [bb])
                            Hout = Houts[gq % 2]; b_Hout = b_Houts[gq % 2]
                            cp("act", Hout, bk[0:16, 0:512].rearrange("p (g c) -> p g c", g=4), [bb], [b_Hout])
                            gs = slice(gq * 4, gq * 4 + 4)
                            dma("sp", sres[:, gs, :], Hout[:, :, 0:64], [b_Hout], [], is_output=True)
                            dma("sp", sims[:, gs, :], Hout[:, :, 64:128], [b_Hout], [], is_output=True)
                    else:
                        cp("act", HPb[:, :, 0], C1[:], [b_C], [b_HPb])
                        tt("dve", cA[:], MG8[:], C1[:], ALU.mult, [b_mg8, b_C], [b_cA])
                        tt("dve", R1[:, :, 0], R1[:, :, 0], cA[:], ALU.add, b_R1 + [b_cA], b_R1)
                        tt("dve", cB[:], MG8[:], C2[:], ALU.mult, [b_mg8, b_C], [b_cB])
                        tt("dve", R2[:, :, 0], R2[:, :, 0], cB[:], ALU.add, b_R2 + [b_cB], b_R2)
                        for g in range(32):
                            for (RR, bR) in ((R1, b_R1), (R2, b_R2)):
                                P.op("dve", lambda e, RR=RR, g=g: e.tensor_tensor_scan(
                                    out=RR[:, g, :], data0=MG8[:, g:g + 1].broadcast_to([128, 64]), data1=RR[:, g, :],
                                    initial=0.0, op0=ALU.mult, op1=ALU.add), [bR[g], b_mg8], [bR[g]])
                        for gq in range(4):
                            gs = slice(gq * 8, gq * 8 + 8)
                            wr1 = b_R1[gq * 8:gq * 8 + 8]; wr2 = b_R2[gq * 8:gq * 8 + 8]
                            tt("dve", mA[:], R1[:, gs, :], TC[:, gs, :], ALU.mult, wr1 + [b_tab], [b_mA])
                            tt("pool", mB[:], R2[:, gs, :], TSs[:, gs, :], ALU.mult, wr2 + [b_tab], [b_mB])
                            tt("dve", HPb[:, gs, 1:64], mA[:, :, 0:63], mB[:, :, 0:63], ALU.subtract, [b_mA, b_mB], [b_HPb])
                            tt("dve", C1[:, gs], mA[:, :, 63], mB[:, :, 63], ALU.subtract, [b_mA, b_mB], [b_C])
                            tt("dve", cA[:, gs], R2[:, gs, 63], TC[:, gs, 63], ALU.mult, wr2 + [b_tab], [b_cA])
                            tt("dve", cB[:, gs], R1[:, gs, 63], TSs[:, gs, 63], ALU.mult, wr1 + [b_tab], [b_cB])
                            tt("dve", C2[:, gs], cA[:, gs], cB[:, gs], ALU.add, [b_cA, b_cB], [b_C])
                def back(hook=None):
                    Zy = Z
                    for gq in range(8):
                        bk, bb = nb()
                        for gl in range(4):
                            g = gq * 4 + gl
                            if sample:
                                mm(bk[0:NC, gl * 128:(gl + 1) * 128], X2[0:64, g, 0:NC], Toep[0:64, g, :], True, False, [b_X2, b_Toep], [bb])
                            else:
                                mm(bk[0:NC, gl * 128:(gl + 1) * 128], X2[:, g, 0:NC], Toep[:, g, :], True, False, [b_X2, b_Toep], [bb])
                            mm(bk[0:NC, gl * 128:(gl + 1) * 128], HPb[:, g, 0:NC], Mout[:, g, :], False, True, [b_HPb, b_Mout], [bb])
                        cp("dve" if gq % 2 == 0 else "act", Zy[0:NC, :, 64 * gq:64 * gq + 64].rearrange("p j (g c) -> p g j c", g=4),
                           bk[0:NC, 0:512].rearrange("p (g j c) -> p g j c", g=4, j=8), [bb], b_Zs)
                    NJ = 4 if sample else 8
                    for cbp in range(2):
                        bk, bb = nb()
                        bkb = bk[:].bitcast(BF16)
                        for cl in range(2):
                            cb = cbp * 2 + cl
                            for j in range(NJ):
                                o0 = (cl * NJ + j) * NC
                                tr(bkb[:, o0:o0 + NC], Zy[0:NC, j, cb * 128:(cb + 1) * 128], identb[0:NC, 0:NC], b_Zs + [b_identb], [bb])
                        for cl in range(2):
                            cb = cbp * 2 + cl
                            o0 = cl * NJ * NC
                            act(gTb[:, cb, 0:ntok].rearrange("p (k j) -> p j k", j=NJ),
                                bkb[:, o0:o0 + NJ * NC].rearrange("p (j k) -> p j k", j=NJ), AF.Gelu_apprx_tanh, [bb], [b_gTb])
                    if hook is not None:
                        hook()
                    for m in range(4):
                        bk, bb = nb()
                        for kt in range(4):
                            mm(bk[:, 0:ntok], w_glu_sb[:, kt, m * 128:(m + 1) * 128], gTb[:, kt, 0:ntok], kt == 0, kt == 3, [b_wglu, b_gTb], [bb])
                        act(sg[:, 0:ntok], bk[:, 0:ntok], AF.Sigmoid, [bb, b_col], [b_sg], bias=bglu[:, m:m + 1])
                        tt("dve", mixTb[:, m, 0:ntok], gTb[:, m, 0:ntok], sg[:, 0:ntok], ALU.mult, [b_gTb, b_sg], [b_mixTb])
                    act(sqb[:, :, 0:ntok], mixTb[:, :, 0:ntok], AF.Square, [b_mixTb], [b_sqb, b_R1[8]])
                    bkR, bbR = nb()
                    for s in range(NS):
                        tsl = slice(s * 128, s * 128 + PT)
                        for m in range(4):
                            mm(bkR[0:PT, s:s + 1], sqb[:, m, tsl], onesb[:, 0:1], m == 0, m == 3, [b_ones, b_sqb, b_R1[8]], [bbR])
                    rstd(rb4[0:PT, 0:NS], bkR[0:PT, 0:NS], 512.0, PT, NS, [bbR], [b_rb4], tb4[0:PT, 0:NS], b_tb4)
                    for s in range(NS):
                        tsl = slice(s * 128, s * 128 + PT)
                        hi_ = cnt_a["xh"] % 2; cnt_a["xh"] += 1
                        xh_ = xh[hi_]; bxh_ = b_xh[hi_]
                        if sample:
                            dma("sp", xh_[0:64, :], src, [], [bxh_])
                        else:
                            dma("sp", xh_[:, :], src[tok0 + 128 * s:tok0 + 128 * (s + 1), :], [], [bxh_])
                        for nh in range(2):
                            bka, bba = nb(); bkb_, bbb = nb()
                            for kt in range(4):
                                mm(bka[0:PT, :], mixTa_t[:, kt, tsl], w_out_sb[:, kt, nh * 512:(nh + 1) * 512], kt == 0, kt == 3, [b_mixTa_t, b_wout], [bba])
                            for kt in range(4):
                                mm(bkb_[0:PT, :], mixTb[:, kt, tsl], w_out_sb[:, 4 + kt, nh * 512:(nh + 1) * 512], kt == 0, kt == 3, [b_mixTb, b_wout], [bbb])
                            xsl = xh_[0:PT, nh * 512:(nh + 1) * 512]
                            tt("dve", xsl, xsl, bka[0:PT, :], ALU.add, [bxh_, bba], [bxh_])
                            stt(xsl, bkb_[0:PT, :], rb4[0:PT, s:s + 1], xsl, ALU.mult, ALU.add, [bxh_, bbb, b_rb4], [bxh_])
                        if sample:
                            dma("sp", h1d[h1row:h1row + 64, :], xh_[0:64, :], [bxh_], [])
                        else:
                            dma("sp", h1d[h1row + 128 * s:h1row + 128 * (s + 1), :], xh_[:, :], [bxh_], [])

                return front, ssm_front, scan, back, z_readback

            tiles = [mixer_tile(512 * t, xp, 512, False, 512 * t, t % 2) for t in range(4)]
            tiles.append(mixer_tile(0, xs, 64, True, 2048, 0))
            tiles[0][0](0)
            sa_late()
            sa_late2()
            tiles[0][0](1)
            tiles[0][4]()
            tiles[0][1]()
            for t in range(5):
                if t + 1 < 5:
                    tiles[t + 1][0](0)
                tiles[t][2]()
                if t + 1 < 5:
                    tiles[t + 1][0](1)
                tiles[t][3](tiles[t + 1][4] if t + 1 < 5 else None)
                if t == 3:
                    bk, bb = nb()
                    tr(bk[0:32, 0:128], C1[:], identf[:], [b_C, b_identf], [bb])
                    cp("act", stS[:], bk[0:32, 0:128], [bb], [b_stS])
                    dma("sp", srep, stS[:, 0:64], [b_stS], [], is_output=True)
                    dma("sp", simp, stS[:, 64:128], [b_stS], [], is_output=True)
                if t + 1 < 5:
                    tiles[t + 1][1]()
            P.barrier()
        SM.close()

        with contextlib.ExitStack() as SB:
            def sbb(name, shape, dt=F32):
                return sb(name, shape, dt, SB)

            wfo = sbb("wfo", [128, 22, 1024], BF16); b_wfo = Buf()
            wfov = w_ffn_out.rearrange("(a p) n -> p a n", p=128)
            wfo_pending = list(range(22))
            gffn_bc = sbb("gffn_bc", [128, 1024]); gfin_bc = sbb("gfin_bc", [128, 1024]); b_gbc = Buf()
            dma("sp", gffn_bc[:], g_ffn.partition_broadcast(128), [], [b_gbc])
            dma("sp", gfin_bc[:], g_final.partition_broadcast(128), [], [b_gbc])
            cwl = sbb("cwl", [88, 2, 128]); b_cwl = Buf()
            cwv = conv_w.rearrange("k (c p) -> (k c) p", p=128)
            dma("sp", cwl[:, 0, :], cwv[0:88, :], [], [b_cwl])
            dma("sp", cwl[0:44, 1, :], cwv[88:132, :], [], [b_cwl])
            dma("sp", cwl[44:88, 1, :], conv_b.rearrange("(c p) -> c p", p=128), [], [b_cwl])
            cw = sbb("cw", [128, 4, 44]); b_cw = Buf()
            bk, bb = nb()
            tr(bk[:, 0:88], cwl[:, 0, :], identf[0:88, 0:88], [b_cwl, b_identf], [bb])
            tr(bk[:, 88:176], cwl[:, 1, :], identf[0:88, 0:88], [b_cwl, b_identf], [bb])
            cp("dve", cw[:].rearrange("p k c -> p (k c)"), bk[:, 0:176], [bb], [b_cw])
            hist = sbb("hist", [128, 2, 44]); b_hist0 = Buf(); b_hist = [Buf() for _ in range(44)]
            P.op("pool", lambda e: e.memset(hist[:], 0.0), [], [b_hist0] + b_hist)
            hist_s = sbb("hist_s", [128, 44, 32]); b_hists = Buf()
            scl = sbb("scl", [32, 1408]); b_scl = Buf()
            for q4 in range(4):
                dma("sp", scl[:], scv[:, q4 * 1408:(q4 + 1) * 1408], [], [b_scl])
                bk, bb = nb()
                for cl in range(11):
                    tr(bk[:, cl * 32:(cl + 1) * 32], scl[:, cl * 128:(cl + 1) * 128], identf[0:32, 0:32], [b_scl, b_identf], [bb])
                cp("act", hist_s[:, q4 * 11:(q4 + 1) * 11, :], bk[:, 0:352].rearrange("p (c r) -> p c r", c=11), [bb], [b_hists])
            cst_s = sbb("cst_s", [128, 44, 32]); b_csts = Buf()

            NTB = 1088
            hb = [sbb("hb%d" % i, [128, 1024]) for i in range(3)]; b_hb = [Buf(), Buf(), Buf()]
            xn = [sbb("xnB%d" % i, [128, 1024], BF16) for i in range(2)]; b_xn = [Buf(), Buf()]
            junk = sbb("junkB", [128, 1024], BF16); b_junk = Buf()
            ssB = [sbb("ssB%d" % i, [128, 1]) for i in range(2)]; rsB = [sbb("rsB%d" % i, [128, 1]) for i in range(2)]
            tmB = [sbb("tmB%d" % i, [128, 1]) for i in range(2)]
            b_ssB = [Buf(), Buf()]; b_rsB = [Buf(), Buf()]; b_tmB = [Buf(), Buf()]
            n2T = sbb("n2T", [128, 8, NTB], BF16); b_n2T = Buf()
            zt = sbb("zt", [128, 22, NTB], BF16); b_z = Buf()
            wst = [sbb("wst%d" % i, [128, 8, 256], BF16) for i in range(3)]; b_wst = [Buf(), Buf(), Buf()]
            upS = [[sbb("upS%d_%d" % (i, j), [128, 514]) for j in range(2)] for i in range(2)]
            b_upS = [[Buf(), Buf()], [Buf(), Buf()]]
            b_upH = [[Buf(), Buf()], [Buf(), Buf()]]
            hsave = [sbb("hsv%d" % i, [128, 2]) for i in range(3)]; b_hsave = [Buf() for _ in range(3)]
            acc = [[sbb("acc%d_%d" % (i, j), [128, 512]) for j in range(2)] for i in range(3)]
            b_acc = [[Buf(), Buf()] for _ in range(3)]
            wfiv = w_ffn_in.rearrange("(a p) n -> p a n", p=128)
            cnt = {"hb": 0, "w": 0, "u": 0, "yo": 0}

            def ffn_tile(subtiles, cgs):
                def b1a(si):
                    hrow, PT, oap, col0 = subtiles[si]
                    i3 = cnt["hb"] % 3; cnt["hb"] += 1
                    k2 = si % 2
                    dma("sp", hb[i3][0:PT, :], h1d[hrow:hrow + PT, :], [], [b_hb[i3]])
                    act(junk[0:PT, :], hb[i3][0:PT, :], AF.Square, [b_hb[i3]], [b_junk, b_ssB[k2]], accum_out=ssB[k2][0:PT, :])
                    rstd(rsB[k2][0:PT, :], ssB[k2][0:PT, :], 1024.0, PT, 1, [b_ssB[k2]], [b_rsB[k2]], tmB[k2][0:PT, :], b_tmB[k2])
                    stt(xn[k2][0:PT, :], hb[i3][0:PT, :], rsB[k2][0:PT, 0:1], gffn_bc[0:PT, :], ALU.mult, ALU.mult,
                        [b_hb[i3], b_rsB[k2], b_gbc], [b_xn[k2]])

                def b1b(si):
                    hrow, PT, oap, col0 = subtiles[si]
                    k2 = si % 2
                    bk, bb = nb()
                    bkb = bk[:].bitcast(BF16)
                    for dti in range(8):
                        tr(bkb[:, dti * PT:(dti + 1) * PT], xn[k2][0:PT, dti * 128:(dti + 1) * 128], identb[0:PT, 0:PT],
                           [b_xn[k2], b_identb], [bb])
                    cp("act", n2T[:, :, col0:col0 + PT], bkb[:, 0:8 * PT].rearrange("p (a t) -> p a t", a=8), [bb], [b_n2T])

                b1a(0)
                for si in range(len(subtiles)):
                    if si + 1 < len(subtiles):
                        b1a(si + 1)
                    b1b(si)
                def wload(cc):
                    wj = (cnt["w"] + cc) % 3
                    dma("pool", wst[wj][:, :, 0:128], wfiv[:, :, cc * 128:(cc + 1) * 128], [], [b_wst[wj]])
                    dma("pool", wst[wj][:, :, 128:256], wfiv[:, :, DFF + cc * 128:DFF + (cc + 1) * 128], [], [b_wst[wj]])
                wload(0); wload(1)
                pend_fin = [None]
                for c in range(22):
                    wi = (cnt["w"] + c) % 3
                    if c + 2 < 22:
                        wload(c + 2)
                    if wfo_pending:
                        a_ = wfo_pending.pop(0)
                        dma("pool", wfo[:, a_, :], wfov[:, a_, :], [], [b_wfo])
                    for (c0, ncol, smp) in cgs:
                        ui = cnt["u"] % 3; uu = cnt["u"] % 2; cnt["u"] += 1
                        bks = [nb(), nb()]
                        for gv in range(2):
                            for dti in range(8):
                                mm(bks[gv][0][:, 0:ncol], wst[wi][:, dti, gv * 128:(gv + 1) * 128], n2T[:, dti, c0:c0 + ncol],
                                   dti == 0, dti == 7, [b_wst[wi], b_n2T], [bks[gv][1]])
                        for gv in range(2):
                            ci = gv * 22 + c
                            bk, bb = bks[gv]
                            U = upS[uu][gv]; bU = b_upS[uu][gv]; A = acc[ui][gv]; bA = b_acc[ui][gv]
                            w0 = cw[:, 0, ci:ci + 1]; w1 = cw[:, 1, ci:ci + 1]; w2 = cw[:, 2, ci:ci + 1]; bs = cw[:, 3, ci:ci + 1]
                            if F_CONVOLD:
                                b_h1 = b_hist[0]
                                if not smp:
                                    cp("pool", U[:, 0:2], hist[:, :, ci], [b_h1], [bU])
                                    cp("act", U[:, 2:2 + ncol], bk[:, 0:ncol], [bb], [bU])
                                    cp("pool", hist[:, :, ci], U[:, ncol:ncol + 2], [bU], [b_h1])
                                    act(A[:, 0:ncol], bk[:, 0:ncol], AF.Identity, [bb, b_cw], [bA], scale=w2, bias=bs)
                                    stt(A[:, 0:ncol], U[:, 1:1 + ncol], w1, A[:, 0:ncol], ALU.mult, ALU.add, [bU, b_cw, bA], [bA])
                                    stt(A[:, 0:ncol], U[:, 0:ncol], w0, A[:, 0:ncol], ALU.mult, ALU.add, [bU, b_cw, bA], [bA])
                                else:
                                    U3 = U[:, 0:96].rearrange("p (b t) -> p b t", t=6)
                                    A3 = A[:, 0:64].rearrange("p (b t) -> p b t", t=4)
                                    cp("pool", U3[:, :, 0:2], hist_s[:, ci, :].rearrange("p (b k) -> p b k", k=2), [b_hists], [bU])
                                    cp("act", U3[:, :, 2:6], bk[:, 0:64].rearrange("p (b t) -> p b t", t=4), [bb], [bU])
                                    cp("pool", cst_s[:, ci, :].rearrange("p (b k) -> p b k", k=2), U3[:, :, 4:6], [bU], [b_csts])
                                    act(A3, bk[:, 0:64].rearrange("p (b t) -> p b t", t=4), AF.Identity, [bb, b_cw], [bA], scale=w2, bias=bs)
                                    stt(A3, U3[:, :, 1:5], w1, A3, ALU.mult, ALU.add, [bU, b_cw, bA], [bA])
                                    stt(A3, U3[:, :, 0:4], w0, A3, ALU.mult, ALU.add, [bU, b_cw, bA], [bA])
                                continue
                            bH = b_upH[uu][gv]
                            if (not smp) and gv == 1:
                                hsv = hsave[ui]; bhs = b_hsave[ui]
                                cp("act", hsv[:, 0:2], hist[:, :, ci], [b_hist[ci]], [bhs])
                                cp("act", hist[:, :, ci], bk[:, ncol - 2:ncol], [bb], [b_hist[ci]])
                                act(A[:, 0:ncol], bk[:, 0:ncol], AF.Identity, [bb, b_cw], [bA], scale=w2, bias=bs)
                                stt(A[:, 1:ncol], bk[:, 0:ncol - 1], w1, A[:, 1:ncol], ALU.mult, ALU.add, [bb, b_cw, bA], [bA])
                                stt(A[:, 2:ncol], bk[:, 0:ncol - 2], w0, A[:, 2:ncol], ALU.mult, ALU.add, [bb, b_cw, bA], [bA])
                                stt(A[:, 0:1], hsv[:, 1:2], w1, A[:, 0:1], ALU.mult, ALU.add, [bhs, b_cw, bA], [bA])
                                stt(A[:, 0:2], hsv[:, 0:2], w0, A[:, 0:2], ALU.mult, ALU.add, [bhs, b_cw, bA], [bA])
                            elif not smp:
                                cp("act", U[:, 0:2], hist[:, :, ci], [b_hist[ci]], [bH])
                                cp("act", U[:, 2:2 + ncol], bk[:, 0:ncol], [bb], [bU])
                                cp("act", hist[:, :, ci], bk[:, ncol - 2:ncol], [bb], [b_hist[ci]])
                                act(A[:, 0:ncol], bk[:, 0:ncol], AF.Identity, [bb, b_cw], [bA], scale=w2, bias=bs)
                                stt(A[:, 0:ncol], U[:, 1:1 + ncol], w1, A[:, 0:ncol], ALU.mult, ALU.add, [bU, bH, b_cw, bA], [bA])
                                stt(A[:, 0:ncol], U[:, 0:ncol], w0, A[:, 0:ncol], ALU.mult, ALU.add, [bU, bH, b_cw, bA], [bA])
                            else:
                                U3 = U[:, 0:96].rearrange("p (b t) -> p b t", t=6)
                                A3 = A[:, 0:64].rearrange("p (b t) -> p b t", t=4)
                                bk3 = bk[:, 0:64].rearrange("p (b t) -> p b t", t=4)
                                cp("act", U3[:, :, 0:2], hist_s[:, ci, :].rearrange("p (b k) -> p b k", k=2), [b_hists], [bH])
                                cp("act", U3[:, :, 2:6], bk3, [bb], [bU])
                                cp("act", cst_s[:, ci, :].rearrange("p (b k) -> p b k", k=2), bk3[:, :, 2:4], [bb], [b_csts])
                                act(A3, bk3, AF.Identity, [bb, b_cw], [bA], scale=w2, bias=bs)
                                stt(A3, U3[:, :, 1:5], w1, A3, ALU.mult, ALU.add, [bU, bH, b_cw, bA], [bA])
                                stt(A3, U3[:, :, 0:4], w0, A3, ALU.mult, ALU.add, [bU, bH, b_cw, bA], [bA])
                        def fin(ui=ui, c=c, c0=c0, ncol=ncol):
                            act(acc[ui][0][:, 0:ncol], acc[ui][0][:, 0:ncol], AF.Gelu_apprx_tanh, [b_acc[ui][0]], [b_acc[ui][0]])
                            tt("pool", zt[:, c, c0:c0 + ncol], acc[ui][0][:, 0:ncol], acc[ui][1][:, 0:ncol], ALU.mult,
                               [b_acc[ui][0], b_acc[ui][1]], [b_z])
                        if pend_fin[0] is not None:
                            pend_fin[0]()
                        pend_fin[0] = fin
                if pend_fin[0] is not None:
                    pend_fin[0]()
                    pend_fin[0] = None
                cnt["w"] += 22
                for si, (hrow, PT, oap, col0) in enumerate(subtiles):
                    i3 = cnt["hb"] % 3; cnt["hb"] += 1
                    k2 = si % 2
                    dma("sp", hb[i3][0:PT, :], h1d[hrow:hrow + PT, :], [], [b_hb[i3]])
                    bks = [nb(), nb()]
                    for nh in range(2):
                        for c in range(22):
                            mm(bks[nh][0][0:PT, :], zt[:, c, col0:col0 + PT], wfo[:, c, nh * 512:(nh + 1) * 512], c == 0, c == 21,
                               [b_z, b_wfo], [bks[nh][1]])
                    for nh in range(2):
                        tt("dve", hb[i3][0:PT, nh * 512:(nh + 1) * 512], hb[i3][0:PT, nh * 512:(nh + 1) * 512], bks[nh][0][0:PT, :], ALU.add,
                           [b_hb[i3], bks[nh][1]], [b_hb[i3]])
                    act(junk[0:PT, :], hb[i3][0:PT, :], AF.Square, [b_hb[i3]], [b_junk, b_ssB[k2]], accum_out=ssB[k2][0:PT, :])
                    rstd(rsB[k2][0:PT, :], ssB[k2][0:PT, :], 1024.0, PT, 1, [b_ssB[k2]], [b_rsB[k2]], tmB[k2][0:PT, :], b_tmB[k2])
                    stt(hb[i3][0:PT, :], hb[i3][0:PT, :], rsB[k2][0:PT, 0:1], gfin_bc[0:PT, :], ALU.mult, ALU.mult,
                        [b_hb[i3], b_rsB[k2], b_gbc], [b_hb[i3]])
                    dma("sp", oap, hb[i3][0:PT, :], [b_hb[i3]], [], is_output=True)

            subA = [(128 * s, 128, yp[128 * s:128 * (s + 1), :], 128 * s) for s in range(8)]
            ffn_tile(subA, [(0, 512, False), (512, 512, False)])
            subB = [(1024 + 128 * s, 128, yp[1024 + 128 * s:1024 + 128 * (s + 1), :], 128 * s) for s in range(8)]
            subB.append((2048, 64, ys, 1024))
            ffn_tile(subB, [(0, 512, False), (512, 512, False), (1024, 64, True)])
            cpo = sbb("cpo", [88, 128]); b_cpo = Buf()
            bk, bb = nb()
            tr(bk[0:88, 0:128], hist[:].rearrange("p k c -> p (k c)"), identf[:], b_hist + [b_identf], [bb])
            cp("act", cpo[:], bk[0:88, 0:128], [bb], [b_cpo])
            dma("sp", cvp[0].rearrange("(c p) -> c p", p=128), cpo[0:44, :], [b_cpo], [], is_output=True)
            dma("sp", cvp[1].rearrange("(c p) -> c p", p=128), cpo[44:88, :], [b_cpo], [], is_output=True)
            cso = sbb("cso", [128, 11, 128]); b_cso = Buf()
            for q3 in range(3):
                bk, bb = nb()
                nblk = 4 if q3 < 2 else 3
                for bl in range(nblk):
                    cbk = q3 * 4 + bl
                    tr(bk[:, bl * 128:(bl + 1) * 128], cst_s[:, cbk * 4:(cbk + 1) * 4, :].rearrange("p c r -> p (c r)"), identf[:],
                       [b_csts, b_identf], [bb])
                cp("act", cso[:, q3 * 4:q3 * 4 + nblk, :], bk[:, 0:nblk * 128].rearrange("p (a c) -> p a c", a=nblk), [bb], [b_cso])
            cvsv = cvs.rearrange("r (cb cl p) -> cl r cb p", cl=4, p=128)
            for cl in range(4):
                dma("sp", cvsv[cl], cso[32 * cl:32 * cl + 32, :, :], [b_cso], [], is_output=True)
        P.build()
    return nc


_CACHE = {}


def _consts():
    ident = np.eye(128, dtype=np.float32)
    ii = np.arange(128) // 16
    mtoep = (ii[None, :] >= ii[:, None]).astype(np.float32)
    ar = np.arange(128)
    msgu = (ar[:, None] <= ar[None, :]).astype(np.float32)
    nv = np.zeros(32, np.float32)
    nv[0:8] = -np.arange(8)
    nv[8:16] = 7 - np.arange(8)
    nv[16:25] = np.arange(9)
    nv[25] = 8; nv[26] = 4; nv[27] = 1
    nvec = np.tile(nv[None, :], (128, 1)).astype(np.float32)
    sigma = np.ones((128, 1), np.float32)
    sigma[64:] = -1.0
    kvec = np.tile(np.arange(1, 65, dtype=np.float32)[None, :], (128, 1))
    return dict(c_ident=ident, c_mtoep=mtoep, c_msgu=msgu, c_nvec=nvec, c_sigma=sigma, c_kvec=kvec)


def kernel(x_prompt, x_sample, state_ssm_re, state_ssm_im, state_conv,
           g_mix, w_in, g_v, w_s, b_s, lam_re, lam_im, log_dt, b_re, b_im, c_re, c_im,
           d_skip, w_glu, b_glu, g_out_a, g_out_b, w_out, g_ffn, w_ffn_in, conv_w, conv_b,
           w_ffn_out, g_final):
    f = lambda a: np.ascontiguousarray(np.asarray(a, dtype=np.float32))
    if "nc" not in _CACHE:
        _CACHE["nc"] = build_nc()
    nc = _CACHE["nc"]
    shared = dict(
        g_mix=f(g_mix), w_in=f(w_in), g_v=f(g_v).reshape(512), w_s=f(w_s), b_s=f(b_s),
        lam_re=f(lam_re), lam_im=f(lam_im), log_dt=f(log_dt), b_re=f(b_re), b_im=f(b_im),
        c_re=f(c_re).reshape(512, 64), c_im=f(c_im).reshape(512, 64), d_skip=f(d_skip),
        w_glu=f(w_glu), b_glu=f(b_glu), g_out_a=f(g_out_a), g_out_b=f(g_out_b), w_out=f(w_out),
        g_ffn=f(g_ffn), w_ffn_in=f(w_ffn_in), conv_w=f(conv_w), conv_b=f(conv_b),
        w_ffn_out=f(w_ffn_out), g_final=f(g_final))
    shared.update(_consts())
    xpf = f(x_prompt); xsf = f(x_sample); sr = f(state_ssm_re); si = f(state_ssm_im); sc = f(state_conv)
    in_maps = []
    for c in range(NCORES):
        m = dict(shared)
        m["xp"] = xpf[c]
        m["xs"] = xsf[16 * c:16 * c + 16].reshape(64, D)
        m["sre"] = sr[16 * c:16 * c + 16]
        m["sim"] = si[16 * c:16 * c + 16]
        m["scv"] = sc[16 * c:16 * c + 16].reshape(32, 2 * DFF)
        in_maps.append(m)
    res = run_bass_kernel_spmd(nc, in_maps, core_ids=list(range(NCORES)))
    R = res.results
    y_prompt = np.stack([R[c]["yp"] for c in range(NCORES)], 0)
    y_sample = np.concatenate([R[c]["ys"].reshape(16, 4, D) for c in range(NCORES)], 0)
    v_sample = np.concatenate([R[c]["vs"].reshape(16, 4, 4, 128) for c in range(NCORES)], 0)
    srp = np.stack([R[c]["srep"] for c in range(NCORES)], 0)
    sip = np.stack([R[c]["simp"] for c in range(NCORES)], 0)
    cvp = np.stack([R[c]["cvp"] for c in range(NCORES)], 0)
    srs = np.concatenate([R[c]["sres"] for c in range(NCORES)], 0)
    sis = np.concatenate([R[c]["sims"] for c in range(NCORES)], 0)
    cvs = np.concatenate([R[c]["cvs"].reshape(16, 2, 2 * DFF) for c in range(NCORES)], 0)
    out = (y_prompt, y_sample, v_sample, srp, sip, cvp, srs, sis, cvs)
    return tuple(np.ascontiguousarray(o, dtype=np.float32) for o in out)
```

```python
import contextlib
import math
import numpy as np
import concourse.bass as bass
import concourse.mybir as mybir
from concourse.bass_utils import run_bass_kernel_spmd

F32 = mybir.dt.float32
BF16 = mybir.dt.bfloat16
I32 = mybir.dt.int32
AF = mybir.ActivationFunctionType
ALU = mybir.AluOpType
AX = mybir.AxisListType

NCORES = 8
import os as _os
F_POW = bool(int(_os.environ.get("F_POW", "0")))
F_SCANPOOL = bool(int(_os.environ.get("F_SCANPOOL", "0")))
F_CONVOLD = bool(int(_os.environ.get("F_CONVOLD", "0")))
F_TSACT = bool(int(_os.environ.get("F_TSACT", "0")))
D = 1024
DFF = 2816
EPS = 1e-6
TWO_PI = 2.0 * math.pi


class Buf:
    __slots__ = ("name", "last_w", "readers")

    def __init__(self, name=""):
        self.name = name
        self.last_w = None
        self.readers = []


class Op:
    __slots__ = ("eng", "kind", "emit", "deps", "marked", "seq", "dsem", "dval", "idx")

    def __init__(self, eng, kind, emit):
        self.eng = eng
        self.kind = kind
        self.emit = emit
        self.deps = []
        self.marked = False
        self.seq = 0
        self.dsem = None
        self.dval = 0


ENGS = ("pe", "act", "dve", "pool", "sp")


class Prog:
    def __init__(self, nc, n_dma_sems=12):
        self.nc = nc
        self.ops = {e: [] for e in ENGS}
        self.all_ops = []
        self.n_dma_sems = n_dma_sems
        self.dma_count = {e: 0 for e in ENGS}
        self.out_dmas = []
        self.since_barrier = []
        self.last_on_sem = {}

    def _add(self, op, reads, writes):
        deps = []
        for b in reads:
            if b.last_w is not None:
                deps.append(b.last_w)
        for b in writes:
            if b.last_w is not None:
                deps.append(b.last_w)
            deps.extend(b.readers)
        seen = set()
        for d in deps:
            if d is op or id(d) in seen:
                continue
            seen.add(id(d))
            if op.eng == "pe" and d.eng == "pe" and d.kind == "c" and op.kind == "c":
                continue
            op.deps.append(d)
            d.marked = True
        for b in reads:
            if op.kind == "c":
                b.readers = [r for r in b.readers if not (r.kind == "c" and r.eng == op.eng)]
            b.readers.append(op)
        for b in writes:
            b.last_w = op
            b.readers = []
        op.idx = len(self.all_ops)
        self.all_ops.append(op)
        self.ops[op.eng].append(op)
        self.since_barrier.append(op)
        return op

    def op(self, eng, emit, reads=(), writes=()):
        return self._add(Op(eng, "c", emit), reads, writes)

    def dma(self, queue, emit, reads=(), writes=(), is_output=False):
        o = Op(queue, "d", emit)
        k = self.dma_count[queue]
        self.dma_count[queue] = k + 1
        o.dsem = k % self.n_dma_sems
        o.dval = 16 * (k // self.n_dma_sems + 1)
        prev = self.last_on_sem.get((queue, o.dsem))
        if prev is not None:
            o.deps.append(prev)
        self.last_on_sem[(queue, o.dsem)] = o
        self._add(o, reads, writes)
        if is_output:
            self.out_dmas.append(o)
        return o

    def barrier(self, exclude=()):
        lasts = []
        for e in ENGS:
            lastc = None
            for o in self.ops[e]:
                if o.kind == "c":
                    lastc = o
            if lastc is not None:
                lasts.append(lastc)
        dmas = [o for o in self.since_barrier if o.kind == "d" and id(o) not in exclude]
        self.since_barrier = []
        for e in ("pe", "act", "dve", "pool", "sp"):
            o = Op(e, "c", lambda eng: eng.nop())
            for d in lasts + dmas:
                o.deps.append(d)
                d.marked = True
            o.idx = len(self.all_ops)
            self.all_ops.append(o)
            self.ops[e].append(o)

    def build(self):
        nc = self.nc
        with contextlib.ExitStack() as st:
            esem = {}
            for e in ("pe", "act", "dve", "pool", "sp"):
                esem[e] = st.enter_context(nc.semaphore("s_" + e))
            dsem = {}
            for q in ENGS:
                if self.dma_count[q] > 0:
                    dsem[q] = [
                        st.enter_context(nc.semaphore("d_%s_%d" % (q, i)))
                        for i in range(min(self.n_dma_sems, self.dma_count[q]))
                    ]
            for e in ENGS:
                c = 0
                for o in self.ops[e]:
                    if o.kind == "c" and o.marked:
                        c += 1
                        o.seq = c
            import os
            if os.environ.get("KDEBUG"):
                for e in ENGS:
                    print("ENG", e, "ops", len(self.ops[e]), "marked", max([o.seq for o in self.ops[e]] + [0]), "dmas", self.dma_count[e])
            block = st.enter_context(nc.Block())

            def run(ename, eng):
                waited = {}
                for o in self.ops[ename]:
                    for d in o.deps:
                        if d.kind == "c":
                            key = ("c", d.eng)
                            val = d.seq
                            sem = esem[d.eng]
                        else:
                            key = ("d", d.eng, d.dsem)
                            val = d.dval
                            sem = dsem[d.eng][d.dsem]
                        if waited.get(key, 0) >= val:
                            continue
                        waited[key] = val
                        eng.wait_ge(sem, val)
                    ins = o.emit(eng)
                    if o.kind == "c":
                        if o.marked:
                            ins.then_inc(esem[ename], 1)
                    else:
                        ins.then_inc(dsem[ename][o.dsem], 16)
                fin = {}
                for o in self.out_dmas:
                    if o.eng == ename:
                        fin[o.dsem] = max(fin.get(o.dsem, 0), o.dval)
                for s, v in fin.items():
                    eng.wait_ge(dsem[ename][s], v)

            @block.sync
            def _(e):
                run("sp", e)

            @block.tensor
            def _(e):
                run("pe", e)

            @block.scalar
            def _(e):
                run("act", e)

            @block.vector
            def _(e):
                run("dve", e)

            @block.gpsimd
            def _(e):
                run("pool", e)


def build_nc():
    nc = bass.Bass("TRN2", target_bir_lowering=False)
    P = Prog(nc)

    def din(name, shape):
        return nc.dram_tensor(name, list(shape), F32, kind="ExternalInput").ap()

    def dout(name, shape):
        return nc.dram_tensor(name, list(shape), F32, kind="ExternalOutput").ap()

    xp = din("xp", [2048, D]); xs = din("xs", [64, D])
    sre = din("sre", [16, 32, 64]); sim = din("sim", [16, 32, 64]); scv = din("scv", [32, 2 * DFF])
    g_mix = din("g_mix", [D]); w_in = din("w_in", [D, 1536]); g_v = din("g_v", [512])
    w_s = din("w_s", [4, 128, 128]); b_s = din("b_s", [4, 128])
    lam_re = din("lam_re", [32, 64]); lam_im = din("lam_im", [32, 64]); log_dt = din("log_dt", [32])
    b_re = din("b_re", [32, 64, 16]); b_im = din("b_im", [32, 64, 16])
    c_re = din("c_re", [512, 64]); c_im = din("c_im", [512, 64]); d_skip = din("d_skip", [512])
    w_glu = din("w_glu", [512, 512]); b_glu = din("b_glu", [512])
    g_out_a = din("g_out_a", [512]); g_out_b = din("g_out_b", [512])
    w_out = din("w_out", [D, D]); g_ffn = din("g_ffn", [D]); w_ffn_in = din("w_ffn_in", [D, 2 * DFF])
    conv_w = din("conv_w", [3, 2 * DFF]); conv_b = din("conv_b", [2 * DFF])
    w_ffn_out = din("w_ffn_out", [DFF, D]); g_final = din("g_final", [D])
    c_ident = din("c_ident", [128, 128]); c_mtoep = din("c_mtoep", [128, 128]); c_msgu = din("c_msgu", [128, 128])
    c_nvec = din("c_nvec", [128, 32]); c_sigma = din("c_sigma", [128, 1]); c_kvec = din("c_kvec", [128, 64])

    yp = dout("yp", [2048, D]); ys = dout("ys", [64, D]); vs = dout("vs", [64, 512])
    srep = dout("srep", [32, 64]); simp = dout("simp", [32, 64]); cvp = dout("cvp", [2, 2 * DFF])
    sres = dout("sres", [16, 32, 64]); sims = dout("sims", [16, 32, 64]); cvs = dout("cvs", [32, 2 * DFF])
    h1d = nc.dram_tensor("h1d", [2112, D], F32).ap()
    zscr = nc.dram_tensor("zscr", [2112, 512], BF16).ap()
    b_zscr = Buf()

    ES = contextlib.ExitStack()
    with ES:
        def sb(name, shape, dt=F32, stack=ES):
            return stack.enter_context(nc.sbuf_tensor(name, list(shape), dt))

        banks = [ES.enter_context(nc.psum_tensor("bank%d" % i, [128, 512], F32)) for i in range(8)]
        bbufs = [Buf("bank%d" % i) for i in range(8)]
        bctr = [0]

        def nb():
            i = bctr[0] % 8
            bctr[0] += 1
            return banks[i], bbufs[i]

        def mm(out, lhsT, rhs, start, stop, reads, writes):
            P.op("pe", lambda e: e.matmul(out, lhsT, rhs, start=start, stop=stop), reads, writes)

        def tr(out, in_, ident, reads, writes):
            P.op("pe", lambda e: e.transpose(out, in_, ident), reads, writes)

        def act(out, in_, func, reads, writes, **kw):
            P.op("act", lambda e: e.activation(out=out, in_=in_, func=func, **kw), reads, writes)

        def tt(eng, out, in0, in1, op, reads, writes):
            P.op(eng, lambda e: e.tensor_tensor(out=out, in0=in0, in1=in1, op=op), reads, writes)

        def ts(eng, out, in0, s1, s2, op0, op1, reads, writes):
            if s2 is None:
                P.op(eng, lambda e: e.tensor_scalar(out=out, in0=in0, scalar1=s1, scalar2=None, op0=op0), reads, writes)
            else:
                P.op(eng, lambda e: e.tensor_scalar(out=out, in0=in0, scalar1=s1, scalar2=s2, op0=op0, op1=op1), reads, writes)

        def stt(out, in0, scalar, in1, op0, op1, reads, writes):
            P.op("dve", lambda e: e.scalar_tensor_tensor(out=out, in0=in0, scalar=scalar, in1=in1, op0=op0, op1=op1), reads, writes)

        def cp(eng, out, in_, reads, writes):
            if eng == "act":
                P.op(eng, lambda e: e.activation(out=out, in_=in_, func=AF.Copy), reads, writes)
            else:
                P.op(eng, lambda e: e.tensor_copy(out=out, in_=in_), reads, writes)

        def dma(q, out, in_, reads, writes, is_output=False, **kw):
            P.dma(q, lambda e: e.dma_start(out=out, in_=in_, **kw), reads, writes, is_output=is_output)

        identf = sb("identf", [128, 128]); b_identf = Buf()
        identb = sb("identb", [128, 128], BF16); b_identb = Buf()
        onesb = sb("onesb", [128, 128], BF16); b_ones = Buf()
        epst = sb("epst", [128, 1]); b_eps = Buf()
        mhalf = sb("mhalf", [128, 512]); b_mhalf = Buf()
        sigma = sb("sigma", [128, 1]); b_sigma = Buf()
        ARR8 = sb("ARR8", [128, 2, 32]); AII8 = sb("AII8", [128, 2, 32]); b_A8 = Buf()
        AR4 = sb("AR4", [128, 32]); AI4 = sb("AI4", [128, 32]); b_A4 = Buf()
        ST = [sb("ST%d" % i, [128, 3, 32]) for i in range(2)]
        b_ST = [Buf(), Buf()]
        SM = contextlib.ExitStack()
        SM.__enter__()
        Toep = sb("Toep", [128, 32, 128], BF16, SM); b_Toep = Buf()
        MinX = sb("MinX", [128, 32, 192], BF16, SM); b_MinX = Buf()
        Mout = sb("Mout", [128, 32, 128], BF16, SM); b_Mout = Buf()
        w_in_sb = sb("w_in_sb", [128, 8, 1536], BF16, SM); b_win = Buf()
        w_glu_sb = sb("w_glu_sb", [128, 4, 512], BF16, SM); b_wglu = Buf()
        w_out_sb = sb("w_out_sb", [128, 8, 1024], BF16, SM); b_wout = Buf()
        TC = sb("TC", [128, 32, 64], F32, SM); TSs = sb("TSs", [128, 32, 64], F32, SM); b_tab = Buf()
        MG8 = sb("MG8", [128, 32], F32, SM); b_mg8 = Buf()
        C1 = sb("C1", [128, 32], F32, SM); C2 = sb("C2", [128, 32], F32, SM); b_C = Buf()
        th8c = sb("th8c", [128, 32], F32, SM); b_th8c = Buf()
        wiv = w_in.rearrange("(a p) n -> p a n", p=128)
        n_pref0 = len(P.all_ops)
        for a in range(8):
            dma("pool", w_in_sb[:, a, 0:1024], wiv[:, a, 0:1024], [], [b_win])
        dma("pool", w_glu_sb[:], w_glu.rearrange("(a p) n -> p a n", p=128), [], [b_wglu])
        wov = w_out.rearrange("(a p) n -> p a n", p=128)
        P.op("pool", lambda e: e.memset(C1[:], 0.0), [], [b_C])
        P.op("pool", lambda e: e.memset(C2[:], 0.0), [], [b_C])

        dma("sp", identf[:], c_ident, [], [b_identf])
        dma("sp", sigma[:], c_sigma, [], [b_sigma])
        cp("dve", identb[:], identf[:], [b_identf], [b_identb])
        P.op("pool", lambda e: e.memset(onesb[:], 1.0), [], [b_ones])
        P.op("pool", lambda e: e.memset(epst[:], EPS), [], [b_eps])
        P.op("pool", lambda e: e.memset(mhalf[:], -0.5), [], [b_mhalf])
        P.op("pool", lambda e: e.memset(ST[0][:], 0.0), [], [b_ST[0]])

        def rstd(out, ssum, n, pt, width, reads, writes, tmp, b_tmp):
            if F_POW:
                ts("dve", tmp, ssum, 1.0 / n, EPS, ALU.mult, ALU.add, reads, [b_tmp])
                tt("pool", out, tmp, mhalf[0:pt, 0:width], ALU.pow, [b_tmp, b_mhalf], writes)
                return
            act(tmp, ssum, AF.Sqrt, list(reads) + [b_eps], [b_tmp], scale=1.0 / n, bias=epst[0:pt, :])
            P.op("dve", lambda e: e.reciprocal(out, tmp), [b_tmp], writes)

        with contextlib.ExitStack() as S0:
            def sb0(name, shape, dt=F32):
                return sb(name, shape, dt, S0)

            wxs = sb0("wxs", [128, 8, 512], BF16); b_wxs = Buf()
            dma("pool", wxs[:], wiv[:, :, 1024:1536], [], [b_wxs])
            for a in range(8):
                dma("pool", w_out_sb[:, a, :], wov[:, a, :], [], [b_wout])
            n_pref1 = len(P.all_ops)
            cp("act", w_in_sb[:, :, 1024:1536].rearrange("p a (c g) -> p a c g", g=32),
               wxs[:].rearrange("p a (g c) -> p a c g", c=16), [b_wxs], [b_win])
            mtoep = sb0("mtoep", [128, 128]); b_mtoep = Buf()
            nvec = sb0("nvec", [128, 32]); b_nvec = Buf()
            dma("sp", mtoep[:], c_mtoep, [], [b_mtoep])
            dma("sp", nvec[:], c_nvec, [], [b_nvec])
            L2 = sb0("L2", [32, 256]); b_L2 = Buf()
            dma("sp", L2[:, 0:64], lam_re, [], [b_L2]); dma("sp", L2[:, 64:128], lam_re, [], [b_L2])
            dma("sp", L2[:, 128:192], lam_im, [], [b_L2]); dma("sp", L2[:, 192:256], lam_im, [], [b_L2])
            lr = sb0("lr", [128, 32]); li = sb0("li", [128, 32]); b_l = Buf()
            bk, bb = nb()
            tr(bk[:, 0:32], L2[:, 0:128], identf[0:32, 0:32], [b_L2, b_identf], [bb])
            tr(bk[:, 32:64], L2[:, 128:256], identf[0:32, 0:32], [b_L2, b_identf], [bb])
            cp("dve", lr[:], bk[:, 0:32], [bb], [b_l]); cp("dve", li[:], bk[:, 32:64], [bb], [b_l])
            dtb = sb0("dtb", [128, 32]); b_dt = Buf()
            dma("sp", dtb[:], log_dt.partition_broadcast(128), [], [b_dt])
            act(dtb[:], dtb[:], AF.Exp, [b_dt], [b_dt])
            rho = sb0("rho", [128, 32]); th = sb0("th", [128, 32]); b_rt = Buf()
            tt("dve", rho[:], lr[:], dtb[:], ALU.mult, [b_l, b_dt], [b_rt])
            tt("dve", th[:], li[:], dtb[:], ALU.mult, [b_l, b_dt], [b_rt])
            NQ = 32
            shp = [128, 32, NQ]
            ARG = sb0("ARG", shp); RHO = sb0("RHO", shp); b_arg = Buf()
            nv_b = nvec[:].unsqueeze(1).broadcast_to(shp)
            tt("dve", ARG[:], th[:].unsqueeze(2).broadcast_to(shp), nv_b, ALU.mult, [b_rt, b_nvec], [b_arg])
            tt("dve", RHO[:], rho[:].unsqueeze(2).broadcast_to(shp), nv_b, ALU.mult, [b_rt, b_nvec], [b_arg])
            mag = RHO; b_mag = Buf()
            act(mag[:], RHO[:], AF.Exp, [b_arg], [b_mag, b_arg])
            ki = sb0("ki", shp, I32); kf = sb0("kf", shp); rr = sb0("rr", shp); mk = sb0("mk", shp); b_red = Buf()
            sinv = sb0("sinv", shp); cosv = sb0("cosv", shp); b_sc = Buf()

            def sin_of(outt, shift):
                ts("dve", rr[:], ARG[:], shift, None, ALU.add, None, [b_arg], [b_red])
                ts("dve", kf[:], rr[:], 1.0 / TWO_PI, None, ALU.mult, None, [b_red], [b_red])
                cp("dve", ki[:], kf[:], [b_red], [b_red])
                cp("dve", kf[:], ki[:], [b_red], [b_red])
                stt(rr[:], kf[:], -TWO_PI, rr[:], ALU.mult, ALU.add, [b_red], [b_red])
                ts("dve", mk[:], rr[:], math.pi, None, ALU.is_gt, None, [b_red], [b_red])
                stt(rr[:], mk[:], -TWO_PI, rr[:], ALU.mult, ALU.add, [b_red], [b_red])
                ts("dve", mk[:], rr[:], -math.pi, None, ALU.is_lt, None, [b_red], [b_red])
                stt(rr[:], mk[:], TWO_PI, rr[:], ALU.mult, ALU.add, [b_red], [b_red])
                ts("dve", rr[:], rr[:], math.pi, -math.pi, ALU.min, ALU.max, [b_red], [b_red])
                act(outt[:], rr[:], AF.Sin, [b_red], [b_sc])

            sin_of(sinv, 0.0)
            sin_of(cosv, math.pi / 2)
            PA = cosv; PB = sinv; b_P = b_sc
            tt("dve", PA[:], mag[:], cosv[:], ALU.mult, [b_mag, b_sc], [b_P])
            tt("dve", PB[:], mag[:], sinv[:], ALU.mult, [b_mag, b_sc], [b_P])
            cp("dve", MG8[:], mag[:, :, 25], [b_mag], [b_mg8])
            den = sb0("den", [128, 32]); t0 = sb0("t0", [128, 32]); t1 = sb0("t1", [128, 32])
            fr = sb0("fr", [128, 32]); fi = sb0("fi", [128, 32]); am1 = sb0("am1", [128, 32]); b_f = Buf()
            tt("dve", den[:], lr[:], lr[:], ALU.mult, [b_l], [b_f])
            tt("dve", t0[:], li[:], li[:], ALU.mult, [b_l], [b_f])
            tt("dve", den[:], den[:], t0[:], ALU.add, [b_f], [b_f])
            P.op("dve", lambda e: e.reciprocal(den[:], den[:]), [b_f], [b_f])
            ts("dve", am1[:], PA[:, :, 27], -1.0, None, ALU.add, None, [b_P], [b_f])
            tt("dve", t0[:], am1[:], lr[:], ALU.mult, [b_f, b_l], [b_f])
            tt("dve", t1[:], PB[:, :, 27], li[:], ALU.mult, [b_P, b_l], [b_f])
            tt("dve", t0[:], t0[:], t1[:], ALU.add, [b_f], [b_f])
            tt("dve", fr[:], t0[:], den[:], ALU.mult, [b_f], [b_f])
            tt("dve", t0[:], PB[:, :, 27], lr[:], ALU.mult, [b_P, b_l], [b_f])
            tt("dve", t1[:], am1[:], li[:], ALU.mult, [b_f, b_l], [b_f])
            tt("dve", t0[:], t0[:], t1[:], ALU.subtract, [b_f], [b_f])
            tt("dve", fi[:], t0[:], den[:], ALU.mult, [b_f], [b_f])
            shp16 = [128, 32, 16]
            FRn = sb0("FRn", shp16); FIs = sb0("FIs", shp16); tq = sb0("tq", shp16); b_FR = Buf()
            frb = fr[:].unsqueeze(2).broadcast_to(shp16); fib = fi[:].unsqueeze(2).broadcast_to(shp16)
            tt("dve", FRn[:], PA[:, :, 0:16], frb, ALU.mult, [b_P, b_f], [b_FR])
            tt("dve", tq[:], PB[:, :, 0:16], fib, ALU.mult, [b_P, b_f], [b_FR])
            tt("dve", FRn[:], FRn[:], tq[:], ALU.subtract, [b_FR], [b_FR])
            tt("dve", FIs[:], PA[:, :, 0:16], fib, ALU.mult, [b_P, b_f], [b_FR])
            tt("dve", tq[:], PB[:, :, 0:16], frb, ALU.mult, [b_P, b_f], [b_FR])
            tt("dve", FIs[:], FIs[:], tq[:], ALU.add, [b_FR], [b_FR])
            ts("dve", FIs[:], FIs[:], sigma[:, 0:1], None, ALU.mult, None, [b_FR, b_sigma], [b_FR])
            BT1 = sb0("BT1", [128, 32, 16]); BT2 = sb0("BT2", [128, 32, 16]); b_BT = Buf()
            brv = b_re.rearrange("g p c -> p g c"); biv = b_im.rearrange("g p c -> p g c")
            dma("sp", BT1[0:64], brv, [], [b_BT]); dma("sp", BT1[64:128], biv, [], [b_BT])
            dma("sp", BT2[0:64], biv, [], [b_BT]); dma("sp", BT2[64:128], brv, [], [b_BT])
            shp4 = [128, 32, 8, 16]
            BBneg = sb0("BBneg", shp4); MinT = BBneg; b_BB = Buf()
            bt1b = BT1[:].unsqueeze(2).broadcast_to(shp4); bt2b = BT2[:].unsqueeze(2).broadcast_to(shp4)

            def build_bb(dst, q0, tbuf, b_tbuf):
                tt("dve", dst[:], FRn[:, :, q0:q0 + 8].unsqueeze(3).broadcast_to(shp4), bt1b, ALU.mult, [b_FR, b_BT], [b_BB])
                tt("dve", tbuf, FIs[:, :, q0:q0 + 8].unsqueeze(3).broadcast_to(shp4), bt2b, ALU.mult, [b_FR, b_BT], [b_tbuf])
                tt("dve", dst[:], dst[:], tbuf, ALU.subtract, [b_BB, b_tbuf], [b_BB])
            CT1 = sb0("CT1", [128, 32, 16]); CT2 = sb0("CT2", [128, 32, 16]); b_CT = Buf()
            CL = sb0("CL", [128, 4, 256]); b_CL = Buf()
            crv = c_re.rearrange("(r p) k -> p r k", p=128); civ = c_im.rearrange("(r p) k -> p r k", p=128)
            dma("sp", CL[:, :, 0:64], crv, [], [b_CL]); dma("sp", CL[:, :, 64:128], civ, [], [b_CL])
            dma("sp", CL[:, :, 128:192], civ, [], [b_CL]); dma("sp", CL[:, :, 192:256], crv, [], [b_CL])
            for half, dst in ((0, CT1), (1, CT2)):
                bk, bb = nb()
                for r in range(4):
                    tr(bk[:, r * 128:(r + 1) * 128], CL[:, r, half * 128:(half + 1) * 128], identf[:], [b_CL, b_identf], [bb])
                cp("dve", dst[:].rearrange("p g c -> p (g c)"), bk[:, 0:512], [bb], [b_CT])
            shp9 = [128, 32, 9, 16]
            EE = sb0("EE", shp9); te = sb0("te", shp9); PAs = sb0("PAs", [128, 32, 9]); b_EE = Buf(); b_te = Buf()
            ts("dve", PAs[:], PA[:, :, 16:25], sigma[:, 0:1], None, ALU.mult, None, [b_P, b_sigma], [b_EE])
            tt("dve", EE[:], PAs[:].unsqueeze(3).broadcast_to(shp9), CT1[:].unsqueeze(2).broadcast_to(shp9), ALU.mult, [b_EE, b_CT], [b_EE])
            tt("dve", te[:], PB[:, :, 16:25].unsqueeze(3).broadcast_to(shp9), CT2[:].unsqueeze(2).broadcast_to(shp9), ALU.mult, [b_P, b_CT], [b_te])
            tt("dve", EE[:], EE[:], te[:], ALU.subtract, [b_EE, b_te], [b_EE])
            build_bb(BBneg, 0, te[:, :, 0:8, :], b_te)
            cp("dve", Mout[:].rearrange("p g (j c) -> p g j c", c=16), EE[:, :, 1:9, :], [b_EE], [b_Mout])
            Drep = sb0("Drep", [128, 32]); b_Drep = Buf()
            dsv = d_skip.rearrange("(g c) -> c g", c=16)
            for i in range(8):
                dma("sp", Drep[16 * i:16 * i + 16, :], dsv, [], [b_Drep], allow_slow_non_contiguous=True)
            tmpT = te[:, 0:4, 0:8, :].rearrange("p g n c -> p g (n c)"); b_tmpT = b_te
            for gq in range(8):
                bk, bb = nb()
                for gl in range(4):
                    g = gq * 4 + gl
                    mm(bk[:, gl * 128:(gl + 1) * 128], BBneg[:, g].rearrange("p i c -> p (i c)"),
                       EE[:, g, 0:8, :].rearrange("p n c -> p (n c)"), True, True, [b_BB, b_EE], [bb])
                tt("dve", tmpT, bk[:, 0:512].rearrange("p (g c) -> p g c", g=4),
                   mtoep[:].unsqueeze(1).broadcast_to([128, 4, 128]), ALU.mult, [bb, b_mtoep], [b_tmpT])
                for gl in range(4):
                    g = gq * 4 + gl
                    stt(Toep[:, g, :], identf[:], Drep[:, g:g + 1], tmpT[:, gl, :], ALU.mult, ALU.add,
                        [b_identf, b_Drep, b_tmpT], [b_Toep])
            build_bb(MinT, 8, te[:, :, 0:8, :], b_te)
            for gq in range(8):
                bk, bb = nb()
                for gl in range(4):
                    g = gq * 4 + gl
                    tr(bk[:, gl * 128:(gl + 1) * 128], MinT[:, g].rearrange("p i c -> p (i c)"), identf[:], [b_BB, b_identf], [bb])
                bv = bk[:, 0:512].rearrange("p (g c) -> p g c", g=4)
                cp("dve", MinX[:, gq * 4:gq * 4 + 4, 0:128], bv, [bb], [b_MinX])
                cp("act", MinX[:, gq * 4:gq * 4 + 4, 128:192], bv[:, :, 0:64], [bb], [b_MinX])
            cp("dve", ARR8[:, 0, :], PA[:, :, 25], [b_P], [b_A8]); cp("dve", ARR8[:, 1, :], PA[:, :, 25], [b_P], [b_A8])
            ts("dve", AII8[:, 1, :], PB[:, :, 25], sigma[:, 0:1], None, ALU.mult, None, [b_P, b_sigma], [b_A8])
            ts("dve", AII8[:, 0, :], AII8[:, 1, :], -1.0, None, ALU.mult, None, [b_A8], [b_A8])
            cp("dve", AR4[:], PA[:, :, 26], [b_P], [b_A4])
            ts("dve", AI4[:], PB[:, :, 26], sigma[:, 0:1], -1.0, ALU.mult, ALU.mult, [b_P, b_sigma], [b_A4])
            ts("dve", th8c[:], th[:], 8.0, None, ALU.mult, None, [b_rt], [b_th8c])
            P.barrier(exclude=set(id(o) for o in P.all_ops[n_pref0:n_pref1]))

        with contextlib.ExitStack() as S1:
            shpk = [128, 32, 64]
            kvec = sb("kvec", [128, 64], F32, S1); b_kvec = Buf()
            dma("sp", kvec[:], c_kvec, [], [b_kvec])
            th8r = sb("th8r", [128, 32], F32, S1); b_th8r = Buf()
            kA = sb("kA", shpk, F32, S1); kR = sb("kR", shpk, F32, S1); kF = sb("kF", shpk, F32, S1)
            kI = sb("kI", shpk, I32, S1); kM = sb("kM", shpk, F32, S1); b_k = Buf()

            def reduce_pi(rr_, src, shift, kf_, ki_, mk_, rd, wr):
                ts("dve", rr_, src, shift, None, ALU.add, None, rd, wr)
                ts("dve", kf_, rr_, 1.0 / TWO_PI, None, ALU.mult, None, wr, wr)
                cp("dve", ki_, kf_, wr, wr)
                cp("dve", kf_, ki_, wr, wr)
                stt(rr_, kf_, -TWO_PI, rr_, ALU.mult, ALU.add, wr, wr)
                ts("dve", mk_, rr_, math.pi, None, ALU.is_gt, None, wr, wr)
                stt(rr_, mk_, -TWO_PI, rr_, ALU.mult, ALU.add, wr, wr)
                ts("dve", mk_, rr_, -math.pi, None, ALU.is_lt, None, wr, wr)
                stt(rr_, mk_, TWO_PI, rr_, ALU.mult, ALU.add, wr, wr)
                ts("dve", rr_, rr_, math.pi, -math.pi, ALU.min, ALU.max, wr, wr)

            reduce_pi(th8r[:], th8c[:], 0.0, kF[:, :, 0], kI[:, :, 0], kM[:, :, 0], [b_th8c], [b_th8r, b_k])
            tt("dve", kA[:], th8r[:].unsqueeze(2).broadcast_to(shpk), kvec[:].unsqueeze(1).broadcast_to(shpk), ALU.mult,
               [b_th8r, b_kvec], [b_k])
            reduce_pi(kR[:], kA[:], 0.0, kF[:], kI[:], kM[:], [b_k], [b_k])
            act(TSs[:], kR[:], AF.Sin, [b_k], [b_tab])
            ts("dve", TSs[:], TSs[:], sigma[:, 0:1], None, ALU.mult, None, [b_tab, b_sigma], [b_tab])
            reduce_pi(kR[:], kA[:], math.pi / 2, kF[:], kI[:], kM[:], [b_k], [b_k])
            act(TC[:], kR[:], AF.Sin, [b_k], [b_tab])
            P.barrier(exclude=set(id(o) for o in P.all_ops[n_pref0:n_pref1]))

        with contextlib.ExitStack() as SA:
            def sba(name, shape, dt=F32):
                return sb(name, shape, dt, SA)

            gmix_bc = sba("gmix_bc", [128, 1024]); b_gmix = Buf()
            dma("sp", gmix_bc[:], g_mix.partition_broadcast(128), [], [b_gmix])
            gv_bc = sba("gv_bc", [128, 512]); b_gv = Buf()
            dma("sp", gv_bc[:], g_v.partition_broadcast(128), [], [b_gv])
            goa_bc = sba("goa_bc", [128, 512]); b_goa = Buf()
            dma("sp", goa_bc[:], g_out_a.partition_broadcast(128), [], [b_goa])
            gob = sba("gob", [128, 4]); bglu = sba("bglu", [128, 4]); b_col = Buf()
            bsT = sba("bsT", [128, 4]); bsT_s = sba("bsT_s", [64, 4])
            def sa_late():
                dma("sp", gob[:], g_out_b.rearrange("(m p) -> p m", p=128), [], [b_col], allow_slow_non_contiguous=True)
                dma("sp", bglu[:], b_glu.rearrange("(m p) -> p m", p=128), [], [b_col], allow_slow_non_contiguous=True)
                dma("sp", bsT[:], b_s.rearrange("h t -> t h"), [], [b_col], allow_slow_non_contiguous=True)
                for b in range(16):
                    dma("sp", bsT_s[4 * b:4 * b + 4, :], b_s[:, 0:4].rearrange("h t -> t h"), [], [b_col], allow_slow_non_contiguous=True)
                for m_ in range(4):
                    ts("dve", w_out_sb[:, 4 + m_, :], w_out_sb[:, 4 + m_, :], gob[:, m_:m_ + 1], None, ALU.mult, None, [b_wout, b_col], [b_wout])

            msgu = sba("msgu", [128, 128]); b_msgu = Buf()
            dma("sp", msgu[:], c_msgu, [], [b_msgu])
            Msg = sba("Msg", [128, 4, 128], BF16); Msg_s = sba("Msg_s", [64, 4, 64], BF16); b_Msg = Buf()
            xa = [sba("xa%d" % i, [128, 1024]) for i in range(2)]; b_xa = [Buf(), Buf()]
            xh = [sba("xh%d" % i, [128, 1024]) for i in range(2)]; b_xh = [Buf(), Buf()]
            b_ss_s = [Buf() for _ in range(4)]; b_rs_s = [Buf() for _ in range(4)]; b_tm_s = [Buf() for _ in range(4)]
            b_xs_s = [Buf() for _ in range(4)]
            cnt_a = {"xh": 0}
            ss4 = sba("ss4", [128, 4]); rs4 = sba("rs4", [128, 4]); tm4 = sba("tm4", [128, 4]); b_ss = Buf(); b_rs = Buf(); b_tm4 = Buf()
            xn = [sba("xn%d" % i, [128, 1024], BF16) for i in range(2)]; b_xn = [Buf(), Buf()]
            n1T = sba("n1T", [128, 8, 512], BF16); b_n1T = Buf(); b_n1T_list = [Buf() for _ in range(4)]
            u_tm = [sba("u_tm%d" % i, [128, 512], BF16) for i in range(2)]; b_u = [Buf(), Buf()]
            vsq = sba("vsq", [128, 512], BF16); b_vsq = Buf()
            ssv = sba("ssv", [128, 4]); rv = sba("rv", [128, 4]); tmv = sba("tmv", [128, 4]); b_ssv = Buf(); b_rv = Buf(); b_tmv = Buf()
            vtmp = sba("vtmp", [128, 512]); b_vtmp = Buf()
            vnf_alias = True
            vn = [sba("vn%d" % i, [128, 512], BF16) for i in range(2)]; b_vn = [Buf(), Buf()]
            xs_tm = sba("xs_tm", [128, 4, 512], BF16); b_xstm = Buf()
            a_tms = [sba("a_tm%d" % i, [128, 512]) for i in range(2)]; b_atms = [Buf(), Buf()]
            vnf = a_tms[1]; b_vnf = b_atms[1]
            ssa = sba("ssa", [128, 1]); ra = sba("ra", [128, 1]); tma = sba("tma", [128, 1]); b_ssa = Buf(); b_ra = Buf(); b_tma = Buf()
            mixas = [sba("mixa%d" % i, [128, 512], BF16) for i in range(2)]; b_mixas = [Buf(), Buf()]
            mixTa = [sba("mixTa%d" % i, [128, 4, 512], BF16) for i in range(2)]; b_mixTa = [Buf(), Buf()]
            mixTb = sba("mixTb", [128, 4, 512], BF16); b_mixTb = Buf()
            Z = sba("Z", [64, 8, 512], BF16); b_Zs = [Buf() for _ in range(4)]
            X2 = sba("X2", [128, 32, 64], BF16); b_X2 = Buf()
            R1 = sba("R1", [128, 32, 64]); R2 = sba("R2", [128, 32, 64])
            b_R1 = [Buf() for _ in range(32)]; b_R2 = [Buf() for _ in range(32)]
            mA = sba("mA", [128, 8, 64]); mB = sba("mB", [128, 8, 64]); b_mA = Buf(); b_mB = Buf()
            cA = sba("cA", [128, 32]); cB = sba("cB", [128, 32]); b_cA = Buf(); b_cB = Buf()
            HPb = sba("HPb", [128, 32, 64], BF16); b_HPb = Buf()
            gTb = sba("gTb", [128, 4, 512], BF16); b_gTb = Buf()
            junk = gTb[:].rearrange("p a t -> p (a t)")[:, 0:1024]; b_junk = b_gTb
            sg = sba("sg", [128, 512]); b_sg = Buf()
            R1f = R1[:].rearrange("p g k -> p (g k)"); R2f = R2[:].rearrange("p g k -> p (g k)")
            allR1 = b_R1; allR2 = b_R2
            sqb = R1f.bitcast(BF16)[:, 0:2048].rearrange("p (a t) -> p a t", a=4); b_sqb = b_R1[0]
            rb_bc = R2f[:, 0:512]; tmb = R2f[:, 512:1024]; b_rb = b_R2[0]; b_tmb = b_R2[8]
            stS = sba("stS", [32, 128]); b_stS = Buf()
            rb4 = sba("rb4", [128, 4]); tb4 = sba("tb4", [128, 4]); b_rb4 = Buf(); b_tb4 = Buf()
            Hins = [R2f[0:16, 1024:2048].rearrange("p (g c) -> p g c", g=4), R1f[0:16, 1024:2048].rearrange("p (g c) -> p g c", g=4)]
            b_Hins = [b_R2[16], b_R1[16]]
            r3 = lambda t_: t_.rearrange("p (g b) -> p g b", b=16)
            V10 = r3(rb_bc); b_V10 = b_rb
            V20 = r3(R1f[:, 512:1024]); b_V20 = b_R1[8]
            Vend = r3(tmb); b_Vend = b_tmb
            Vt = r3(sg[:]); b_Vt = b_sg
            Houts = [vtmp[0:16, :].rearrange("p (g c) -> p g c", g=4), mA[0:16].rearrange("p a k -> p (a k)").rearrange("p (g c) -> p g c", g=4)]
            b_Houts = [b_vtmp, b_mA]

            wsl = R1f[:, 0:512].rearrange("p (h s) -> p h s", h=4)
            wblk = R1f[0:64, 512:768].rearrange("p (h s) -> p h s", h=4)
            dma("sp", wsl, w_s.rearrange("h t s -> t h s"), [], b_R1[0:8])
            P.op("dve", lambda e: e.memset(wblk, 0.0), [], b_R1[8:12])
            for b in range(16):
                dma("act", wblk[4 * b:4 * b + 4, :, 4 * b:4 * b + 4], w_s[:, 0:4, 0:4].rearrange("h t s -> t h s"), b_R1[8:12], b_R1[8:12],
                    allow_slow_non_contiguous=True)
            bk, bb = nb()
            for h in range(4):
                tr(bk[:, h * 128:(h + 1) * 128], wsl[:, h, :], identf[:], b_R1[0:8] + [b_identf], [bb])
            tt("dve", Msg[:], bk[:, 0:512].rearrange("p (h t) -> p h t", h=4), msgu[:].unsqueeze(1).broadcast_to([128, 4, 128]),
               ALU.mult, [bb, b_msgu], [b_Msg])
            bk, bb = nb()
            for h in range(4):
                tr(bk[0:64, h * 64:(h + 1) * 64], wblk[:, h, :], identf[0:64, 0:64], b_R1[8:12] + [b_identf], [bb])
            tt("dve", Msg_s[:], bk[0:64, 0:256].rearrange("p (h t) -> p h t", h=4),
               msgu[0:64, 0:64].unsqueeze(1).broadcast_to([64, 4, 64]), ALU.mult, [bb, b_msgu], [b_Msg])

            st_cur = [0]

            def mixer_tile(tok0, src, ntok, sample, h1row, par):
                NS = 1 if sample else 4
                PT = 64 if sample else 128
                NC = 16 if sample else 64
                mixTa_t = mixTa[par]; b_mixTa_t = b_mixTa[par]
                shared = {}

                def front(part):
                    b_n1T_s = b_n1T_list

                    def a2(s):
                        xs_ = xa[s % 2]; bxs_ = b_xa[s % 2]
                        if sample:
                            dma("sp", xs_[0:64, :], src, [], [bxs_])
                        else:
                            dma("sp", xs_[:, :], src[tok0 + 128 * s:tok0 + 128 * (s + 1), :], [], [bxs_])
                        act(junk[0:PT, :], xs_[0:PT, :], AF.Square, [bxs_], [b_junk, b_ss_s[s]], accum_out=ss4[0:PT, s:s + 1])
                        rstd(rs4[0:PT, s:s + 1], ss4[0:PT, s:s + 1], 1024.0, PT, 1, [b_ss_s[s]], [b_rs_s[s]], tm4[0:PT, s:s + 1], b_tm_s[s])

                    def a4(s):
                        k2 = s % 2
                        stt(xn[k2][0:PT, :], xa[k2][0:PT, :], rs4[0:PT, s:s + 1], gmix_bc[0:PT, :], ALU.mult, ALU.mult,
                            [b_xa[k2], b_rs_s[s], b_gmix], [b_xn[k2]])
                        bk, bb = nb()
                        bkb = bk[:].bitcast(BF16)
                        for dti in range(8):
                            tr(bkb[:, dti * PT:(dti + 1) * PT], xn[k2][0:PT, dti * 128:(dti + 1) * 128], identb[0:PT, 0:PT],
                               [b_xn[k2], b_identb], [bb])
                        cp("act", n1T[:, :, s * 128:s * 128 + PT], bkb[:, 0:8 * PT].rearrange("p (a t) -> p a t", a=8), [bb], [b_n1T_s[s]])

                    if part == 0:
                        a2(0)
                        for s in range(NS):
                            if s + 1 < NS:
                                a2(s + 1)
                            a4(s)
                        return

                    def stage1(s):
                        k2 = s % 2
                        tsl = slice(s * 128, s * 128 + PT)
                        bu, bbu = nb(); bv, bbv = nb(); bx, bbx = nb()
                        for (bkx, bbx_, c0) in ((bv, bbv, 512), (bu, bbu, 0), (bx, bbx, 1024)):
                            for dti in range(8):
                                mm(bkx[0:PT, :], n1T[:, dti, tsl], w_in_sb[:, dti, c0:c0 + 512], dti == 0, dti == 7,
                                   [b_n1T_s[s], b_win], [bbx_])
                        return (bv, bbv, bu, bbu, bx, bbx)

                    def stage1e(s, bu, bbu, bx, bbx):
                        k2 = s % 2
                        cp("act", u_tm[k2][0:PT, :], bu[0:PT, :], [bbu], [b_u[k2]])
                        cp("act", xs_tm[0:PT, s, :], bx[0:PT, :], [bbx], [b_xs_s[s]])
                        if sample:
                            dma("sp", zscr[2048:2112, :], xs_tm[0:64, 0, :], [b_xs_s[0]], [b_zscr])
                        else:
                            r0 = h1row + 128 * s
                            dma("sp", zscr[r0:r0 + 128, :], xs_tm[:, s, :], [b_xs_s[s]], [b_zscr])

                    def stage2(s, bv, bbv):
                        k2 = s % 2
                        tsl = slice(s * 128, s * 128 + PT)
                        act(vsq[0:PT, :], bv[0:PT, :], AF.Square, [bbv], [b_vsq])
                        P.op("dve", lambda e: e.reduce_sum(out=ssv[0:PT, :], in_=vsq[0:PT, :].rearrange("p (h d) -> p h d", h=4), axis=AX.X),
                             [b_vsq], [b_ssv])
                        rstd(rv[0:PT, :], ssv[0:PT, :], 128.0, PT, 4, [b_ssv], [b_rv], tmv[0:PT, :], b_tmv)
                        tt("dve", vtmp[0:PT, :].rearrange("p (h d) -> p h d", h=4), bv[0:PT, :].rearrange("p (h d) -> p h d", h=4),
                           rv[0:PT, :].unsqueeze(2).broadcast_to([PT, 4, 128]), ALU.mult, [bbv, b_rv], [b_vtmp])
                        if sample:
                            tt("dve", vnf[0:PT, :], vtmp[0:PT, :], gv_bc[0:PT, :], ALU.mult, [b_vtmp, b_gv], [b_vnf])
                            dma("sp", vs, vnf[0:PT, :], [b_vnf], [], is_output=True)
                            cp("dve", vn[k2][0:PT, :], vnf[0:PT, :], [b_vnf], [b_vn[k2]])
                        else:
                            tt("dve", vn[k2][0:PT, :], vtmp[0:PT, :], gv_bc[0:PT, :], ALU.mult, [b_vtmp, b_gv], [b_vn[k2]])
                        bs_, bbs = nb()
                        for h in range(4):
                            lhs = Msg_s[:, h, :] if sample else Msg[:, h, :]
                            mm(bs_[0:PT, h * 128:(h + 1) * 128], lhs, vn[k2][0:PT, h * 128:(h + 1) * 128], True, True,
                               [b_Msg, b_vn[k2]], [bbs])
                        bsb = bsT_s if sample else bsT
                        a_tm = a_tms[k2]; b_atm = b_atms[k2]
                        for h in range(4):
                            stt(a_tm[0:PT, h * 128:(h + 1) * 128], bs_[0:PT, h * 128:(h + 1) * 128], bsb[0:PT, h:h + 1],
                                u_tm[k2][0:PT, h * 128:(h + 1) * 128], ALU.add, ALU.mult, [bbs, b_col, b_u[k2]], [b_atm])

                    def stage2b(s):
                        k2 = s % 2
                        tsl = slice(s * 128, s * 128 + PT)
                        a_tm = a_tms[k2]; b_atm = b_atms[k2]; mixa = mixas[k2]; b_mixa = b_mixas[k2]
                        act(junk[0:PT, 0:512], a_tm[0:PT, :], AF.Square, [b_atm], [b_junk, b_ssa], accum_out=ssa[0:PT, :])
                        rstd(ra[0:PT, :], ssa[0:PT, :], 512.0, PT, 1, [b_ssa], [b_ra], tma[0:PT, :], b_tma)
                        stt(mixa[0:PT, :], a_tm[0:PT, :], ra[0:PT, 0:1], goa_bc[0:PT, :], ALU.mult, ALU.mult, [b_atm, b_ra, b_goa], [b_mixa])
                        bk, bb = nb()
                        bkb = bk[:].bitcast(BF16)
                        for m in range(4):
                            tr(bkb[:, m * PT:(m + 1) * PT], mixa[0:PT, m * 128:(m + 1) * 128], identb[0:PT, 0:PT], [b_mixa, b_identb], [bb])
                        cp("act", mixTa_t[:, :, tsl], bkb[:, 0:4 * PT].rearrange("p (a t) -> p a t", a=4), [bb], [b_mixTa_t])

                    pend = stage1(0)
                    stage1e(0, *pend[2:])
                    lag = None
                    for s in range(NS):
                        nxt = stage1(s + 1) if s + 1 < NS else None
                        stage2(s, pend[0], pend[1])
                        if lag is not None:
                            stage2b(lag)
                        if nxt is not None:
                            stage1e(s + 1, *nxt[2:])
                        lag = s
                        pend = nxt
                    stage2b(lag)

                def z_readback():
                    if sample:
                        zv = zscr[2048:2112, :].rearrange("(b t) c -> b t c", t=4)
                        dma("pool", Z[0:16, 0:4, :], zv, [b_zscr], [b_Zs[0]])
                        dma("pool", Z[0:16, 4:8, :], zv, [b_zscr], [b_Zs[0]])
                    else:
                        for s_ in range(4):
                            r0 = h1row + 128 * s_
                            dma("pool", Z[16 * s_:16 * s_ + 16, :, :], zscr[r0:r0 + 128, :].rearrange("(k i) c -> k i c", i=8), [b_zscr], [b_Zs[s_]])

                def ssm_front():
                    if sample:
                        bk1, bbk1 = nb(); bk2, bbk2 = nb()
                        for gq in range(8):
                            gs = slice(gq * 4, gq * 4 + 4)
                            Hin = Hins[gq % 2]; b_Hin = b_Hins[gq % 2]
                            dma("sp", Hin[:, :, 0:64], sre[:, gs, :], [], [b_Hin]); dma("sp", Hin[:, :, 64:128], sim[:, gs, :], [], [b_Hin])
                            dma("sp", Hin[:, :, 128:192], sim[:, gs, :], [], [b_Hin]); dma("sp", Hin[:, :, 192:256], sre[:, gs, :], [], [b_Hin])
                            for gl in range(4):
                                g = gq * 4 + gl
                                tr(bk1[:, g * 16:(g + 1) * 16], Hin[:, gl, 0:128], identf[0:16, 0:16], [b_Hin, b_identf], [bbk1])
                                tr(bk2[:, g * 16:(g + 1) * 16], Hin[:, gl, 128:256], identf[0:16, 0:16], [b_Hin, b_identf], [bbk2])
                        f2 = lambda v_: v_.rearrange("p g b -> p (g b)")
                        cp("act", f2(V10), bk1[:, 0:512], [bbk1], [b_V10])
                        cp("dve", f2(V20), bk2[:, 0:512], [bbk2], [b_V20])
                        cp("dve", HPb[:, :, 0:16], V10, [b_V10], [b_HPb])
                        shb = [128, 32, 16]
                        tt("dve", Vend, V10, AR4[:].unsqueeze(2).broadcast_to(shb), ALU.mult, [b_V10, b_A4], [b_Vend])
                        tt("dve", Vt, V20, AI4[:].unsqueeze(2).broadcast_to(shb), ALU.mult, [b_V20, b_A4], [b_Vt])
                        tt("dve", Vend, Vend, Vt, ALU.add, [b_Vend, b_Vt], [b_Vend])

                    for gh in range(2):
                        bk, bb = nb()
                        bkb = bk[:].bitcast(BF16)
                        for gl in range(16):
                            g = gh * 16 + gl
                            tr(bkb[:, gl * NC:(gl + 1) * NC], Z[:].rearrange("k i c -> k (i c)")[0:NC, g:4096:32], identb[0:NC, 0:NC],
                               b_Zs + [b_identb], [bb])
                        cp("act" if gh == 0 else "dve", X2[:, gh * 16:gh * 16 + 16, 0:NC],
                           bkb[:, 0:16 * NC].rearrange("p (g k) -> p g k", g=16), [bb], [b_X2])
                    if sample:
                        b1, bb1 = nb()
                        for g in range(32):
                            mm(b1[:, g * 16:(g + 1) * 16], MinX[64:128, g, 0:128], X2[64:128, g, 0:16], True, True, [b_MinX, b_X2], [bb1])
                    else:
                        GPB = 8
                        for gq in range(4):
                            gs = slice(gq * GPB, (gq + 1) * GPB)
                            b1, bb1 = nb(); b2, bb2 = nb()
                            for gl in range(GPB):
                                g = gq * GPB + gl
                                mm(b1[:, gl * 64:(gl + 1) * 64], MinX[:, g, 0:128], X2[:, g, 0:64], True, True, [b_MinX, b_X2], [bb1])
                                mm(b2[:, gl * 64:(gl + 1) * 64], MinX[:, g, 64:192], X2[:, g, 0:64], True, True, [b_MinX, b_X2], [bb2])
                            s1v = b1[:, 0:512].rearrange("p (g k) -> p g k", g=GPB)
                            s2v = b2[:, 0:512].rearrange("p (g k) -> p g k", g=GPB)
                            wr1 = b_R1[gq * GPB:(gq + 1) * GPB]; wr2 = b_R2[gq * GPB:(gq + 1) * GPB]
                            tt("dve", R1[:, gs, :], s1v, TC[:, gs, :], ALU.mult, [bb1, b_tab], wr1)
                            tt("dve", mA[:], s2v, TSs[:, gs, :], ALU.mult, [bb2, b_tab], [b_mA])
                            tt("pool", R1[:, gs, :], R1[:, gs, :], mA[:], ALU.add, wr1 + [b_mA], wr1)
                            tt("dve", R2[:, gs, :], s2v, TC[:, gs, :], ALU.mult, [bb2, b_tab], wr2)
                            tt("dve", mB[:], s1v, TSs[:, gs, :], ALU.mult, [bb1, b_tab], [b_mB])
                            tt("pool", R2[:, gs, :], R2[:, gs, :], mB[:], ALU.subtract, wr2 + [b_mB], wr2)
                    if sample:
                        shared["b1"] = (b1, bb1)

                def scan():
                    if sample:
                        f2 = lambda v_: v_.rearrange("p g b -> p (g b)")
                        b1, bb1 = shared["b1"]
                        tt("dve", f2(Vend), f2(Vend), b1[:, 0:512], ALU.add, [b_Vend, bb1], [b_Vend])
                        for gq in range(8):
                            bk, bb = nb()
                            for gl in range(4):
                                g = gq * 4 + gl
                                tr(bk[0:16, gl * 128:(gl + 1) * 128], Vend[:, g, :], identf[:], [b_Vend, b_identf], [bb])
                            Hout = Houts[gq % 2]; b_Hout = b_Houts[gq % 2]
                            cp("act", Hout, bk[0:16, 0:512].rearrange("p (g c) -> p g c", g=4), [bb], [b_Hout])
                            gs = slice(gq * 4, gq * 4 + 4)
                            dma("sp", sres[:, gs, :], Hout[:, :, 0:64], [b_Hout], [], is_output=True)
                            dma("sp", sims[:, gs, :], Hout[:, :, 64:128], [b_Hout], [], is_output=True)
                    else:
                        cp("act", HPb[:, :, 0], C1[:], [b_C], [b_HPb])
                        tt("dve", cA[:], MG8[:], C1[:], ALU.mult, [b_mg8, b_C], [b_cA])
                        tt("dve", R1[:, :, 0], R1[:, :, 0], cA[:], ALU.add, b_R1 + [b_cA], b_R1)
                        tt("dve", cB[:], MG8[:], C2[:], ALU.mult, [b_mg8, b_C], [b_cB])
                        tt("dve", R2[:, :, 0], R2[:, :, 0], cB[:], ALU.add, b_R2 + [b_cB], b_R2)
                        for g in range(32):
                            for (RR, bR) in ((R1, b_R1), (R2, b_R2)):
                                P.op("dve", lambda e, RR=RR, g=g: e.tensor_tensor_scan(
                                    out=RR[:, g, :], data0=MG8[:, g:g + 1].broadcast_to([128, 64]), data1=RR[:, g, :],
                                    initial=0.0, op0=ALU.mult, op1=ALU.add), [bR[g], b_mg8], [bR[g]])
                        tt("dve", cA[:], R1[:, :, 63], TC[:, :, 63], ALU.mult, b_R1 + [b_tab], [b_cA])
                        tt("dve", cB[:], R2[:, :, 63], TSs[:, :, 63], ALU.mult, b_R2 + [b_tab], [b_cB])
                        tt("dve", C1[:], cA[:], cB[:], ALU.subtract, [b_cA, b_cB], [b_C])
                        tt("dve", cA[:], R2[:, :, 63], TC[:, :, 63], ALU.mult, b_R2 + [b_tab], [b_cA])
                        tt("dve", cB[:], R1[:, :, 63], TSs[:, :, 63], ALU.mult, b_R1 + [b_tab], [b_cB])
                        tt("dve", C2[:], cA[:], cB[:], ALU.add, [b_cA, b_cB], [b_C])
                        for gq in range(4):
                            gs = slice(gq * 8, gq * 8 + 8)
                            wr1 = b_R1[gq * 8:gq * 8 + 8]; wr2 = b_R2[gq * 8:gq * 8 + 8]
                            tt("dve", mA[:], R1[:, gs, :], TC[:, gs, :], ALU.mult, wr1 + [b_tab], [b_mA])
                            tt("pool", mB[:], R2[:, gs, :], TSs[:, gs, :], ALU.mult, wr2 + [b_tab], [b_mB])
                            tt("dve", HPb[:, gs, 1:64], mA[:, :, 0:63], mB[:, :, 0:63], ALU.subtract, [b_mA, b_mB], [b_HPb])
                def back(hook=None):
                    Zy = Z
                    for gq in range(8):
                        bk, bb = nb()
                        for gl in range(4):
                            g = gq * 4 + gl
                            if sample:
                                mm(bk[0:NC, gl * 128:(gl + 1) * 128], X2[0:64, g, 0:NC], Toep[0:64, g, :], True, False, [b_X2, b_Toep], [bb])
                            else:
                                mm(bk[0:NC, gl * 128:(gl + 1) * 128], X2[:, g, 0:NC], Toep[:, g, :], True, False, [b_X2, b_Toep], [bb])
                            mm(bk[0:NC, gl * 128:(gl + 1) * 128], HPb[:, g, 0:NC], Mout[:, g, :], False, True, [b_HPb, b_Mout], [bb])
                        cp("dve" if gq % 2 == 0 else "act", Zy[0:NC, :, 64 * gq:64 * gq + 64].rearrange("p j (g c) -> p g j c", g=4),
                           bk[0:NC, 0:512].rearrange("p (g j c) -> p g j c", g=4, j=8), [bb], b_Zs)
                    NJ = 4 if sample else 8
                    for cbp in range(2):
                        bk, bb = nb()
                        bkb = bk[:].bitcast(BF16)
                        for cl in range(2):
                            cb = cbp * 2 + cl
                            for j in range(NJ):
                                o0 = (cl * NJ + j) * NC
                                tr(bkb[:, o0:o0 + NC], Zy[0:NC, j, cb * 128:(cb + 1) * 128], identb[0:NC, 0:NC], b_Zs + [b_identb], [bb])
                        for cl in range(2):
                            cb = cbp * 2 + cl
                            o0 = cl * NJ * NC
                            act(gTb[:, cb, 0:ntok].rearrange("p (k j) -> p j k", j=NJ),
                                bkb[:, o0:o0 + NJ * NC].rearrange("p (j k) -> p j k", j=NJ), AF.Gelu_apprx_tanh, [bb], [b_gTb])
                    if hook is not None:
                        hook()
                    for m in range(4):
                        bk, bb = nb()
                        for kt in range(4):
                            mm(bk[:, 0:ntok], w_glu_sb[:, kt, m * 128:(m + 1) * 128], gTb[:, kt, 0:ntok], kt == 0, kt == 3, [b_wglu, b_gTb], [bb])
                        act(sg[:, 0:ntok], bk[:, 0:ntok], AF.Sigmoid, [bb, b_col], [b_sg], bias=bglu[:, m:m + 1])
                        tt("dve", mixTb[:, m, 0:ntok], gTb[:, m, 0:ntok], sg[:, 0:ntok], ALU.mult, [b_gTb, b_sg], [b_mixTb])
                    act(sqb[:, :, 0:ntok], mixTb[:, :, 0:ntok], AF.Square, [b_mixTb], [b_sqb, b_R1[8]])
                    bkR, bbR = nb()
                    for s in range(NS):
                        tsl = slice(s * 128, s * 128 + PT)
                        for m in range(4):
                            mm(bkR[0:PT, s:s + 1], sqb[:, m, tsl], onesb[:, 0:1], m == 0, m == 3, [b_ones, b_sqb, b_R1[8]], [bbR])
                    rstd(rb4[0:PT, 0:NS], bkR[0:PT, 0:NS], 512.0, PT, NS, [bbR], [b_rb4], tb4[0:PT, 0:NS], b_tb4)
                    for s in range(NS):
                        tsl = slice(s * 128, s * 128 + PT)
                        hi_ = cnt_a["xh"] % 2; cnt_a["xh"] += 1
                        xh_ = xh[hi_]; bxh_ = b_xh[hi_]
                        if sample:
                            dma("sp", xh_[0:64, :], src, [], [bxh_])
                        else:
                            dma("sp", xh_[:, :], src[tok0 + 128 * s:tok0 + 128 * (s + 1), :], [], [bxh_])
                        for nh in range(2):
                            bka, bba = nb(); bkb_, bbb = nb()
                            for kt in range(4):
                                mm(bka[0:PT, :], mixTa_t[:, kt, tsl], w_out_sb[:, kt, nh * 512:(nh + 1) * 512], kt == 0, kt == 3, [b_mixTa_t, b_wout], [bba])
                            for kt in range(4):
                                mm(bkb_[0:PT, :], mixTb[:, kt, tsl], w_out_sb[:, 4 + kt, nh * 512:(nh + 1) * 512], kt == 0, kt == 3, [b_mixTb, b_wout], [bbb])
                            xsl = xh_[0:PT, nh * 512:(nh + 1) * 512]
                            tt("dve", xsl, xsl, bka[0:PT, :], ALU.add, [bxh_, bba], [bxh_])
                            stt(xsl, bkb_[0:PT, :], rb4[0:PT, s:s + 1], xsl, ALU.mult, ALU.add, [bxh_, bbb, b_rb4], [bxh_])
                        if sample:
                            dma("sp", h1d[h1row:h1row + 64, :], xh_[0:64, :], [bxh_], [])
                        else:
                            dma("sp", h1d[h1row + 128 * s:h1row + 128 * (s + 1), :], xh_[:, :], [bxh_], [])

                return front, ssm_front, scan, back, z_readback

            tiles = [mixer_tile(512 * t, xp, 512, False, 512 * t, t % 2) for t in range(4)]
            tiles.append(mixer_tile(0, xs, 64, True, 2048, 0))
            tiles[0][0](0)
            sa_late()
            tiles[0][0](1)
            tiles[0][4]()
            tiles[0][1]()
            for t in range(5):
                if t + 1 < 5:
                    tiles[t + 1][0](0)
                tiles[t][2]()
                if t + 1 < 5:
                    tiles[t + 1][0](1)
                tiles[t][3](tiles[t + 1][4] if t + 1 < 5 else None)
                if t == 3:
                    bk, bb = nb()
                    tr(bk[0:32, 0:128], C1[:], identf[:], [b_C, b_identf], [bb])
                    cp("act", stS[:], bk[0:32, 0:128], [bb], [b_stS])
                    dma("sp", srep, stS[:, 0:64], [b_stS], [], is_output=True)
                    dma("sp", simp, stS[:, 64:128], [b_stS], [], is_output=True)
                if t + 1 < 5:
                    tiles[t + 1][1]()
            P.barrier()
        SM.close()

        with contextlib.ExitStack() as SB:
            def sbb(name, shape, dt=F32):
                return sb(name, shape, dt, SB)

            wfo = sbb("wfo", [128, 22, 1024], BF16); b_wfo = Buf()
            wfov = w_ffn_out.rearrange("(a p) n -> p a n", p=128)
            wfo_pending = list(range(22))
            gffn_bc = sbb("gffn_bc", [128, 1024]); gfin_bc = sbb("gfin_bc", [128, 1024]); b_gbc = Buf()
            dma("sp", gffn_bc[:], g_ffn.partition_broadcast(128), [], [b_gbc])
            dma("sp", gfin_bc[:], g_final.partition_broadcast(128), [], [b_gbc])
            cwl = sbb("cwl", [88, 2, 128]); b_cwl = Buf()
            cwv = conv_w.rearrange("k (c p) -> (k c) p", p=128)
            dma("sp", cwl[:, 0, :], cwv[0:88, :], [], [b_cwl])
            dma("sp", cwl[0:44, 1, :], cwv[88:132, :], [], [b_cwl])
            dma("sp", cwl[44:88, 1, :], conv_b.rearrange("(c p) -> c p", p=128), [], [b_cwl])
            cw = sbb("cw", [128, 4, 44]); b_cw = Buf()
            bk, bb = nb()
            tr(bk[:, 0:88], cwl[:, 0, :], identf[0:88, 0:88], [b_cwl, b_identf], [bb])
            tr(bk[:, 88:176], cwl[:, 1, :], identf[0:88, 0:88], [b_cwl, b_identf], [bb])
            cp("dve", cw[:].rearrange("p k c -> p (k c)"), bk[:, 0:176], [bb], [b_cw])
            hist = sbb("hist", [128, 2, 44]); b_hist0 = Buf(); b_hist = [Buf() for _ in range(44)]
            P.op("pool", lambda e: e.memset(hist[:], 0.0), [], [b_hist0] + b_hist)
            hist_s = sbb("hist_s", [128, 44, 32]); b_hists = Buf()
            scl = sbb("scl", [32, 1408]); b_scl = Buf()
            for q4 in range(4):
                dma("sp", scl[:], scv[:, q4 * 1408:(q4 + 1) * 1408], [], [b_scl])
                bk, bb = nb()
                for cl in range(11):
                    tr(bk[:, cl * 32:(cl + 1) * 32], scl[:, cl * 128:(cl + 1) * 128], identf[0:32, 0:32], [b_scl, b_identf], [bb])
                cp("act", hist_s[:, q4 * 11:(q4 + 1) * 11, :], bk[:, 0:352].rearrange("p (c r) -> p c r", c=11), [bb], [b_hists])
            cst_s = sbb("cst_s", [128, 44, 32]); b_csts = Buf()

            NTB = 1088
            hb = [sbb("hb%d" % i, [128, 1024]) for i in range(3)]; b_hb = [Buf(), Buf(), Buf()]
            xn = [sbb("xnB%d" % i, [128, 1024], BF16) for i in range(2)]; b_xn = [Buf(), Buf()]
            junk = sbb("junkB", [128, 1024], BF16); b_junk = Buf()
            ssB = [sbb("ssB%d" % i, [128, 1]) for i in range(2)]; rsB = [sbb("rsB%d" % i, [128, 1]) for i in range(2)]
            tmB = [sbb("tmB%d" % i, [128, 1]) for i in range(2)]
            b_ssB = [Buf(), Buf()]; b_rsB = [Buf(), Buf()]; b_tmB = [Buf(), Buf()]
            n2T = sbb("n2T", [128, 8, NTB], BF16); b_n2T = Buf()
            zt = sbb("zt", [128, 22, NTB], BF16); b_z = Buf()
            wst = [sbb("wst%d" % i, [128, 8, 256], BF16) for i in range(3)]; b_wst = [Buf(), Buf(), Buf()]
            upS = [[sbb("upS%d_%d" % (i, j), [128, 514]) for j in range(2)] for i in range(2)]
            b_upS = [[Buf(), Buf()], [Buf(), Buf()]]
            b_upH = [[Buf(), Buf()], [Buf(), Buf()]]
            hsave = [sbb("hsv%d" % i, [128, 2]) for i in range(3)]; b_hsave = [Buf() for _ in range(3)]
            acc = [[sbb("acc%d_%d" % (i, j), [128, 512]) for j in range(2)] for i in range(3)]
            b_acc = [[Buf(), Buf()] for _ in range(3)]
            wfiv = w_ffn_in.rearrange("(a p) n -> p a n", p=128)
            cnt = {"hb": 0, "w": 0, "u": 0, "yo": 0}

            def ffn_tile(subtiles, cgs):
                def b1a(si):
                    hrow, PT, oap, col0 = subtiles[si]
                    i3 = cnt["hb"] % 3; cnt["hb"] += 1
                    k2 = si % 2
                    dma("sp", hb[i3][0:PT, :], h1d[hrow:hrow + PT, :], [], [b_hb[i3]])
                    act(junk[0:PT, :], hb[i3][0:PT, :], AF.Square, [b_hb[i3]], [b_junk, b_ssB[k2]], accum_out=ssB[k2][0:PT, :])
                    rstd(rsB[k2][0:PT, :], ssB[k2][0:PT, :], 1024.0, PT, 1, [b_ssB[k2]], [b_rsB[k2]], tmB[k2][0:PT, :], b_tmB[k2])
                    stt(xn[k2][0:PT, :], hb[i3][0:PT, :], rsB[k2][0:PT, 0:1], gffn_bc[0:PT, :], ALU.mult, ALU.mult,
                        [b_hb[i3], b_rsB[k2], b_gbc], [b_xn[k2]])

                def b1b(si):
                    hrow, PT, oap, col0 = subtiles[si]
                    k2 = si % 2
                    bk, bb = nb()
                    bkb = bk[:].bitcast(BF16)
                    for dti in range(8):
                        tr(bkb[:, dti * PT:(dti + 1) * PT], xn[k2][0:PT, dti * 128:(dti + 1) * 128], identb[0:PT, 0:PT],
                           [b_xn[k2], b_identb], [bb])
                    cp("act", n2T[:, :, col0:col0 + PT], bkb[:, 0:8 * PT].rearrange("p (a t) -> p a t", a=8), [bb], [b_n2T])

                b1a(0)
                for si in range(len(subtiles)):
                    if si + 1 < len(subtiles):
                        b1a(si + 1)
                    b1b(si)
                def wload(cc):
                    wj = (cnt["w"] + cc) % 3
                    dma("pool", wst[wj][:, :, 0:128], wfiv[:, :, cc * 128:(cc + 1) * 128], [], [b_wst[wj]])
                    dma("pool", wst[wj][:, :, 128:256], wfiv[:, :, DFF + cc * 128:DFF + (cc + 1) * 128], [], [b_wst[wj]])
                wload(0); wload(1)
                pend_fin = [None]
                for c in range(22):
                    wi = (cnt["w"] + c) % 3
                    if c + 2 < 22:
                        wload(c + 2)
                    if wfo_pending:
                        a_ = wfo_pending.pop(0)
                        dma("pool", wfo[:, a_, :], wfov[:, a_, :], [], [b_wfo])
                    for (c0, ncol, smp) in cgs:
                        ui = cnt["u"] % 3; uu = cnt["u"] % 2; cnt["u"] += 1
                        bks = [nb(), nb()]
                        for gv in range(2):
                            for dti in range(8):
                                mm(bks[gv][0][:, 0:ncol], wst[wi][:, dti, gv * 128:(gv + 1) * 128], n2T[:, dti, c0:c0 + ncol],
                                   dti == 0, dti == 7, [b_wst[wi], b_n2T], [bks[gv][1]])
                        for gv in range(2):
                            ci = gv * 22 + c
                            bk, bb = bks[gv]
                            U = upS[uu][gv]; bU = b_upS[uu][gv]; A = acc[ui][gv]; bA = b_acc[ui][gv]
                            w0 = cw[:, 0, ci:ci + 1]; w1 = cw[:, 1, ci:ci + 1]; w2 = cw[:, 2, ci:ci + 1]; bs = cw[:, 3, ci:ci + 1]
                            if F_CONVOLD:
                                b_h1 = b_hist[0]
                                if not smp:
                                    cp("pool", U[:, 0:2], hist[:, :, ci], [b_h1], [bU])
                                    cp("act", U[:, 2:2 + ncol], bk[:, 0:ncol], [bb], [bU])
                                    cp("pool", hist[:, :, ci], U[:, ncol:ncol + 2], [bU], [b_h1])
                                    act(A[:, 0:ncol], bk[:, 0:ncol], AF.Identity, [bb, b_cw], [bA], scale=w2, bias=bs)
                                    stt(A[:, 0:ncol], U[:, 1:1 + ncol], w1, A[:, 0:ncol], ALU.mult, ALU.add, [bU, b_cw, bA], [bA])
                                    stt(A[:, 0:ncol], U[:, 0:ncol], w0, A[:, 0:ncol], ALU.mult, ALU.add, [bU, b_cw, bA], [bA])
                                else:
                                    U3 = U[:, 0:96].rearrange("p (b t) -> p b t", t=6)
                                    A3 = A[:, 0:64].rearrange("p (b t) -> p b t", t=4)
                                    cp("pool", U3[:, :, 0:2], hist_s[:, ci, :].rearrange("p (b k) -> p b k", k=2), [b_hists], [bU])
                                    cp("act", U3[:, :, 2:6], bk[:, 0:64].rearrange("p (b t) -> p b t", t=4), [bb], [bU])
                                    cp("pool", cst_s[:, ci, :].rearrange("p (b k) -> p b k", k=2), U3[:, :, 4:6], [bU], [b_csts])
                                    act(A3, bk[:, 0:64].rearrange("p (b t) -> p b t", t=4), AF.Identity, [bb, b_cw], [bA], scale=w2, bias=bs)
                                    stt(A3, U3[:, :, 1:5], w1, A3, ALU.mult, ALU.add, [bU, b_cw, bA], [bA])
                                    stt(A3, U3[:, :, 0:4], w0, A3, ALU.mult, ALU.add, [bU, b_cw, bA], [bA])
                                continue
                            bH = b_upH[uu][gv]
                            if (not smp) and gv == 1:
                                hsv = hsave[ui]; bhs = b_hsave[ui]
                                cp("act", hsv[:, 0:2], hist[:, :, ci], [b_hist[ci]], [bhs])
                                cp("act", hist[:, :, ci], bk[:, ncol - 2:ncol], [bb], [b_hist[ci]])
                                act(A[:, 0:ncol], bk[:, 0:ncol], AF.Identity, [bb, b_cw], [bA], scale=w2, bias=bs)
                                stt(A[:, 1:ncol], bk[:, 0:ncol - 1], w1, A[:, 1:ncol], ALU.mult, ALU.add, [bb, b_cw, bA], [bA])
                                stt(A[:, 2:ncol], bk[:, 0:ncol - 2], w0, A[:, 2:ncol], ALU.mult, ALU.add, [bb, b_cw, bA], [bA])
                                stt(A[:, 0:1], hsv[:, 1:2], w1, A[:, 0:1], ALU.mult, ALU.add, [bhs, b_cw, bA], [bA])
                                stt(A[:, 0:2], hsv[:, 0:2], w0, A[:, 0:2], ALU.mult, ALU.add, [bhs, b_cw, bA], [bA])
                            elif not smp:
                                cp("act", U[:, 0:2], hist[:, :, ci], [b_hist[ci]], [bH])
                                cp("act", U[:, 2:2 + ncol], bk[:, 0:ncol], [bb], [bU])
                                cp("act", hist[:, :, ci], bk[:, ncol - 2:ncol], [bb], [b_hist[ci]])
                                act(A[:, 0:ncol], bk[:, 0:ncol], AF.Identity, [bb, b_cw], [bA], scale=w2, bias=bs)
                                stt(A[:, 0:ncol], U[:, 1:1 + ncol], w1, A[:, 0:ncol], ALU.mult, ALU.add, [bU, bH, b_cw, bA], [bA])
                                stt(A[:, 0:ncol], U[:, 0:ncol], w0, A[:, 0:ncol], ALU.mult, ALU.add, [bU, bH, b_cw, bA], [bA])
                            else:
                                U3 = U[:, 0:96].rearrange("p (b t) -> p b t", t=6)
                                A3 = A[:, 0:64].rearrange("p (b t) -> p b t", t=4)
                                bk3 = bk[:, 0:64].rearrange("p (b t) -> p b t", t=4)
                                cp("act", U3[:, :, 0:2], hist_s[:, ci, :].rearrange("p (b k) -> p b k", k=2), [b_hists], [bH])
                                cp("act", U3[:, :, 2:6], bk3, [bb], [bU])
                                cp("act", cst_s[:, ci, :].rearrange("p (b k) -> p b k", k=2), bk3[:, :, 2:4], [bb], [b_csts])
                                act(A3, bk3, AF.Identity, [bb, b_cw], [bA], scale=w2, bias=bs)
                                stt(A3, U3[:, :, 1:5], w1, A3, ALU.mult, ALU.add, [bU, bH, b_cw, bA], [bA])
                                stt(A3, U3[:, :, 0:4], w0, A3, ALU.mult, ALU.add, [bU, bH, b_cw, bA], [bA])
                        def fin(ui=ui, c=c, c0=c0, ncol=ncol):
                            act(acc[ui][0][:, 0:ncol], acc[ui][0][:, 0:ncol], AF.Gelu_apprx_tanh, [b_acc[ui][0]], [b_acc[ui][0]])
                            tt("pool", zt[:, c, c0:c0 + ncol], acc[ui][0][:, 0:ncol], acc[ui][1][:, 0:ncol], ALU.mult,
                               [b_acc[ui][0], b_acc[ui][1]], [b_z])
                        if pend_fin[0] is not None:
                            pend_fin[0]()
                        pend_fin[0] = fin
                if pend_fin[0] is not None:
                    pend_fin[0]()
                    pend_fin[0] = None
                cnt["w"] += 22
                for si, (hrow, PT, oap, col0) in enumerate(subtiles):
                    i3 = cnt["hb"] % 3; cnt["hb"] += 1
                    k2 = si % 2
                    dma("sp", hb[i3][0:PT, :], h1d[hrow:hrow + PT, :], [], [b_hb[i3]])
                    bks = [nb(), nb()]
                    for nh in range(2):
                        for c in range(22):
                            mm(bks[nh][0][0:PT, :], zt[:, c, col0:col0 + PT], wfo[:, c, nh * 512:(nh + 1) * 512], c == 0, c == 21,
                               [b_z, b_wfo], [bks[nh][1]])
                    for nh in range(2):
                        tt("dve", hb[i3][0:PT, nh * 512:(nh + 1) * 512], hb[i3][0:PT, nh * 512:(nh + 1) * 512], bks[nh][0][0:PT, :], ALU.add,
                           [b_hb[i3], bks[nh][1]], [b_hb[i3]])
                    act(junk[0:PT, :], hb[i3][0:PT, :], AF.Square, [b_hb[i3]], [b_junk, b_ssB[k2]], accum_out=ssB[k2][0:PT, :])
                    rstd(rsB[k2][0:PT, :], ssB[k2][0:PT, :], 1024.0, PT, 1, [b_ssB[k2]], [b_rsB[k2]], tmB[k2][0:PT, :], b_tmB[k2])
                    stt(hb[i3][0:PT, :], hb[i3][0:PT, :], rsB[k2][0:PT, 0:1], gfin_bc[0:PT, :], ALU.mult, ALU.mult,
                        [b_hb[i3], b_rsB[k2], b_gbc], [b_hb[i3]])
                    dma("sp", oap, hb[i3][0:PT, :], [b_hb[i3]], [], is_output=True)

            subA = [(128 * s, 128, yp[128 * s:128 * (s + 1), :], 128 * s) for s in range(8)]
            ffn_tile(subA, [(0, 512, False), (512, 512, False)])
            subB = [(1024 + 128 * s, 128, yp[1024 + 128 * s:1024 + 128 * (s + 1), :], 128 * s) for s in range(8)]
            subB.append((2048, 64, ys, 1024))
            ffn_tile(subB, [(0, 512, False), (512, 512, False), (1024, 64, True)])
            cpo = sbb("cpo", [88, 128]); b_cpo = Buf()
            bk, bb = nb()
            tr(bk[0:88, 0:128], hist[:].rearrange("p k c -> p (k c)"), identf[:], b_hist + [b_identf], [bb])
            cp("act", cpo[:], bk[0:88, 0:128], [bb], [b_cpo])
            dma("sp", cvp[0].rearrange("(c p) -> c p", p=128), cpo[0:44, :], [b_cpo], [], is_output=True)
            dma("sp", cvp[1].rearrange("(c p) -> c p", p=128), cpo[44:88, :], [b_cpo], [], is_output=True)
            cso = sbb("cso", [128, 11, 128]); b_cso = Buf()
            for q3 in range(3):
                bk, bb = nb()
                nblk = 4 if q3 < 2 else 3
                for bl in range(nblk):
                    cbk = q3 * 4 + bl
                    tr(bk[:, bl * 128:(bl + 1) * 128], cst_s[:, cbk * 4:(cbk + 1) * 4, :].rearrange("p c r -> p (c r)"), identf[:],
                       [b_csts, b_identf], [bb])
                cp("act", cso[:, q3 * 4:q3 * 4 + nblk, :], bk[:, 0:nblk * 128].rearrange("p (a c) -> p a c", a=nblk), [bb], [b_cso])
            cvsv = cvs.rearrange("r (cb cl p) -> cl r cb p", cl=4, p=128)
            for cl in range(4):
                dma("sp", cvsv[cl], cso[32 * cl:32 * cl + 32, :, :], [b_cso], [], is_output=True)
        P.build()
    return nc


_CACHE = {}


def _consts():
    ident = np.eye(128, dtype=np.float32)
    ii = np.arange(128) // 16
    mtoep = (ii[None, :] >= ii[:, None]).astype(np.float32)
    ar = np.arange(128)
    msgu = (ar[:, None] <= ar[None, :]).astype(np.float32)
    nv = np.zeros(32, np.float32)
    nv[0:8] = -np.arange(8)
    nv[8:16] = 7 - np.arange(8)
    nv[16:25] = np.arange(9)
    nv[25] = 8; nv[26] = 4; nv[27] = 1
    nvec = np.tile(nv[None, :], (128, 1)).astype(np.float32)
    sigma = np.ones((128, 1), np.float32)
    sigma[64:] = -1.0
    kvec = np.tile(np.arange(1, 65, dtype=np.float32)[None, :], (128, 1))
    return dict(c_ident=ident, c_mtoep=mtoep, c_msgu=msgu, c_nvec=nvec, c_sigma=sigma, c_kvec=kvec)


def kernel(x_prompt, x_sample, state_ssm_re, state_ssm_im, state_conv,
           g_mix, w_in, g_v, w_s, b_s, lam_re, lam_im, log_dt, b_re, b_im, c_re, c_im,
           d_skip, w_glu, b_glu, g_out_a, g_out_b, w_out, g_ffn, w_ffn_in, conv_w, conv_b,
           w_ffn_out, g_final):
    f = lambda a: np.ascontiguousarray(np.asarray(a, dtype=np.float32))
    if "nc" not in _CACHE:
        _CACHE["nc"] = build_nc()
    nc = _CACHE["nc"]
    shared = dict(
        g_mix=f(g_mix), w_in=f(w_in), g_v=f(g_v).reshape(512), w_s=f(w_s), b_s=f(b_s),
        lam_re=f(lam_re), lam_im=f(lam_im), log_dt=f(log_dt), b_re=f(b_re), b_im=f(b_im),
        c_re=f(c_re).reshape(512, 64), c_im=f(c_im).reshape(512, 64), d_skip=f(d_skip),
        w_glu=f(w_glu), b_glu=f(b_glu), g_out_a=f(g_out_a), g_out_b=f(g_out_b), w_out=f(w_out),
        g_ffn=f(g_ffn), w_ffn_in=f(w_ffn_in), conv_w=f(conv_w), conv_b=f(conv_b),
        w_ffn_out=f(w_ffn_out), g_final=f(g_final))
    shared.update(_consts())
    xpf = f(x_prompt); xsf = f(x_sample); sr = f(state_ssm_re); si = f(state_ssm_im); sc = f(state_conv)
    in_maps = []
    for c in range(NCORES):
        m = dict(shared)
        m["xp"] = xpf[c]
        m["xs"] = xsf[16 * c:16 * c + 16].reshape(64, D)
        m["sre"] = sr[16 * c:16 * c + 16]
        m["sim"] = si[16 * c:16 * c + 16]
        m["scv"] = sc[16 * c:16 * c + 16].reshape(32, 2 * DFF)
        in_maps.append(m)
    res = run_bass_kernel_spmd(nc, in_maps, core_ids=list(range(NCORES)))
    R = res.results
    y_prompt = np.stack([R[c]["yp"] for c in range(NCORES)], 0)
    y_sample = np.concatenate([R[c]["ys"].reshape(16, 4, D) for c in range(NCORES)], 0)
    v_sample = np.concatenate([R[c]["vs"].reshape(16, 4, 4, 128) for c in range(NCORES)], 0)
    srp = np.stack([R[c]["srep"] for c in range(NCORES)], 0)
    sip = np.stack([R[c]["simp"] for c in range(NCORES)], 0)
    cvp = np.stack([R[c]["cvp"] for c in range(NCORES)], 0)
    srs = np.concatenate([R[c]["sres"] for c in range(NCORES)], 0)
    sis = np.concatenate([R[c]["sims"] for c in range(NCORES)], 0)
    cvs = np.concatenate([R[c]["cvs"].reshape(16, 2, 2 * DFF) for c in range(NCORES)], 0)
    out = (y_prompt, y_sample, v_sample, srp, sip, cvp, srs, sis, cvs)
    return tuple(np.ascontiguousarray(o, dtype=np.float32) for o in out)
```

```python
import contextlib
import math
import numpy as np
import concourse.bass as bass
import concourse.mybir as mybir
from concourse.bass_utils import run_bass_kernel_spmd

F32 = mybir.dt.float32
BF16 = mybir.dt.bfloat16
I32 = mybir.dt.int32
AF = mybir.ActivationFunctionType
ALU = mybir.AluOpType
AX = mybir.AxisListType

NCORES = 8
import os as _os
F_POW = bool(int(_os.environ.get("F_POW", "0")))
F_SCANPOOL = bool(int(_os.environ.get("F_SCANPOOL", "0")))
F_CONVOLD = bool(int(_os.environ.get("F_CONVOLD", "0")))
F_TSACT = bool(int(_os.environ.get("F_TSACT", "0")))
D = 1024
DFF = 2816
EPS = 1e-6
TWO_PI = 2.0 * math.pi


class Buf:
    __slots__ = ("name", "last_w", "readers")

    def __init__(self, name=""):
        self.name = name
        self.last_w = None
        self.readers = []


class Op:
    __slots__ = ("eng", "kind", "emit", "deps", "marked", "seq", "dsem", "dval", "idx")

    def __init__(self, eng, kind, emit):
        self.eng = eng
        self.kind = kind
        self.emit = emit
        self.deps = []
        self.marked = False
        self.seq = 0
        self.dsem = None
        self.dval = 0


ENGS = ("pe", "act", "dve", "pool", "sp")


class Prog:
    def __init__(self, nc, n_dma_sems=12):
        self.nc = nc
        self.ops = {e: [] for e in ENGS}
        self.all_ops = []
        self.n_dma_sems = n_dma_sems
        self.dma_count = {e: 0 for e in ENGS}
        self.out_dmas = []
        self.since_barrier = []
        self.last_on_sem = {}

    def _add(self, op, reads, writes):
        deps = []
        for b in reads:
            if b.last_w is not None:
                deps.append(b.last_w)
        for b in writes:
            if b.last_w is not None:
                deps.append(b.last_w)
            deps.extend(b.readers)
        seen = set()
        for d in deps:
            if d is op or id(d) in seen:
                continue
            seen.add(id(d))
            if op.eng == "pe" and d.eng == "pe" and d.kind == "c" and op.kind == "c":
                continue
            op.deps.append(d)
            d.marked = True
        for b in reads:
            if op.kind == "c":
                b.readers = [r for r in b.readers if not (r.kind == "c" and r.eng == op.eng)]
            b.readers.append(op)
        for b in writes:
            b.last_w = op
            b.readers = []
        op.idx = len(self.all_ops)
        self.all_ops.append(op)
        self.ops[op.eng].append(op)
        self.since_barrier.append(op)
        return op

    def op(self, eng, emit, reads=(), writes=()):
        return self._add(Op(eng, "c", emit), reads, writes)

    def dma(self, queue, emit, reads=(), writes=(), is_output=False):
        o = Op(queue, "d", emit)
        k = self.dma_count[queue]
        self.dma_count[queue] = k + 1
        o.dsem = k % self.n_dma_sems
        o.dval = 16 * (k // self.n_dma_sems + 1)
        prev = self.last_on_sem.get((queue, o.dsem))
        if prev is not None:
            o.deps.append(prev)
        self.last_on_sem[(queue, o.dsem)] = o
        self._add(o, reads, writes)
        if is_output:
            self.out_dmas.append(o)
        return o

    def barrier(self, exclude=()):
        lasts = []
        for e in ENGS:
            lastc = None
            for o in self.ops[e]:
                if o.kind == "c":
                    lastc = o
            if lastc is not None:
                lasts.append(lastc)
        dmas = [o for o in self.since_barrier if o.kind == "d" and id(o) not in exclude]
        self.since_barrier = []
        for e in ("pe", "act", "dve", "pool", "sp"):
            o = Op(e, "c", lambda eng: eng.nop())
            for d in lasts + dmas:
                o.deps.append(d)
                d.marked = True
            o.idx = len(self.all_ops)
            self.all_ops.append(o)
            self.ops[e].append(o)

    def build(self):
        nc = self.nc
        with contextlib.ExitStack() as st:
            esem = {}
            for e in ("pe", "act", "dve", "pool", "sp"):
                esem[e] = st.enter_context(nc.semaphore("s_" + e))
            dsem = {}
            for q in ENGS:
                if self.dma_count[q] > 0:
                    dsem[q] = [
                        st.enter_context(nc.semaphore("d_%s_%d" % (q, i)))
                        for i in range(min(self.n_dma_sems, self.dma_count[q]))
                    ]
            for e in ENGS:
                c = 0
                for o in self.ops[e]:
                    if o.kind == "c" and o.marked:
                        c += 1
                        o.seq = c
            import os
            if os.environ.get("KDEBUG"):
                for e in ENGS:
                    print("ENG", e, "ops", len(self.ops[e]), "marked", max([o.seq for o in self.ops[e]] + [0]), "dmas", self.dma_count[e])
            block = st.enter_context(nc.Block())

            def run(ename, eng):
                waited = {}
                for o in self.ops[ename]:
                    for d in o.deps:
                        if d.kind == "c":
                            key = ("c", d.eng)
                            val = d.seq
                            sem = esem[d.eng]
                        else:
                            key = ("d", d.eng, d.dsem)
                            val = d.dval
                            sem = dsem[d.eng][d.dsem]
                        if waited.get(key, 0) >= val:
                            continue
                        waited[key] = val
                        eng.wait_ge(sem, val)
                    ins = o.emit(eng)
                    if o.kind == "c":
                        if o.marked:
                            ins.then_inc(esem[ename], 1)
                    else:
                        ins.then_inc(dsem[ename][o.dsem], 16)
                fin = {}
                for o in self.out_dmas:
                    if o.eng == ename:
                        fin[o.dsem] = max(fin.get(o.dsem, 0), o.dval)
                for s, v in fin.items():
                    eng.wait_ge(dsem[ename][s], v)

            @block.sync
            def _(e):
                run("sp", e)

            @block.tensor
            def _(e):
                run("pe", e)

            @block.scalar
            def _(e):
                run("act", e)

            @block.vector
            def _(e):
                run("dve", e)

            @block.gpsimd
            def _(e):
                run("pool", e)


def build_nc():
    nc = bass.Bass("TRN2", target_bir_lowering=False)
    P = Prog(nc)

    def din(name, shape):
        return nc.dram_tensor(name, list(shape), F32, kind="ExternalInput").ap()

    def dout(name, shape):
        return nc.dram_tensor(name, list(shape), F32, kind="ExternalOutput").ap()

    xp = din("xp", [2048, D]); xs = din("xs", [64, D])
    sre = din("sre", [16, 32, 64]); sim = din("sim", [16, 32, 64]); scv = din("scv", [32, 2 * DFF])
    g_mix = din("g_mix", [D]); w_in = din("w_in", [D, 1536]); g_v = din("g_v", [512])
    w_s = din("w_s", [4, 128, 128]); b_s = din("b_s", [4, 128])
    lam_re = din("lam_re", [32, 64]); lam_im = din("lam_im", [32, 64]); log_dt = din("log_dt", [32])
    b_re = din("b_re", [32, 64, 16]); b_im = din("b_im", [32, 64, 16])
    c_re = din("c_re", [512, 64]); c_im = din("c_im", [512, 64]); d_skip = din("d_skip", [512])
    w_glu = din("w_glu", [512, 512]); b_glu = din("b_glu", [512])
    g_out_a = din("g_out_a", [512]); g_out_b = din("g_out_b", [512])
    w_out = din("w_out", [D, D]); g_ffn = din("g_ffn", [D]); w_ffn_in = din("w_ffn_in", [D, 2 * DFF])
    conv_w = din("conv_w", [3, 2 * DFF]); conv_b = din("conv_b", [2 * DFF])
    w_ffn_out = din("w_ffn_out", [DFF, D]); g_final = din("g_final", [D])
    c_ident = din("c_ident", [128, 128]); c_mtoep = din("c_mtoep", [128, 128]); c_msgu = din("c_msgu", [128, 128])
    c_nvec = din("c_nvec", [128, 32]); c_sigma = din("c_sigma", [128, 1]); c_kvec = din("c_kvec", [128, 64])

    yp = dout("yp", [2048, D]); ys = dout("ys", [64, D]); vs = dout("vs", [64, 512])
    srep = dout("srep", [32, 64]); simp = dout("simp", [32, 64]); cvp = dout("cvp", [2, 2 * DFF])
    sres = dout("sres", [16, 32, 64]); sims = dout("sims", [16, 32, 64]); cvs = dout("cvs", [32, 2 * DFF])
    h1d = nc.dram_tensor("h1d", [2112, D], F32).ap()
    zscr = nc.dram_tensor("zscr", [2112, 512], BF16).ap()
    b_zscr = Buf()

    ES = contextlib.ExitStack()
    with ES:
        def sb(name, shape, dt=F32, stack=ES):
            return stack.enter_context(nc.sbuf_tensor(name, list(shape), dt))

        banks = [ES.enter_context(nc.psum_tensor("bank%d" % i, [128, 512], F32)) for i in range(8)]
        bbufs = [Buf("bank%d" % i) for i in range(8)]
        bctr = [0]

        def nb():
            i = bctr[0] % 8
            bctr[0] += 1
            return banks[i], bbufs[i]

        def mm(out, lhsT, rhs, start, stop, reads, writes):
            P.op("pe", lambda e: e.matmul(out, lhsT, rhs, start=start, stop=stop), reads, writes)

        def tr(out, in_, ident, reads, writes):
            P.op("pe", lambda e: e.transpose(out, in_, ident), reads, writes)

        def act(out, in_, func, reads, writes, **kw):
            P.op("act", lambda e: e.activation(out=out, in_=in_, func=func, **kw), reads, writes)

        def tt(eng, out, in0, in1, op, reads, writes):
            P.op(eng, lambda e: e.tensor_tensor(out=out, in0=in0, in1=in1, op=op), reads, writes)

        def ts(eng, out, in0, s1, s2, op0, op1, reads, writes):
            if s2 is None:
                P.op(eng, lambda e: e.tensor_scalar(out=out, in0=in0, scalar1=s1, scalar2=None, op0=op0), reads, writes)
            else:
                P.op(eng, lambda e: e.tensor_scalar(out=out, in0=in0, scalar1=s1, scalar2=s2, op0=op0, op1=op1), reads, writes)

        def stt(out, in0, scalar, in1, op0, op1, reads, writes):
            P.op("dve", lambda e: e.scalar_tensor_tensor(out=out, in0=in0, scalar=scalar, in1=in1, op0=op0, op1=op1), reads, writes)

        def cp(eng, out, in_, reads, writes):
            if eng == "act":
                P.op(eng, lambda e: e.activation(out=out, in_=in_, func=AF.Copy), reads, writes)
            else:
                P.op(eng, lambda e: e.tensor_copy(out=out, in_=in_), reads, writes)

        def dma(q, out, in_, reads, writes, is_output=False, **kw):
            P.dma(q, lambda e: e.dma_start(out=out, in_=in_, **kw), reads, writes, is_output=is_output)

        identf = sb("identf", [128, 128]); b_identf = Buf()
        identb = sb("identb", [128, 128], BF16); b_identb = Buf()
        onesb = sb("onesb", [128, 128], BF16); b_ones = Buf()
        epst = sb("epst", [128, 1]); b_eps = Buf()
        mhalf = sb("mhalf", [128, 512]); b_mhalf = Buf()
        sigma = sb("sigma", [128, 1]); b_sigma = Buf()
        ARR8 = sb("ARR8", [128, 2, 32]); AII8 = sb("AII8", [128, 2, 32]); b_A8 = Buf()
        AR4 = sb("AR4", [128, 32]); AI4 = sb("AI4", [128, 32]); b_A4 = Buf()
        ST = [sb("ST%d" % i, [128, 3, 32]) for i in range(2)]
        b_ST = [Buf(), Buf()]
        SM = contextlib.ExitStack()
        SM.__enter__()
        Toep = sb("Toep", [128, 32, 128], BF16, SM); b_Toep = Buf()
        MinX = sb("MinX", [128, 32, 192], BF16, SM); b_MinX = Buf()
        Mout = sb("Mout", [128, 32, 128], BF16, SM); b_Mout = Buf()
        w_in_sb = sb("w_in_sb", [128, 8, 1536], BF16, SM); b_win = Buf()
        w_glu_sb = sb("w_glu_sb", [128, 4, 512], BF16, SM); b_wglu = Buf()
        w_out_sb = sb("w_out_sb", [128, 8, 1024], BF16, SM); b_wout = Buf()
        TC = sb("TC", [128, 32, 64], F32, SM); TSs = sb("TSs", [128, 32, 64], F32, SM); b_tab = Buf()
        MG8 = sb("MG8", [128, 32], F32, SM); b_mg8 = Buf()
        C1 = sb("C1", [128, 32], F32, SM); C2 = sb("C2", [128, 32], F32, SM); b_C = Buf()
        th8c = sb("th8c", [128, 32], F32, SM); b_th8c = Buf()
        wiv = w_in.rearrange("(a p) n -> p a n", p=128)
        n_pref0 = len(P.all_ops)
        for a in range(8):
            dma("pool", w_in_sb[:, a, 0:1024], wiv[:, a, 0:1024], [], [b_win])
        dma("pool", w_glu_sb[:], w_glu.rearrange("(a p) n -> p a n", p=128), [], [b_wglu])
        wov = w_out.rearrange("(a p) n -> p a n", p=128)
        P.op("pool", lambda e: e.memset(C1[:], 0.0), [], [b_C])
        P.op("pool", lambda e: e.memset(C2[:], 0.0), [], [b_C])

        dma("sp", identf[:], c_ident, [], [b_identf])
        dma("sp", sigma[:], c_sigma, [], [b_sigma])
        cp("dve", identb[:], identf[:], [b_identf], [b_identb])
        P.op("pool", lambda e: e.memset(onesb[:], 1.0), [], [b_ones])
        P.op("pool", lambda e: e.memset(epst[:], EPS), [], [b_eps])
        P.op("pool", lambda e: e.memset(mhalf[:], -0.5), [], [b_mhalf])
        P.op("pool", lambda e: e.memset(ST[0][:], 0.0), [], [b_ST[0]])

        def rstd(out, ssum, n, pt, width, reads, writes, tmp, b_tmp):
            if F_POW:
                ts("dve", tmp, ssum, 1.0 / n, EPS, ALU.mult, ALU.add, reads, [b_tmp])
                tt("pool", out, tmp, mhalf[0:pt, 0:width], ALU.pow, [b_tmp, b_mhalf], writes)
                return
            act(tmp, ssum, AF.Sqrt, list(reads) + [b_eps], [b_tmp], scale=1.0 / n, bias=epst[0:pt, :])
            P.op("dve", lambda e: e.reciprocal(out, tmp), [b_tmp], writes)

        with contextlib.ExitStack() as S0:
            def sb0(name, shape, dt=F32):
                return sb(name, shape, dt, S0)

            wxs = sb0("wxs", [128, 8, 512], BF16); b_wxs = Buf()
            dma("pool", wxs[:], wiv[:, :, 1024:1536], [], [b_wxs])
            for a in range(8):
                dma("pool", w_out_sb[:, a, :], wov[:, a, :], [], [b_wout])
            n_pref1 = len(P.all_ops)
            cp("act", w_in_sb[:, :, 1024:1536].rearrange("p a (c g) -> p a c g", g=32),
               wxs[:].rearrange("p a (g c) -> p a c g", c=16), [b_wxs], [b_win])
            mtoep = sb0("mtoep", [128, 128]); b_mtoep = Buf()
            nvec = sb0("nvec", [128, 32]); b_nvec = Buf()
            dma("sp", mtoep[:], c_mtoep, [], [b_mtoep])
            dma("sp", nvec[:], c_nvec, [], [b_nvec])
            L2 = sb0("L2", [32, 256]); b_L2 = Buf()
            dma("sp", L2[:, 0:64], lam_re, [], [b_L2]); dma("sp", L2[:, 64:128], lam_re, [], [b_L2])
            dma("sp", L2[:, 128:192], lam_im, [], [b_L2]); dma("sp", L2[:, 192:256], lam_im, [], [b_L2])
            lr = sb0("lr", [128, 32]); li = sb0("li", [128, 32]); b_l = Buf()
            bk, bb = nb()
            tr(bk[:, 0:32], L2[:, 0:128], identf[0:32, 0:32], [b_L2, b_identf], [bb])
            tr(bk[:, 32:64], L2[:, 128:256], identf[0:32, 0:32], [b_L2, b_identf], [bb])
            cp("dve", lr[:], bk[:, 0:32], [bb], [b_l]); cp("dve", li[:], bk[:, 32:64], [bb], [b_l])
            dtb = sb0("dtb", [128, 32]); b_dt = Buf()
            dma("sp", dtb[:], log_dt.partition_broadcast(128), [], [b_dt])
            act(dtb[:], dtb[:], AF.Exp, [b_dt], [b_dt])
            rho = sb0("rho", [128, 32]); th = sb0("th", [128, 32]); b_rt = Buf()
            tt("dve", rho[:], lr[:], dtb[:], ALU.mult, [b_l, b_dt], [b_rt])
            tt("dve", th[:], li[:], dtb[:], ALU.mult, [b_l, b_dt], [b_rt])
            NQ = 32
            shp = [128, 32, NQ]
            ARG = sb0("ARG", shp); RHO = sb0("RHO", shp); b_arg = Buf()
            nv_b = nvec[:].unsqueeze(1).broadcast_to(shp)
            tt("dve", ARG[:], th[:].unsqueeze(2).broadcast_to(shp), nv_b, ALU.mult, [b_rt, b_nvec], [b_arg])
            tt("dve", RHO[:], rho[:].unsqueeze(2).broadcast_to(shp), nv_b, ALU.mult, [b_rt, b_nvec], [b_arg])
            mag = RHO; b_mag = Buf()
            act(mag[:], RHO[:], AF.Exp, [b_arg], [b_mag, b_arg])
            ki = sb0("ki", shp, I32); kf = sb0("kf", shp); rr = sb0("rr", shp); mk = sb0("mk", shp); b_red = Buf()
            sinv = sb0("sinv", shp); cosv = sb0("cosv", shp); b_sc = Buf()

            def sin_of(outt, shift):
                ts("dve", rr[:], ARG[:], shift, None, ALU.add, None, [b_arg], [b_red])
                ts("dve", kf[:], rr[:], 1.0 / TWO_PI, None, ALU.mult, None, [b_red], [b_red])
                cp("dve", ki[:], kf[:], [b_red], [b_red])
                cp("dve", kf[:], ki[:], [b_red], [b_red])
                stt(rr[:], kf[:], -TWO_PI, rr[:], ALU.mult, ALU.add, [b_red], [b_red])
                ts("dve", mk[:], rr[:], math.pi, None, ALU.is_gt, None, [b_red], [b_red])
                stt(rr[:], mk[:], -TWO_PI, rr[:], ALU.mult, ALU.add, [b_red], [b_red])
                ts("dve", mk[:], rr[:], -math.pi, None, ALU.is_lt, None, [b_red], [b_red])
                stt(rr[:], mk[:], TWO_PI, rr[:], ALU.mult, ALU.add, [b_red], [b_red])
                ts("dve", rr[:], rr[:], math.pi, -math.pi, ALU.min, ALU.max, [b_red], [b_red])
                act(outt[:], rr[:], AF.Sin, [b_red], [b_sc])

            sin_of(sinv, 0.0)
            sin_of(cosv, math.pi / 2)
            PA = cosv; PB = sinv; b_P = b_sc
            tt("dve", PA[:], mag[:], cosv[:], ALU.mult, [b_mag, b_sc], [b_P])
            tt("dve", PB[:], mag[:], sinv[:], ALU.mult, [b_mag, b_sc], [b_P])
            cp("dve", MG8[:], mag[:, :, 25], [b_mag], [b_mg8])
            den = sb0("den", [128, 32]); t0 = sb0("t0", [128, 32]); t1 = sb0("t1", [128, 32])
            fr = sb0("fr", [128, 32]); fi = sb0("fi", [128, 32]); am1 = sb0("am1", [128, 32]); b_f = Buf()
            tt("dve", den[:], lr[:], lr[:], ALU.mult, [b_l], [b_f])
            tt("dve", t0[:], li[:], li[:], ALU.mult, [b_l], [b_f])
            tt("dve", den[:], den[:], t0[:], ALU.add, [b_f], [b_f])
            P.op("dve", lambda e: e.reciprocal(den[:], den[:]), [b_f], [b_f])
            ts("dve", am1[:], PA[:, :, 27], -1.0, None, ALU.add, None, [b_P], [b_f])
            tt("dve", t0[:], am1[:], lr[:], ALU.mult, [b_f, b_l], [b_f])
            tt("dve", t1[:], PB[:, :, 27], li[:], ALU.mult, [b_P, b_l], [b_f])
            tt("dve", t0[:], t0[:], t1[:], ALU.add, [b_f], [b_f])
            tt("dve", fr[:], t0[:], den[:], ALU.mult, [b_f], [b_f])
            tt("dve", t0[:], PB[:, :, 27], lr[:], ALU.mult, [b_P, b_l], [b_f])
            tt("dve", t1[:], am1[:], li[:], ALU.mult, [b_f, b_l], [b_f])
            tt("dve", t0[:], t0[:], t1[:], ALU.subtract, [b_f], [b_f])
            tt("dve", fi[:], t0[:], den[:], ALU.mult, [b_f], [b_f])
            shp16 = [128, 32, 16]
            FRn = sb0("FRn", shp16); FIs = sb0("FIs", shp16); tq = sb0("tq", shp16); b_FR = Buf()
            frb = fr[:].unsqueeze(2).broadcast_to(shp16); fib = fi[:].unsqueeze(2).broadcast_to(shp16)
            tt("dve", FRn[:], PA[:, :, 0:16], frb, ALU.mult, [b_P, b_f], [b_FR])
            tt("dve", tq[:], PB[:, :, 0:16], fib, ALU.mult, [b_P, b_f], [b_FR])
            tt("dve", FRn[:], FRn[:], tq[:], ALU.subtract, [b_FR], [b_FR])
            tt("dve", FIs[:], PA[:, :, 0:16], fib, ALU.mult, [b_P, b_f], [b_FR])
            tt("dve", tq[:], PB[:, :, 0:16], frb, ALU.mult, [b_P, b_f], [b_FR])
            tt("dve", FIs[:], FIs[:], tq[:], ALU.add, [b_FR], [b_FR])
            ts("dve", FIs[:], FIs[:], sigma[:, 0:1], None, ALU.mult, None, [b_FR, b_sigma], [b_FR])
            BT1 = sb0("BT1", [128, 32, 16]); BT2 = sb0("BT2", [128, 32, 16]); b_BT = Buf()
            brv = b_re.rearrange("g p c -> p g c"); biv = b_im.rearrange("g p c -> p g c")
            dma("sp", BT1[0:64], brv, [], [b_BT]); dma("sp", BT1[64:128], biv, [], [b_BT])
            dma("sp", BT2[0:64], biv, [], [b_BT]); dma("sp", BT2[64:128], brv, [], [b_BT])
            shp4 = [128, 32, 8, 16]
            BBneg = sb0("BBneg", shp4); MinT = BBneg; b_BB = Buf()
            bt1b = BT1[:].unsqueeze(2).broadcast_to(shp4); bt2b = BT2[:].unsqueeze(2).broadcast_to(shp4)

            def build_bb(dst, q0, tbuf, b_tbuf):
                tt("dve", dst[:], FRn[:, :, q0:q0 + 8].unsqueeze(3).broadcast_to(shp4), bt1b, ALU.mult, [b_FR, b_BT], [b_BB])
                tt("dve", tbuf, FIs[:, :, q0:q0 + 8].unsqueeze(3).broadcast_to(shp4), bt2b, ALU.mult, [b_FR, b_BT], [b_tbuf])
                tt("dve", dst[:], dst[:], tbuf, ALU.subtract, [b_BB, b_tbuf], [b_BB])
            CT1 = sb0("CT1", [128, 32, 16]); CT2 = sb0("CT2", [128, 32, 16]); b_CT = Buf()
            CL = sb0("CL", [128, 4, 256]); b_CL = Buf()
            crv = c_re.rearrange("(r p) k -> p r k", p=128); civ = c_im.rearrange("(r p) k -> p r k", p=128)
            dma("sp", CL[:, :, 0:64], crv, [], [b_CL]); dma("sp", CL[:, :, 64:128], civ, [], [b_CL])
            dma("sp", CL[:, :, 128:192], civ, [], [b_CL]); dma("sp", CL[:, :, 192:256], crv, [], [b_CL])
            for half, dst in ((0, CT1), (1, CT2)):
                bk, bb = nb()
                for r in range(4):
                    tr(bk[:, r * 128:(r + 1) * 128], CL[:, r, half * 128:(half + 1) * 128], identf[:], [b_CL, b_identf], [bb])
                cp("dve", dst[:].rearrange("p g c -> p (g c)"), bk[:, 0:512], [bb], [b_CT])
            shp9 = [128, 32, 9, 16]
            EE = sb0("EE", shp9); te = sb0("te", shp9); PAs = sb0("PAs", [128, 32, 9]); b_EE = Buf(); b_te = Buf()
            ts("dve", PAs[:], PA[:, :, 16:25], sigma[:, 0:1], None, ALU.mult, None, [b_P, b_sigma], [b_EE])
            tt("dve", EE[:], PAs[:].unsqueeze(3).broadcast_to(shp9), CT1[:].unsqueeze(2).broadcast_to(shp9), ALU.mult, [b_EE, b_CT], [b_EE])
            tt("dve", te[:], PB[:, :, 16:25].unsqueeze(3).broadcast_to(shp9), CT2[:].unsqueeze(2).broadcast_to(shp9), ALU.mult, [b_P, b_CT], [b_te])
            tt("dve", EE[:], EE[:], te[:], ALU.subtract, [b_EE, b_te], [b_EE])
            build_bb(BBneg, 0, te[:, :, 0:8, :], b_te)
            cp("dve", Mout[:].rearrange("p g (j c) -> p g j c", c=16), EE[:, :, 1:9, :], [b_EE], [b_Mout])
            Drep = sb0("Drep", [128, 32]); b_Drep = Buf()
            dsv = d_skip.rearrange("(g c) -> c g", c=16)
            for i in range(8):
                dma("sp", Drep[16 * i:16 * i + 16, :], dsv, [], [b_Drep], allow_slow_non_contiguous=True)
            tmpT = te[:, 0:4, 0:8, :].rearrange("p g n c -> p g (n c)"); b_tmpT = b_te
            for gq in range(8):
                bk, bb = nb()
                for gl in range(4):
                    g = gq * 4 + gl
                    mm(bk[:, gl * 128:(gl + 1) * 128], BBneg[:, g].rearrange("p i c -> p (i c)"),
                       EE[:, g, 0:8, :].rearrange("p n c -> p (n c)"), True, True, [b_BB, b_EE], [bb])
                tt("dve", tmpT, bk[:, 0:512].rearrange("p (g c) -> p g c", g=4),
                   mtoep[:].unsqueeze(1).broadcast_to([128, 4, 128]), ALU.mult, [bb, b_mtoep], [b_tmpT])
                for gl in range(4):
                    g = gq * 4 + gl
                    stt(Toep[:, g, :], identf[:], Drep[:, g:g + 1], tmpT[:, gl, :], ALU.mult, ALU.add,
                        [b_identf, b_Drep, b_tmpT], [b_Toep])
            build_bb(MinT, 8, te[:, :, 0:8, :], b_te)
            for gq in range(8):
                bk, bb = nb()
                for gl in range(4):
                    g = gq * 4 + gl
                    tr(bk[:, gl * 128:(gl + 1) * 128], MinT[:, g].rearrange("p i c -> p (i c)"), identf[:], [b_BB, b_identf], [bb])
                bv = bk[:, 0:512].rearrange("p (g c) -> p g c", g=4)
                cp("dve", MinX[:, gq * 4:gq * 4 + 4, 0:128], bv, [bb], [b_MinX])
                cp("act", MinX[:, gq * 4:gq * 4 + 4, 128:192], bv[:, :, 0:64], [bb], [b_MinX])
            cp("dve", ARR8[:, 0, :], PA[:, :, 25], [b_P], [b_A8]); cp("dve", ARR8[:, 1, :], PA[:, :, 25], [b_P], [b_A8])
            ts("dve", AII8[:, 1, :], PB[:, :, 25], sigma[:, 0:1], None, ALU.mult, None, [b_P, b_sigma], [b_A8])
            ts("dve", AII8[:, 0, :], AII8[:, 1, :], -1.0, None, ALU.mult, None, [b_A8], [b_A8])
            cp("dve", AR4[:], PA[:, :, 26], [b_P], [b_A4])
            ts("dve", AI4[:], PB[:, :, 26], sigma[:, 0:1], -1.0, ALU.mult, ALU.mult, [b_P, b_sigma], [b_A4])
            ts("dve", th8c[:], th[:], 8.0, None, ALU.mult, None, [b_rt], [b_th8c])
            P.barrier(exclude=set(id(o) for o in P.all_ops[n_pref0:n_pref1]))

        with contextlib.ExitStack() as S1:
            shpk = [128, 32, 64]
            kvec = sb("kvec", [128, 64], F32, S1); b_kvec = Buf()
            dma("sp", kvec[:], c_kvec, [], [b_kvec])
            th8r = sb("th8r", [128, 32], F32, S1); b_th8r = Buf()
            kA = sb("kA", shpk, F32, S1); kR = sb("kR", shpk, F32, S1); kF = sb("kF", shpk, F32, S1)
            kI = sb("kI", shpk, I32, S1); kM = sb("kM", shpk, F32, S1); b_k = Buf()

            def reduce_pi(rr_, src, shift, kf_, ki_, mk_, rd, wr):
                ts("dve", rr_, src, shift, None, ALU.add, None, rd, wr)
                ts("dve", kf_, rr_, 1.0 / TWO_PI, None, ALU.mult, None, wr, wr)
                cp("dve", ki_, kf_, wr, wr)
                cp("dve", kf_, ki_, wr, wr)
                stt(rr_, kf_, -TWO_PI, rr_, ALU.mult, ALU.add, wr, wr)
                ts("dve", mk_, rr_, math.pi, None, ALU.is_gt, None, wr, wr)
                stt(rr_, mk_, -TWO_PI, rr_, ALU.mult, ALU.add, wr, wr)
                ts("dve", mk_, rr_, -math.pi, None, ALU.is_lt, None, wr, wr)
                stt(rr_, mk_, TWO_PI, rr_, ALU.mult, ALU.add, wr, wr)
                ts("dve", rr_, rr_, math.pi, -math.pi, ALU.min, ALU.max, wr, wr)

            reduce_pi(th8r[:], th8c[:], 0.0, kF[:, :, 0], kI[:, :, 0], kM[:, :, 0], [b_th8c], [b_th8r, b_k])
            tt("dve", kA[:], th8r[:].unsqueeze(2).broadcast_to(shpk), kvec[:].unsqueeze(1).broadcast_to(shpk), ALU.mult,
               [b_th8r, b_kvec], [b_k])
            reduce_pi(kR[:], kA[:], 0.0, kF[:], kI[:], kM[:], [b_k], [b_k])
            act(TSs[:], kR[:], AF.Sin, [b_k], [b_tab])
            ts("dve", TSs[:], TSs[:], sigma[:, 0:1], None, ALU.mult, None, [b_tab, b_sigma], [b_tab])
            reduce_pi(kR[:], kA[:], math.pi / 2, kF[:], kI[:], kM[:], [b_k], [b_k])
            act(TC[:], kR[:], AF.Sin, [b_k], [b_tab])
            P.barrier(exclude=set(id(o) for o in P.all_ops[n_pref0:n_pref1]))

        with contextlib.ExitStack() as SA:
            def sba(name, shape, dt=F32):
                return sb(name, shape, dt, SA)

            gmix_bc = sba("gmix_bc", [128, 1024]); b_gmix = Buf()
            dma("sp", gmix_bc[:], g_mix.partition_broadcast(128), [], [b_gmix])
            gv_bc = sba("gv_bc", [128, 512]); b_gv = Buf()
            dma("sp", gv_bc[:], g_v.partition_broadcast(128), [], [b_gv])
            goa_bc = sba("goa_bc", [128, 512]); b_goa = Buf()
            dma("sp", goa_bc[:], g_out_a.partition_broadcast(128), [], [b_goa])
            gob = sba("gob", [128, 4]); bglu = sba("bglu", [128, 4]); b_col = Buf()
            bsT = sba("bsT", [128, 4]); bsT_s = sba("bsT_s", [64, 4])
            def sa_late():
                dma("sp", gob[:], g_out_b.rearrange("(m p) -> p m", p=128), [], [b_col], allow_slow_non_contiguous=True)
                dma("sp", bglu[:], b_glu.rearrange("(m p) -> p m", p=128), [], [b_col], allow_slow_non_contiguous=True)
                dma("sp", bsT[:], b_s.rearrange("h t -> t h"), [], [b_col], allow_slow_non_contiguous=True)
                for b in range(16):
                    dma("sp", bsT_s[4 * b:4 * b + 4, :], b_s[:, 0:4].rearrange("h t -> t h"), [], [b_col], allow_slow_non_contiguous=True)
                for m_ in range(4):
                    ts("dve", w_out_sb[:, 4 + m_, :], w_out_sb[:, 4 + m_, :], gob[:, m_:m_ + 1], None, ALU.mult, None, [b_wout, b_col], [b_wout])

            msgu = sba("msgu", [128, 128]); b_msgu = Buf()
            dma("sp", msgu[:], c_msgu, [], [b_msgu])
            Msg = sba("Msg", [128, 4, 128], BF16); Msg_s = sba("Msg_s", [64, 4, 64], BF16); b_Msg = Buf()
            xa = [sba("xa%d" % i, [128, 1024]) for i in range(2)]; b_xa = [Buf(), Buf()]
            xh = [sba("xh%d" % i, [128, 1024]) for i in range(2)]; b_xh = [Buf(), Buf()]
            b_ss_s = [Buf() for _ in range(4)]; b_rs_s = [Buf() for _ in range(4)]; b_tm_s = [Buf() for _ in range(4)]
            b_xs_s = [Buf() for _ in range(4)]
            cnt_a = {"xh": 0}
            ss4 = sba("ss4", [128, 4]); rs4 = sba("rs4", [128, 4]); tm4 = sba("tm4", [128, 4]); b_ss = Buf(); b_rs = Buf(); b_tm4 = Buf()
            xn = [sba("xn%d" % i, [128, 1024], BF16) for i in range(2)]; b_xn = [Buf(), Buf()]
            n1T = sba("n1T", [128, 8, 512], BF16); b_n1T = Buf(); b_n1T_list = [Buf() for _ in range(4)]
            u_tm = [sba("u_tm%d" % i, [128, 512], BF16) for i in range(2)]; b_u = [Buf(), Buf()]
            vsq = sba("vsq", [128, 512], BF16); b_vsq = Buf()
            ssv = sba("ssv", [128, 4]); rv = sba("rv", [128, 4]); tmv = sba("tmv", [128, 4]); b_ssv = Buf(); b_rv = Buf(); b_tmv = Buf()
            vtmp = sba("vtmp", [128, 512]); b_vtmp = Buf()
            vnf_alias = True
            vn = [sba("vn%d" % i, [128, 512], BF16) for i in range(2)]; b_vn = [Buf(), Buf()]
            xs_tm = sba("xs_tm", [128, 4, 512], BF16); b_xstm = Buf()
            a_tms = [sba("a_tm%d" % i, [128, 512]) for i in range(2)]; b_atms = [Buf(), Buf()]
            vnf = a_tms[1]; b_vnf = b_atms[1]
            ssa = sba("ssa", [128, 1]); ra = sba("ra", [128, 1]); tma = sba("tma", [128, 1]); b_ssa = Buf(); b_ra = Buf(); b_tma = Buf()
            mixas = [sba("mixa%d" % i, [128, 512], BF16) for i in range(2)]; b_mixas = [Buf(), Buf()]
            mixTa = [sba("mixTa%d" % i, [128, 4, 512], BF16) for i in range(2)]; b_mixTa = [Buf(), Buf()]
            mixTb = sba("mixTb", [128, 4, 512], BF16); b_mixTb = Buf()
            Z = sba("Z", [64, 8, 512], BF16); b_Zs = [Buf() for _ in range(4)]
            X2 = sba("X2", [128, 32, 64], BF16); b_X2 = Buf()
            R1 = sba("R1", [128, 32, 64]); R2 = sba("R2", [128, 32, 64])
            b_R1 = [Buf() for _ in range(32)]; b_R2 = [Buf() for _ in range(32)]
            mA = sba("mA", [128, 8, 64]); mB = sba("mB", [128, 8, 64]); b_mA = Buf(); b_mB = Buf()
            cA = sba("cA", [128, 32]); cB = sba("cB", [128, 32]); b_cA = Buf(); b_cB = Buf()
            HPb = sba("HPb", [128, 32, 64], BF16); b_HPb = Buf()
            gTb = sba("gTb", [128, 4, 512], BF16); b_gTb = Buf()
            junk = gTb[:].rearrange("p a t -> p (a t)")[:, 0:1024]; b_junk = b_gTb
            sg = sba("sg", [128, 512]); b_sg = Buf()
            R1f = R1[:].rearrange("p g k -> p (g k)"); R2f = R2[:].rearrange("p g k -> p (g k)")
            allR1 = b_R1; allR2 = b_R2
            sqb = R1f.bitcast(BF16)[:, 0:2048].rearrange("p (a t) -> p a t", a=4); b_sqb = b_R1[0]
            rb_bc = R2f[:, 0:512]; tmb = R2f[:, 512:1024]; b_rb = b_R2[0]; b_tmb = b_R2[8]
            stS = sba("stS", [32, 128]); b_stS = Buf()
            rb4 = sba("rb4", [128, 4]); tb4 = sba("tb4", [128, 4]); b_rb4 = Buf(); b_tb4 = Buf()
            Hins = [R2f[0:16, 1024:2048].rearrange("p (g c) -> p g c", g=4), R1f[0:16, 1024:2048].rearrange("p (g c) -> p g c", g=4)]
            b_Hins = [b_R2[16], b_R1[16]]
            r3 = lambda t_: t_.rearrange("p (g b) -> p g b", b=16)
            V10 = r3(rb_bc); b_V10 = b_rb
            V20 = r3(R1f[:, 512:1024]); b_V20 = b_R1[8]
            Vend = r3(tmb); b_Vend = b_tmb
            Vt = r3(sg[:]); b_Vt = b_sg
            Houts = [vtmp[0:16, :].rearrange("p (g c) -> p g c", g=4), mA[0:16].rearrange("p a k -> p (a k)").rearrange("p (g c) -> p g c", g=4)]
            b_Houts = [b_vtmp, b_mA]

            wsl = R1f[:, 0:512].rearrange("p (h s) -> p h s", h=4)
            wblk = R1f[0:64, 512:768].rearrange("p (h s) -> p h s", h=4)
            dma("sp", wsl, w_s.rearrange("h t s -> t h s"), [], b_R1[0:8])
            P.op("dve", lambda e: e.memset(wblk, 0.0), [], b_R1[8:12])
            for b in range(16):
                dma("act", wblk[4 * b:4 * b + 4, :, 4 * b:4 * b + 4], w_s[:, 0:4, 0:4].rearrange("h t s -> t h s"), b_R1[8:12], b_R1[8:12],
                    allow_slow_non_contiguous=True)
            bk, bb = nb()
            for h in range(4):
                tr(bk[:, h * 128:(h + 1) * 128], wsl[:, h, :], identf[:], b_R1[0:8] + [b_identf], [bb])
            tt("dve", Msg[:], bk[:, 0:512].rearrange("p (h t) -> p h t", h=4), msgu[:].unsqueeze(1).broadcast_to([128, 4, 128]),
               ALU.mult, [bb, b_msgu], [b_Msg])
            bk, bb = nb()
            for h in range(4):
                tr(bk[0:64, h * 64:(h + 1) * 64], wblk[:, h, :], identf[0:64, 0:64], b_R1[8:12] + [b_identf], [bb])
            tt("dve", Msg_s[:], bk[0:64, 0:256].rearrange("p (h t) -> p h t", h=4),
               msgu[0:64, 0:64].unsqueeze(1).broadcast_to([64, 4, 64]), ALU.mult, [bb, b_msgu], [b_Msg])

            st_cur = [0]

            def mixer_tile(tok0, src, ntok, sample, h1row, par):
                NS = 1 if sample else 4
                PT = 64 if sample else 128
                NC = 16 if sample else 64
                mixTa_t = mixTa[par]; b_mixTa_t = b_mixTa[par]
                shared = {}

                def front(part):
                    b_n1T_s = b_n1T_list

                    def a2(s):
                        xs_ = xa[s % 2]; bxs_ = b_xa[s % 2]
                        if sample:
                            dma("sp", xs_[0:64, :], src, [], [bxs_])
                        else:
                            dma("sp", xs_[:, :], src[tok0 + 128 * s:tok0 + 128 * (s + 1), :], [], [bxs_])
                        act(junk[0:PT, :], xs_[0:PT, :], AF.Square, [bxs_], [b_junk, b_ss_s[s]], accum_out=ss4[0:PT, s:s + 1])
                        rstd(rs4[0:PT, s:s + 1], ss4[0:PT, s:s + 1], 1024.0, PT, 1, [b_ss_s[s]], [b_rs_s[s]], tm4[0:PT, s:s + 1], b_tm_s[s])

                    def a4(s):
                        k2 = s % 2
                        stt(xn[k2][0:PT, :], xa[k2][0:PT, :], rs4[0:PT, s:s + 1], gmix_bc[0:PT, :], ALU.mult, ALU.mult,
                            [b_xa[k2], b_rs_s[s], b_gmix], [b_xn[k2]])
                        bk, bb = nb()
                        bkb = bk[:].bitcast(BF16)
                        for dti in range(8):
                            tr(bkb[:, dti * PT:(dti + 1) * PT], xn[k2][0:PT, dti * 128:(dti + 1) * 128], identb[0:PT, 0:PT],
                               [b_xn[k2], b_identb], [bb])
                        cp("act", n1T[:, :, s * 128:s * 128 + PT], bkb[:, 0:8 * PT].rearrange("p (a t) -> p a t", a=8), [bb], [b_n1T_s[s]])

                    if part == 0:
                        a2(0)
                        for s in range(NS):
                            if s + 1 < NS:
                                a2(s + 1)
                            a4(s)
                        return

                    def stage1(s):
                        k2 = s % 2
                        tsl = slice(s * 128, s * 128 + PT)
                        bu, bbu = nb(); bv, bbv = nb(); bx, bbx = nb()
                        for (bkx, bbx_, c0) in ((bv, bbv, 512), (bu, bbu, 0), (bx, bbx, 1024)):
                            for dti in range(8):
                                mm(bkx[0:PT, :], n1T[:, dti, tsl], w_in_sb[:, dti, c0:c0 + 512], dti == 0, dti == 7,
                                   [b_n1T_s[s], b_win], [bbx_])
                        return (bv, bbv, bu, bbu, bx, bbx)

                    def stage1e(s, bu, bbu, bx, bbx):
                        k2 = s % 2
                        cp("act", u_tm[k2][0:PT, :], bu[0:PT, :], [bbu], [b_u[k2]])
                        cp("act", xs_tm[0:PT, s, :], bx[0:PT, :], [bbx], [b_xs_s[s]])
                        if sample:
                            dma("sp", zscr[2048:2112, :], xs_tm[0:64, 0, :], [b_xs_s[0]], [b_zscr])
                        else:
                            r0 = h1row + 128 * s
                            dma("sp", zscr[r0:r0 + 128, :], xs_tm[:, s, :], [b_xs_s[s]], [b_zscr])

                    def stage2(s, bv, bbv):
                        k2 = s % 2
                        tsl = slice(s * 128, s * 128 + PT)
                        act(vsq[0:PT, :], bv[0:PT, :], AF.Square, [bbv], [b_vsq])
                        P.op("dve", lambda e: e.reduce_sum(out=ssv[0:PT, :], in_=vsq[0:PT, :].rearrange("p (h d) -> p h d", h=4), axis=AX.X),
                             [b_vsq], [b_ssv])
                        rstd(rv[0:PT, :], ssv[0:PT, :], 128.0, PT, 4, [b_ssv], [b_rv], tmv[0:PT, :], b_tmv)
                        tt("dve", vtmp[0:PT, :].rearrange("p (h d) -> p h d", h=4), bv[0:PT, :].rearrange("p (h d) -> p h d", h=4),
                           rv[0:PT, :].unsqueeze(2).broadcast_to([PT, 4, 128]), ALU.mult, [bbv, b_rv], [b_vtmp])
                        if sample:
                            tt("dve", vnf[0:PT, :], vtmp[0:PT, :], gv_bc[0:PT, :], ALU.mult, [b_vtmp, b_gv], [b_vnf])
                            dma("sp", vs, vnf[0:PT, :], [b_vnf], [], is_output=True)
                            cp("dve", vn[k2][0:PT, :], vnf[0:PT, :], [b_vnf], [b_vn[k2]])
                        else:
                            tt("dve", vn[k2][0:PT, :], vtmp[0:PT, :], gv_bc[0:PT, :], ALU.mult, [b_vtmp, b_gv], [b_vn[k2]])
                        bs_, bbs = nb()
                        for h in range(4):
                            lhs = Msg_s[:, h, :] if sample else Msg[:, h, :]
                            mm(bs_[0:PT, h * 128:(h + 1) * 128], lhs, vn[k2][0:PT, h * 128:(h + 1) * 128], True, True,
                               [b_Msg, b_vn[k2]], [bbs])
                        bsb = bsT_s if sample else bsT
                        a_tm = a_tms[k2]; b_atm = b_atms[k2]
                        for h in range(4):
                            stt(a_tm[0:PT, h * 128:(h + 1) * 128], bs_[0:PT, h * 128:(h + 1) * 128], bsb[0:PT, h:h + 1],
                                u_tm[k2][0:PT, h * 128:(h + 1) * 128], ALU.add, ALU.mult, [bbs, b_col, b_u[k2]], [b_atm])

                    def stage2b(s):
                        k2 = s % 2
                        tsl = slice(s * 128, s * 128 + PT)
                        a_tm = a_tms[k2]; b_atm = b_atms[k2]; mixa = mixas[k2]; b_mixa = b_mixas[k2]
                        act(junk[0:PT, 0:512], a_tm[0:PT, :], AF.Square, [b_atm], [b_junk, b_ssa], accum_out=ssa[0:PT, :])
                        rstd(ra[0:PT, :], ssa[0:PT, :], 512.0, PT, 1, [b_ssa], [b_ra], tma[0:PT, :], b_tma)
                        stt(mixa[0:PT, :], a_tm[0:PT, :], ra[0:PT, 0:1], goa_bc[0:PT, :], ALU.mult, ALU.mult, [b_atm, b_ra, b_goa], [b_mixa])
                        bk, bb = nb()
                        bkb = bk[:].bitcast(BF16)
                        for m in range(4):
                            tr(bkb[:, m * PT:(m + 1) * PT], mixa[0:PT, m * 128:(m + 1) * 128], identb[0:PT, 0:PT], [b_mixa, b_identb], [bb])
                        cp("act", mixTa_t[:, :, tsl], bkb[:, 0:4 * PT].rearrange("p (a t) -> p a t", a=4), [bb], [b_mixTa_t])

                    pend = stage1(0)
                    stage1e(0, *pend[2:])
                    lag = None
                    for s in range(NS):
                        nxt = stage1(s + 1) if s + 1 < NS else None
                        stage2(s, pend[0], pend[1])
                        if lag is not None:
                            stage2b(lag)
                        if nxt is not None:
                            stage1e(s + 1, *nxt[2:])
                        lag = s
                        pend = nxt
                    stage2b(lag)

                def z_readback():
                    if sample:
                        zv = zscr[2048:2112, :].rearrange("(b t) c -> b t c", t=4)
                        dma("pool", Z[0:16, 0:4, :], zv, [b_zscr], [b_Zs[0]])
                        dma("pool", Z[0:16, 4:8, :], zv, [b_zscr], [b_Zs[0]])
                    else:
                        for s_ in range(4):
                            r0 = h1row + 128 * s_
                            dma("pool", Z[16 * s_:16 * s_ + 16, :, :], zscr[r0:r0 + 128, :].rearrange("(k i) c -> k i c", i=8), [b_zscr], [b_Zs[s_]])

                def ssm_front():
                    if sample:
                        bk1, bbk1 = nb(); bk2, bbk2 = nb()
                        for gq in range(8):
                            gs = slice(gq * 4, gq * 4 + 4)
                            Hin = Hins[gq % 2]; b_Hin = b_Hins[gq % 2]
                            dma("sp", Hin[:, :, 0:64], sre[:, gs, :], [], [b_Hin]); dma("sp", Hin[:, :, 64:128], sim[:, gs, :], [], [b_Hin])
                            dma("sp", Hin[:, :, 128:192], sim[:, gs, :], [], [b_Hin]); dma("sp", Hin[:, :, 192:256], sre[:, gs, :], [], [b_Hin])
                            for gl in range(4):
                                g = gq * 4 + gl
                                tr(bk1[:, g * 16:(g + 1) * 16], Hin[:, gl, 0:128], identf[0:16, 0:16], [b_Hin, b_identf], [bbk1])
                                tr(bk2[:, g * 16:(g + 1) * 16], Hin[:, gl, 128:256], identf[0:16, 0:16], [b_Hin, b_identf], [bbk2])
                        f2 = lambda v_: v_.rearrange("p g b -> p (g b)")
                        cp("act", f2(V10), bk1[:, 0:512], [bbk1], [b_V10])
                        cp("dve", f2(V20), bk2[:, 0:512], [bbk2], [b_V20])
                        cp("dve", HPb[:, :, 0:16], V10, [b_V10], [b_HPb])
                        shb = [128, 32, 16]
                        tt("dve", Vend, V10, AR4[:].unsqueeze(2).broadcast_to(shb), ALU.mult, [b_V10, b_A4], [b_Vend])
                        tt("dve", Vt, V20, AI4[:].unsqueeze(2).broadcast_to(shb), ALU.mult, [b_V20, b_A4], [b_Vt])
                        tt("dve", Vend, Vend, Vt, ALU.add, [b_Vend, b_Vt], [b_Vend])

                    for gh in range(2):
                        bk, bb = nb()
                        bkb = bk[:].bitcast(BF16)
                        for gl in range(16):
                            g = gh * 16 + gl
                            tr(bkb[:, gl * NC:(gl + 1) * NC], Z[:].rearrange("k i c -> k (i c)")[0:NC, g:4096:32], identb[0:NC, 0:NC],
                               b_Zs + [b_identb], [bb])
                        cp("act" if gh == 0 else "dve", X2[:, gh * 16:gh * 16 + 16, 0:NC],
                           bkb[:, 0:16 * NC].rearrange("p (g k) -> p g k", g=16), [bb], [b_X2])
                    if sample:
                        b1, bb1 = nb()
                        for g in range(32):
                            mm(b1[:, g * 16:(g + 1) * 16], MinX[64:128, g, 0:128], X2[64:128, g, 0:16], True, True, [b_MinX, b_X2], [bb1])
                    else:
                        GPB = 8
                        for gq in range(4):
                            gs = slice(gq * GPB, (gq + 1) * GPB)
                            b1, bb1 = nb(); b2, bb2 = nb()
                            for gl in range(GPB):
                                g = gq * GPB + gl
                                mm(b1[:, gl * 64:(gl + 1) * 64], MinX[:, g, 0:128], X2[:, g, 0:64], True, True, [b_MinX, b_X2], [bb1])
                                mm(b2[:, gl * 64:(gl + 1) * 64], MinX[:, g, 64:192], X2[:, g, 0:64], True, True, [b_MinX, b_X2], [bb2])
                            s1v = b1[:, 0:512].rearrange("p (g k) -> p g k", g=GPB)
                            s2v = b2[:, 0:512].rearrange("p (g k) -> p g k", g=GPB)
                            wr1 = b_R1[gq * GPB:(gq + 1) * GPB]; wr2 = b_R2[gq * GPB:(gq + 1) * GPB]
                            tt("dve", R1[:, gs, :], s1v, TC[:, gs, :], ALU.mult, [bb1, b_tab], wr1)
                            tt("dve", mA[:], s2v, TSs[:, gs, :], ALU.mult, [bb2, b_tab], [b_mA])
                            tt("pool", R1[:, gs, :], R1[:, gs, :], mA[:], ALU.add, wr1 + [b_mA], wr1)
                            tt("dve", R2[:, gs, :], s2v, TC[:, gs, :], ALU.mult, [bb2, b_tab], wr2)
                            tt("dve", mB[:], s1v, TSs[:, gs, :], ALU.mult, [bb1, b_tab], [b_mB])
                            tt("pool", R2[:, gs, :], R2[:, gs, :], mB[:], ALU.subtract, wr2 + [b_mB], wr2)
                    if sample:
                        shared["b1"] = (b1, bb1)

                def scan():
                    if sample:
                        f2 = lambda v_: v_.rearrange("p g b -> p (g b)")
                        b1, bb1 = shared["b1"]
                        tt("dve", f2(Vend), f2(Vend), b1[:, 0:512], ALU.add, [b_Vend, bb1], [b_Vend])
                        for gq in range(8):
                            bk, bb = nb()
                            for gl in range(4):
                                g = gq * 4 + gl
                                tr(bk[0:16, gl * 128:(gl + 1) * 128], Vend[:, g, :], identf[:], [b_Vend, b_identf], [bb])
                            Hout = Houts[gq % 2]; b_Hout = b_Houts[gq % 2]
                            cp("act", Hout, bk[0:16, 0:512].rearrange("p (g c) -> p g c", g=4), [bb], [b_Hout])
                            gs = slice(gq * 4, gq * 4 + 4)
                            dma("sp", sres[:, gs, :], Hout[:, :, 0:64], [b_Hout], [], is_output=True)
                            dma("sp", sims[:, gs, :], Hout[:, :, 64:128], [b_Hout], [], is_output=True)
                    else:
                        cp("act", HPb[:, :, 0], C1[:], [b_C], [b_HPb])
                        tt("dve", cA[:], MG8[:], C1[:], ALU.mult, [b_mg8, b_C], [b_cA])
                        tt("dve", R1[:, :, 0], R1[:, :, 0], cA[:], ALU.add, b_R1 + [b_cA], b_R1)
                        tt("dve", cB[:], MG8[:], C2[:], ALU.mult, [b_mg8, b_C], [b_cB])
                        tt("dve", R2[:, :, 0], R2[:, :, 0], cB[:], ALU.add, b_R2 + [b_cB], b_R2)
                        for g in range(32):
                            for (RR, bR) in ((R1, b_R1), (R2, b_R2)):
                                P.op("dve", lambda e, RR=RR, g=g: e.tensor_tensor_scan(
                                    out=RR[:, g, :], data0=MG8[:, g:g + 1].broadcast_to([128, 64]), data1=RR[:, g, :],
                                    initial=0.0, op0=ALU.mult, op1=ALU.add), [bR[g], b_mg8], [bR[g]])
                        for gq in range(4):
                            gs = slice(gq * 8, gq * 8 + 8)
                            wr1 = b_R1[gq * 8:gq * 8 + 8]; wr2 = b_R2[gq * 8:gq * 8 + 8]
                            tt("dve", mA[:], R1[:, gs, :], TC[:, gs, :], ALU.mult, wr1 + [b_tab], [b_mA])
                            tt("pool", mB[:], R2[:, gs, :], TSs[:, gs, :], ALU.mult, wr2 + [b_tab], [b_mB])
                            tt("dve", HPb[:, gs, 1:64], mA[:, :, 0:63], mB[:, :, 0:63], ALU.subtract, [b_mA, b_mB], [b_HPb])
                        tt("dve", cA[:], R1[:, :, 63], TC[:, :, 63], ALU.mult, b_R1 + [b_tab], [b_cA])
                        tt("dve", cB[:], R2[:, :, 63], TSs[:, :, 63], ALU.mult, b_R2 + [b_tab], [b_cB])
                        tt("dve", C1[:], cA[:], cB[:], ALU.subtract, [b_cA, b_cB], [b_C])
                        tt("dve", cA[:], R2[:, :, 63], TC[:, :, 63], ALU.mult, b_R2 + [b_tab], [b_cA])
                        tt("dve", cB[:], R1[:, :, 63], TSs[:, :, 63], ALU.mult, b_R1 + [b_tab], [b_cB])
                        tt("dve", C2[:], cA[:], cB[:], ALU.add, [b_cA, b_cB], [b_C])

                def back(hook=None):
                    Zy = Z
                    for gq in range(8):
                        bk, bb = nb()
                        for gl in range(4):
                            g = gq * 4 + gl
                            if sample:
                                mm(bk[0:NC, gl * 128:(gl + 1) * 128], X2[0:64, g, 0:NC], Toep[0:64, g, :], True, False, [b_X2, b_Toep], [bb])
                            else:
                                mm(bk[0:NC, gl * 128:(gl + 1) * 128], X2[:, g, 0:NC], Toep[:, g, :], True, False, [b_X2, b_Toep], [bb])
                            mm(bk[0:NC, gl * 128:(gl + 1) * 128], HPb[:, g, 0:NC], Mout[:, g, :], False, True, [b_HPb, b_Mout], [bb])
                        cp("dve" if gq % 2 == 0 else "act", Zy[0:NC, :, 64 * gq:64 * gq + 64].rearrange("p j (g c) -> p g j c", g=4),
                           bk[0:NC, 0:512].rearrange("p (g j c) -> p g j c", g=4, j=8), [bb], b_Zs)
                    NJ = 4 if sample else 8
                    for cbp in range(2):
                        bk, bb = nb()
                        bkb = bk[:].bitcast(BF16)
                        for cl in range(2):
                            cb = cbp * 2 + cl
                            for j in range(NJ):
                                o0 = (cl * NJ + j) * NC
                                tr(bkb[:, o0:o0 + NC], Zy[0:NC, j, cb * 128:(cb + 1) * 128], identb[0:NC, 0:NC], b_Zs + [b_identb], [bb])
                        for cl in range(2):
                            cb = cbp * 2 + cl
                            o0 = cl * NJ * NC
                            act(gTb[:, cb, 0:ntok].rearrange("p (k j) -> p j k", j=NJ),
                                bkb[:, o0:o0 + NJ * NC].rearrange("p (j k) -> p j k", j=NJ), AF.Gelu_apprx_tanh, [bb], [b_gTb])
                    if hook is not None:
                        hook()
                    for m in range(4):
                        bk, bb = nb()
                        for kt in range(4):
                            mm(bk[:, 0:ntok], w_glu_sb[:, kt, m * 128:(m + 1) * 128], gTb[:, kt, 0:ntok], kt == 0, kt == 3, [b_wglu, b_gTb], [bb])
                        act(sg[:, 0:ntok], bk[:, 0:ntok], AF.Sigmoid, [bb, b_col], [b_sg], bias=bglu[:, m:m + 1])
                        tt("dve", mixTb[:, m, 0:ntok], gTb[:, m, 0:ntok], sg[:, 0:ntok], ALU.mult, [b_gTb, b_sg], [b_mixTb])
                    act(sqb[:, :, 0:ntok], mixTb[:, :, 0:ntok], AF.Square, [b_mixTb], [b_sqb, b_R1[8]])
                    bkR, bbR = nb()
                    for s in range(NS):
                        tsl = slice(s * 128, s * 128 + PT)
                        for m in range(4):
                            mm(bkR[0:PT, s:s + 1], sqb[:, m, tsl], onesb[:, 0:1], m == 0, m == 3, [b_ones, b_sqb, b_R1[8]], [bbR])
                    rstd(rb4[0:PT, 0:NS], bkR[0:PT, 0:NS], 512.0, PT, NS, [bbR], [b_rb4], tb4[0:PT, 0:NS], b_tb4)
                    for s in range(NS):
                        tsl = slice(s * 128, s * 128 + PT)
                        hi_ = cnt_a["xh"] % 2; cnt_a["xh"] += 1
                        xh_ = xh[hi_]; bxh_ = b_xh[hi_]
                        if sample:
                            dma("sp", xh_[0:64, :], src, [], [bxh_])
                        else:
                            dma("sp", xh_[:, :], src[tok0 + 128 * s:tok0 + 128 * (s + 1), :], [], [bxh_])
                        for nh in range(2):
                            bka, bba = nb(); bkb_, bbb = nb()
                            for kt in range(4):
                                mm(bka[0:PT, :], mixTa_t[:, kt, tsl], w_out_sb[:, kt, nh * 512:(nh + 1) * 512], kt == 0, kt == 3, [b_mixTa_t, b_wout], [bba])
                            for kt in range(4):
                                mm(bkb_[0:PT, :], mixTb[:, kt, tsl], w_out_sb[:, 4 + kt, nh * 512:(nh + 1) * 512], kt == 0, kt == 3, [b_mixTb, b_wout], [bbb])
                            xsl = xh_[0:PT, nh * 512:(nh + 1) * 512]
                            tt("dve", xsl, xsl, bka[0:PT, :], ALU.add, [bxh_, bba], [bxh_])
                            stt(xsl, bkb_[0:PT, :], rb4[0:PT, s:s + 1], xsl, ALU.mult, ALU.add, [bxh_, bbb, b_rb4], [bxh_])
                        if sample:
                            dma("sp", h1d[h1row:h1row + 64, :], xh_[0:64, :], [bxh_], [])
                        else:
                            dma("sp", h1d[h1row + 128 * s:h1row + 128 * (s + 1), :], xh_[:, :], [bxh_], [])

                return front, ssm_front, scan, back, z_readback

            tiles = [mixer_tile(512 * t, xp, 512, False, 512 * t, t % 2) for t in range(4)]
            tiles.append(mixer_tile(0, xs, 64, True, 2048, 0))
            tiles[0][0](0)
            sa_late()
            tiles[0][0](1)
            tiles[0][4]()
            tiles[0][1]()
            for t in range(5):
                if t + 1 < 5:
                    tiles[t + 1][0](0)
                tiles[t][2]()
                if t + 1 < 5:
                    tiles[t + 1][0](1)
                tiles[t][3](tiles[t + 1][4] if t + 1 < 5 else None)
                if t == 3:
                    bk, bb = nb()
                    tr(bk[0:32, 0:128], C1[:], identf[:], [b_C, b_identf], [bb])
                    cp("act", stS[:], bk[0:32, 0:128], [bb], [b_stS])
                    dma("sp", srep, stS[:, 0:64], [b_stS], [], is_output=True)
                    dma("sp", simp, stS[:, 64:128], [b_stS], [], is_output=True)
                if t + 1 < 5:
                    tiles[t + 1][1]()
            P.barrier()
        SM.close()

        with contextlib.ExitStack() as SB:
            def sbb(name, shape, dt=F32):
                return sb(name, shape, dt, SB)

            wfo = sbb("wfo", [128, 22, 1024], BF16); b_wfo = Buf()
            wfov = w_ffn_out.rearrange("(a p) n -> p a n", p=128)
            wfo_pending = list(range(22))
            gffn_bc = sbb("gffn_bc", [128, 1024]); gfin_bc = sbb("gfin_bc", [128, 1024]); b_gbc = Buf()
            dma("sp", gffn_bc[:], g_ffn.partition_broadcast(128), [], [b_gbc])
            dma("sp", gfin_bc[:], g_final.partition_broadcast(128), [], [b_gbc])
            cwl = sbb("cwl", [88, 2, 128]); b_cwl = Buf()
            cwv = conv_w.rearrange("k (c p) -> (k c) p", p=128)
            dma("sp", cwl[:, 0, :], cwv[0:88, :], [], [b_cwl])
            dma("sp", cwl[0:44, 1, :], cwv[88:132, :], [], [b_cwl])
            dma("sp", cwl[44:88, 1, :], conv_b.rearrange("(c p) -> c p", p=128), [], [b_cwl])
            cw = sbb("cw", [128, 4, 44]); b_cw = Buf()
            bk, bb = nb()
            tr(bk[:, 0:88], cwl[:, 0, :], identf[0:88, 0:88], [b_cwl, b_identf], [bb])
            tr(bk[:, 88:176], cwl[:, 1, :], identf[0:88, 0:88], [b_cwl, b_identf], [bb])
            cp("dve", cw[:].rearrange("p k c -> p (k c)"), bk[:, 0:176], [bb], [b_cw])
            hist = sbb("hist", [128, 2, 44]); b_hist0 = Buf(); b_hist = [Buf() for _ in range(44)]
            P.op("pool", lambda e: e.memset(hist[:], 0.0), [], [b_hist0] + b_hist)
            hist_s = sbb("hist_s", [128, 44, 32]); b_hists = Buf()
            scl = sbb("scl", [32, 1408]); b_scl = Buf()
            for q4 in range(4):
                dma("sp", scl[:], scv[:, q4 * 1408:(q4 + 1) * 1408], [], [b_scl])
                bk, bb = nb()
                for cl in range(11):
                    tr(bk[:, cl * 32:(cl + 1) * 32], scl[:, cl * 128:(cl + 1) * 128], identf[0:32, 0:32], [b_scl, b_identf], [bb])
                cp("act", hist_s[:, q4 * 11:(q4 + 1) * 11, :], bk[:, 0:352].rearrange("p (c r) -> p c r", c=11), [bb], [b_hists])
            cst_s = sbb("cst_s", [128, 44, 32]); b_csts = Buf()

            NTB = 1088
            hb = [sbb("hb%d" % i, [128, 1024]) for i in range(3)]; b_hb = [Buf(), Buf(), Buf()]
            xn = [sbb("xnB%d" % i, [128, 1024], BF16) for i in range(2)]; b_xn = [Buf(), Buf()]
            junk = sbb("junkB", [128, 1024], BF16); b_junk = Buf()
            ssB = [sbb("ssB%d" % i, [128, 1]) for i in range(2)]; rsB = [sbb("rsB%d" % i, [128, 1]) for i in range(2)]
            tmB = [sbb("tmB%d" % i, [128, 1]) for i in range(2)]
            b_ssB = [Buf(), Buf()]; b_rsB = [Buf(), Buf()]; b_tmB = [Buf(), Buf()]
            n2T = sbb("n2T", [128, 8, NTB], BF16); b_n2T = Buf()
            zt = sbb("zt", [128, 22, NTB], BF16); b_z = Buf()
            wst = [sbb("wst%d" % i, [128, 8, 256], BF16) for i in range(3)]; b_wst = [Buf(), Buf(), Buf()]
            upS = [[sbb("upS%d_%d" % (i, j), [128, 514]) for j in range(2)] for i in range(2)]
            b_upS = [[Buf(), Buf()], [Buf(), Buf()]]
            b_upH = [[Buf(), Buf()], [Buf(), Buf()]]
            hsave = [sbb("hsv%d" % i, [128, 2]) for i in range(3)]; b_hsave = [Buf() for _ in range(3)]
            acc = [[sbb("acc%d_%d" % (i, j), [128, 512]) for j in range(2)] for i in range(3)]
            b_acc = [[Buf(), Buf()] for _ in range(3)]
            wfiv = w_ffn_in.rearrange("(a p) n -> p a n", p=128)
            cnt = {"hb": 0, "w": 0, "u": 0, "yo": 0}

            def ffn_tile(subtiles, cgs):
                def b1a(si):
                    hrow, PT, oap, col0 = subtiles[si]
                    i3 = cnt["hb"] % 3; cnt["hb"] += 1
                    k2 = si % 2
                    dma("sp", hb[i3][0:PT, :], h1d[hrow:hrow + PT, :], [], [b_hb[i3]])
                    act(junk[0:PT, :], hb[i3][0:PT, :], AF.Square, [b_hb[i3]], [b_junk, b_ssB[k2]], accum_out=ssB[k2][0:PT, :])
                    rstd(rsB[k2][0:PT, :], ssB[k2][0:PT, :], 1024.0, PT, 1, [b_ssB[k2]], [b_rsB[k2]], tmB[k2][0:PT, :], b_tmB[k2])
                    stt(xn[k2][0:PT, :], hb[i3][0:PT, :], rsB[k2][0:PT, 0:1], gffn_bc[0:PT, :], ALU.mult, ALU.mult,
                        [b_hb[i3], b_rsB[k2], b_gbc], [b_xn[k2]])

                def b1b(si):
                    hrow, PT, oap, col0 = subtiles[si]
                    k2 = si % 2
                    bk, bb = nb()
                    bkb = bk[:].bitcast(BF16)
                    for dti in range(8):
                        tr(bkb[:, dti * PT:(dti + 1) * PT], xn[k2][0:PT, dti * 128:(dti + 1) * 128], identb[0:PT, 0:PT],
                           [b_xn[k2], b_identb], [bb])
                    cp("act", n2T[:, :, col0:col0 + PT], bkb[:, 0:8 * PT].rearrange("p (a t) -> p a t", a=8), [bb], [b_n2T])

                b1a(0)
                for si in range(len(subtiles)):
                    if si + 1 < len(subtiles):
                        b1a(si + 1)
                    b1b(si)
                def wload(cc):
                    wj = (cnt["w"] + cc) % 3
                    dma("pool", wst[wj][:, :, 0:128], wfiv[:, :, cc * 128:(cc + 1) * 128], [], [b_wst[wj]])
                    dma("pool", wst[wj][:, :, 128:256], wfiv[:, :, DFF + cc * 128:DFF + (cc + 1) * 128], [], [b_wst[wj]])
                wload(0); wload(1)
                pend_fin = [None]
                for c in range(22):
                    wi = (cnt["w"] + c) % 3
                    if c + 2 < 22:
                        wload(c + 2)
                    if wfo_pending:
                        a_ = wfo_pending.pop(0)
                        dma("pool", wfo[:, a_, :], wfov[:, a_, :], [], [b_wfo])
                    for (c0, ncol, smp) in cgs:
                        ui = cnt["u"] % 3; uu = cnt["u"] % 2; cnt["u"] += 1
                        bks = [nb(), nb()]
                        for gv in range(2):
                            for dti in range(8):
                                mm(bks[gv][0][:, 0:ncol], wst[wi][:, dti, gv * 128:(gv + 1) * 128], n2T[:, dti, c0:c0 + ncol],
                                   dti == 0, dti == 7, [b_wst[wi], b_n2T], [bks[gv][1]])
                        for gv in range(2):
                            ci = gv * 22 + c
                            bk, bb = bks[gv]
                            U = upS[uu][gv]; bU = b_upS[uu][gv]; A = acc[ui][gv]; bA = b_acc[ui][gv]
                            w0 = cw[:, 0, ci:ci + 1]; w1 = cw[:, 1, ci:ci + 1]; w2 = cw[:, 2, ci:ci + 1]; bs = cw[:, 3, ci:ci + 1]
                            if F_CONVOLD:
                                b_h1 = b_hist[0]
                                if not smp:
                                    cp("pool", U[:, 0:2], hist[:, :, ci], [b_h1], [bU])
                                    cp("act", U[:, 2:2 + ncol], bk[:, 0:ncol], [bb], [bU])
                                    cp("pool", hist[:, :, ci], U[:, ncol:ncol + 2], [bU], [b_h1])
                                    act(A[:, 0:ncol], bk[:, 0:ncol], AF.Identity, [bb, b_cw], [bA], scale=w2, bias=bs)
                                    stt(A[:, 0:ncol], U[:, 1:1 + ncol], w1, A[:, 0:ncol], ALU.mult, ALU.add, [bU, b_cw, bA], [bA])
                                    stt(A[:, 0:ncol], U[:, 0:ncol], w0, A[:, 0:ncol], ALU.mult, ALU.add, [bU, b_cw, bA], [bA])
                                else:
                                    U3 = U[:, 0:96].rearrange("p (b t) -> p b t", t=6)
                                    A3 = A[:, 0:64].rearrange("p (b t) -> p b t", t=4)
                                    cp("pool", U3[:, :, 0:2], hist_s[:, ci, :].rearrange("p (b k) -> p b k", k=2), [b_hists], [bU])
                                    cp("act", U3[:, :, 2:6], bk[:, 0:64].rearrange("p (b t) -> p b t", t=4), [bb], [bU])
                                    cp("pool", cst_s[:, ci, :].rearrange("p (b k) -> p b k", k=2), U3[:, :, 4:6], [bU], [b_csts])
                                    act(A3, bk[:, 0:64].rearrange("p (b t) -> p b t", t=4), AF.Identity, [bb, b_cw], [bA], scale=w2, bias=bs)
                                    stt(A3, U3[:, :, 1:5], w1, A3, ALU.mult, ALU.add, [bU, b_cw, bA], [bA])
                                    stt(A3, U3[:, :, 0:4], w0, A3, ALU.mult, ALU.add, [bU, b_cw, bA], [bA])
                                continue
                            bH = b_upH[uu][gv]
                            if (not smp) and gv == 1:
                                hsv = hsave[ui]; bhs = b_hsave[ui]
                                cp("act", hsv[:, 0:2], hist[:, :, ci], [b_hist[ci]], [bhs])
                                cp("act", hist[:, :, ci], bk[:, ncol - 2:ncol], [bb], [b_hist[ci]])
                                act(A[:, 0:ncol], bk[:, 0:ncol], AF.Identity, [bb, b_cw], [bA], scale=w2, bias=bs)
                                stt(A[:, 1:ncol], bk[:, 0:ncol - 1], w1, A[:, 1:ncol], ALU.mult, ALU.add, [bb, b_cw, bA], [bA])
                                stt(A[:, 2:ncol], bk[:, 0:ncol - 2], w0, A[:, 2:ncol], ALU.mult, ALU.add, [bb, b_cw, bA], [bA])
                                stt(A[:, 0:1], hsv[:, 1:2], w1, A[:, 0:1], ALU.mult, ALU.add, [bhs, b_cw, bA], [bA])
                                stt(A[:, 0:2], hsv[:, 0:2], w0, A[:, 0:2], ALU.mult, ALU.add, [bhs, b_cw, bA], [bA])
                            elif not smp:
                                cp("act", U[:, 0:2], hist[:, :, ci], [b_hist[ci]], [bH])
                                cp("act", U[:, 2:2 + ncol], bk[:, 0:ncol], [bb], [bU])
                                cp("act", hist[:, :, ci], bk[:, ncol - 2:ncol], [bb], [b_hist[ci]])
                                act(A[:, 0:ncol], bk[:, 0:ncol], AF.Identity, [bb, b_cw], [bA], scale=w2, bias=bs)
                                stt(A[:, 0:ncol], U[:, 1:1 + ncol], w1, A[:, 0:ncol], ALU.mult, ALU.add, [bU, bH, b_cw, bA], [bA])
                                stt(A[:, 0:ncol], U[:, 0:ncol], w0, A[:, 0:ncol], ALU.mult, ALU.add, [bU, bH, b_cw, bA], [bA])
                            else:
                                U3 = U[:, 0:96].rearrange("p (b t) -> p b t", t=6)
                                A3 = A[:, 0:64].rearrange("p (b t) -> p b t", t=4)
                                bk3 = bk[:, 0:64].rearrange("p (b t) -> p b t", t=4)
                                cp("act", U3[:, :, 0:2], hist_s[:, ci, :].rearrange("p (b k) -> p b k", k=2), [b_hists], [bH])
                                cp("act", U3[:, :, 2:6], bk3, [bb], [bU])
                                cp("act", cst_s[:, ci, :].rearrange("p (b k) -> p b k", k=2), bk3[:, :, 2:4], [bb], [b_csts])
                                act(A3, bk3, AF.Identity, [bb, b_cw], [bA], scale=w2, bias=bs)
                                stt(A3, U3[:, :, 1:5], w1, A3, ALU.mult, ALU.add, [bU, bH, b_cw, bA], [bA])
                                stt(A3, U3[:, :, 0:4], w0, A3, ALU.mult, ALU.add, [bU, bH, b_cw, bA], [bA])
                        def fin(ui=ui, c=c, c0=c0, ncol=ncol):
                            act(acc[ui][0][:, 0:ncol], acc[ui][0][:, 0:ncol], AF.Gelu_apprx_tanh, [b_acc[ui][0]], [b_acc[ui][0]])
                            tt("pool", zt[:, c, c0:c0 + ncol], acc[ui][0][:, 0:ncol], acc[ui][1][:, 0:ncol], ALU.mult,
                               [b_acc[ui][0], b_acc[ui][1]], [b_z])
                        if pend_fin[0] is not None:
                            pend_fin[0]()
                        pend_fin[0] = fin
                if pend_fin[0] is not None:
                    pend_fin[0]()
                    pend_fin[0] = None
                cnt["w"] += 22
                for si, (hrow, PT, oap, col0) in enumerate(subtiles):
                    i3 = cnt["hb"] % 3; cnt["hb"] += 1
                    k2 = si % 2
                    dma("sp", hb[i3][0:PT, :], h1d[hrow:hrow + PT, :], [], [b_hb[i3]])
                    bks = [nb(), nb()]
                    for nh in range(2):
                        for c in range(22):
                            mm(bks[nh][0][0:PT, :], zt[:, c, col0:col0 + PT], wfo[:, c, nh * 512:(nh + 1) * 512], c == 0, c == 21,
                               [b_z, b_wfo], [bks[nh][1]])
                    for nh in range(2):
                        tt("dve", hb[i3][0:PT, nh * 512:(nh + 1) * 512], hb[i3][0:PT, nh * 512:(nh + 1) * 512], bks[nh][0][0:PT, :], ALU.add,
                           [b_hb[i3], bks[nh][1]], [b_hb[i3]])
                    act(junk[0:PT, :], hb[i3][0:PT, :], AF.Square, [b_hb[i3]], [b_junk, b_ssB[k2]], accum_out=ssB[k2][0:PT, :])
                    rstd(rsB[k2][0:PT, :], ssB[k2][0:PT, :], 1024.0, PT, 1, [b_ssB[k2]], [b_rsB[k2]], tmB[k2][0:PT, :], b_tmB[k2])
                    stt(hb[i3][0:PT, :], hb[i3][0:PT, :], rsB[k2][0:PT, 0:1], gfin_bc[0:PT, :], ALU.mult, ALU.mult,
                        [b_hb[i3], b_rsB[k2], b_gbc], [b_hb[i3]])
                    dma("sp", oap, hb[i3][0:PT, :], [b_hb[i3]], [], is_output=True)

            subA = [(128 * s, 128, yp[128 * s:128 * (s + 1), :], 128 * s) for s in range(8)]
            ffn_tile(subA, [(0, 512, False), (512, 512, False)])
            subB = [(1024 + 128 * s, 128, yp[1024 + 128 * s:1024 + 128 * (s + 1), :], 128 * s) for s in range(8)]
            subB.append((2048, 64, ys, 1024))
            ffn_tile(subB, [(0, 512, False), (512, 512, False), (1024, 64, True)])
            cpo = sbb("cpo", [88, 128]); b_cpo = Buf()
            bk, bb = nb()
            tr(bk[0:88, 0:128], hist[:].rearrange("p k c -> p (k c)"), identf[:], b_hist + [b_identf], [bb])
            cp("act", cpo[:], bk[0:88, 0:128], [bb], [b_cpo])
            dma("sp", cvp[0].rearrange("(c p) -> c p", p=128), cpo[0:44, :], [b_cpo], [], is_output=True)
            dma("sp", cvp[1].rearrange("(c p) -> c p", p=128), cpo[44:88, :], [b_cpo], [], is_output=True)
            cso = sbb("cso", [128, 11, 128]); b_cso = Buf()
            for q3 in range(3):
                bk, bb = nb()
                nblk = 4 if q3 < 2 else 3
                for bl in range(nblk):
                    cbk = q3 * 4 + bl
                    tr(bk[:, bl * 128:(bl + 1) * 128], cst_s[:, cbk * 4:(cbk + 1) * 4, :].rearrange("p c r -> p (c r)"), identf[:],
                       [b_csts, b_identf], [bb])
                cp("act", cso[:, q3 * 4:q3 * 4 + nblk, :], bk[:, 0:nblk * 128].rearrange("p (a c) -> p a c", a=nblk), [bb], [b_cso])
            cvsv = cvs.rearrange("r (cb cl p) -> cl r cb p", cl=4, p=128)
            for cl in range(4):
                dma("sp", cvsv[cl], cso[32 * cl:32 * cl + 32, :, :], [b_cso], [], is_output=True)
        P.build()
    return nc


_CACHE = {}


def _consts():
    ident = np.eye(128, dtype=np.float32)
    ii = np.arange(128) // 16
    mtoep = (ii[None, :] >= ii[:, None]).astype(np.float32)
    ar = np.arange(128)
    msgu = (ar[:, None] <= ar[None, :]).astype(np.float32)
    nv = np.zeros(32, np.float32)
    nv[0:8] = -np.arange(8)
    nv[8:16] = 7 - np.arange(8)
    nv[16:25] = np.arange(9)
    nv[25] = 8; nv[26] = 4; nv[27] = 1
    nvec = np.tile(nv[None, :], (128, 1)).astype(np.float32)
    sigma = np.ones((128, 1), np.float32)
    sigma[64:] = -1.0
    kvec = np.tile(np.arange(1, 65, dtype=np.float32)[None, :], (128, 1))
    return dict(c_ident=ident, c_mtoep=mtoep, c_msgu=msgu, c_nvec=nvec, c_sigma=sigma, c_kvec=kvec)


def kernel(x_prompt, x_sample, state_ssm_re, state_ssm_im, state_conv,
           g_mix, w_in, g_v, w_s, b_s, lam_re, lam_im, log_dt, b_re, b_im, c_re, c_im,
           d_skip, w_glu, b_glu, g_out_a, g_out_b, w_out, g_ffn, w_ffn_in, conv_w, conv_b,
           w_ffn_out, g_final):
    f = lambda a: np.ascontiguousarray(np.asarray(a, dtype=np.float32))
    if "nc" not in _CACHE:
        _CACHE["nc"] = build_nc()
    nc = _CACHE["nc"]
    shared = dict(
        g_mix=f(g_mix), w_in=f(w_in), g_v=f(g_v).reshape(512), w_s=f(w_s), b_s=f(b_s),
        lam_re=f(lam_re), lam_im=f(lam_im), log_dt=f(log_dt), b_re=f(b_re), b_im=f(b_im),
        c_re=f(c_re).reshape(512, 64), c_im=f(c_im).reshape(512, 64), d_skip=f(d_skip),
        w_glu=f(w_glu), b_glu=f(b_glu), g_out_a=f(g_out_a), g_out_b=f(g_out_b), w_out=f(w_out),
        g_ffn=f(g_ffn), w_ffn_in=f(w_ffn_in), conv_w=f(conv_w), conv_b=f(conv_b),
        w_ffn_out=f(w_ffn_out), g_final=f(g_final))
    shared.update(_consts())
    xpf = f(x_prompt); xsf = f(x_sample); sr = f(state_ssm_re); si = f(state_ssm_im); sc = f(state_conv)
    in_maps = []
    for c in range(NCORES):
        m = dict(shared)
        m["xp"] = xpf[c]
        m["xs"] = xsf[16 * c:16 * c + 16].reshape(64, D)
        m["sre"] = sr[16 * c:16 * c + 16]
        m["sim"] = si[16 * c:16 * c + 16]
        m["scv"] = sc[16 * c:16 * c + 16].reshape(32, 2 * DFF)
        in_maps.append(m)
    res = run_bass_kernel_spmd(nc, in_maps, core_ids=list(range(NCORES)))
    R = res.results
    y_prompt = np.stack([R[c]["yp"] for c in range(NCORES)], 0)
    y_sample = np.concatenate([R[c]["ys"].reshape(16, 4, D) for c in range(NCORES)], 0)
    v_sample = np.concatenate([R[c]["vs"].reshape(16, 4, 4, 128) for c in range(NCORES)], 0)
    srp = np.stack([R[c]["srep"] for c in range(NCORES)], 0)
    sip = np.stack([R[c]["simp"] for c in range(NCORES)], 0)
    cvp = np.stack([R[c]["cvp"] for c in range(NCORES)], 0)
    srs = np.concatenate([R[c]["sres"] for c in range(NCORES)], 0)
    sis = np.concatenate([R[c]["sims"] for c in range(NCORES)], 0)
    cvs = np.concatenate([R[c]["cvs"].reshape(16, 2, 2 * DFF) for c in range(NCORES)], 0)
    out = (y_prompt, y_sample, v_sample, srp, sip, cvp, srs, sis, cvs)
    return tuple(np.ascontiguousarray(o, dtype=np.float32) for o in out)
```

```python
import contextlib
import math
import numpy as np
import concourse.bass as bass
import concourse.mybir as mybir
from concourse.bass_utils import run_bass_kernel_spmd

F32 = mybir.dt.float32
BF16 = mybir.dt.bfloat16
I32 = mybir.dt.int32
AF = mybir.ActivationFunctionType
ALU = mybir.AluOpType
AX = mybir.AxisListType

NCORES = 8
import os as _os
F_POW = bool(int(_os.environ.get("F_POW", "0")))
F_SCANPOOL = bool(int(_os.environ.get("F_SCANPOOL", "0")))
F_CONVOLD = bool(int(_os.environ.get("F_CONVOLD", "0")))
F_TSACT = bool(int(_os.environ.get("F_TSACT", "0")))
D = 1024
DFF = 2816
EPS = 1e-6
TWO_PI = 2.0 * math.pi


class Buf:
    __slots__ = ("name", "last_w", "readers")

    def __init__(self, name=""):
        self.name = name
        self.last_w = None
        self.readers = []


class Op:
    __slots__ = ("eng", "kind", "emit", "deps", "marked", "seq", "dsem", "dval", "idx")

    def __init__(self, eng, kind, emit):
        self.eng = eng
        self.kind = kind
        self.emit = emit
        self.deps = []
        self.marked = False
        self.seq = 0
        self.dsem = None
        self.dval = 0


ENGS = ("pe", "act", "dve", "pool", "sp")


class Prog:
    def __init__(self, nc, n_dma_sems=12):
        self.nc = nc
        self.ops = {e: [] for e in ENGS}
        self.all_ops = []
        self.n_dma_sems = n_dma_sems
        self.dma_count = {e: 0 for e in ENGS}
        self.out_dmas = []
        self.since_barrier = []
        self.last_on_sem = {}

    def _add(self, op, reads, writes):
        deps = []
        for b in reads:
            if b.last_w is not None:
                deps.append(b.last_w)
        for b in writes:
            if b.last_w is not None:
                deps.append(b.last_w)
            deps.extend(b.readers)
        seen = set()
        for d in deps:
            if d is op or id(d) in seen:
                continue
            seen.add(id(d))
            if op.eng == "pe" and d.eng == "pe" and d.kind == "c" and op.kind == "c":
                continue
            op.deps.append(d)
            d.marked = True
        for b in reads:
            if op.kind == "c":
                b.readers = [r for r in b.readers if not (r.kind == "c" and r.eng == op.eng)]
            b.readers.append(op)
        for b in writes:
            b.last_w = op
            b.readers = []
        op.idx = len(self.all_ops)
        self.all_ops.append(op)
        self.ops[op.eng].append(op)
        self.since_barrier.append(op)
        return op

    def op(self, eng, emit, reads=(), writes=()):
        return self._add(Op(eng, "c", emit), reads, writes)

    def dma(self, queue, emit, reads=(), writes=(), is_output=False):
        o = Op(queue, "d", emit)
        k = self.dma_count[queue]
        self.dma_count[queue] = k + 1
        o.dsem = k % self.n_dma_sems
        o.dval = 16 * (k // self.n_dma_sems + 1)
        prev = self.last_on_sem.get((queue, o.dsem))
        if prev is not None:
            o.deps.append(prev)
        self.last_on_sem[(queue, o.dsem)] = o
        self._add(o, reads, writes)
        if is_output:
            self.out_dmas.append(o)
        return o

    def barrier(self, exclude=()):
        lasts = []
        for e in ENGS:
            lastc = None
            for o in self.ops[e]:
                if o.kind == "c":
                    lastc = o
            if lastc is not None:
                lasts.append(lastc)
        dmas = [o for o in self.since_barrier if o.kind == "d" and id(o) not in exclude]
        self.since_barrier = []
        for e in ("pe", "act", "dve", "pool", "sp"):
            o = Op(e, "c", lambda eng: eng.nop())
            for d in lasts + dmas:
                o.deps.append(d)
                d.marked = True
            o.idx = len(self.all_ops)
            self.all_ops.append(o)
            self.ops[e].append(o)

    def build(self):
        nc = self.nc
        with contextlib.ExitStack() as st:
            esem = {}
            for e in ("pe", "act", "dve", "pool", "sp"):
                esem[e] = st.enter_context(nc.semaphore("s_" + e))
            dsem = {}
            for q in ENGS:
                if self.dma_count[q] > 0:
                    dsem[q] = [
                        st.enter_context(nc.semaphore("d_%s_%d" % (q, i)))
                        for i in range(min(self.n_dma_sems, self.dma_count[q]))
                    ]
            for e in ENGS:
                c = 0
                for o in self.ops[e]:
                    if o.kind == "c" and o.marked:
                        c += 1
                        o.seq = c
            import os
            if os.environ.get("KDEBUG"):
                for e in ENGS:
                    print("ENG", e, "ops", len(self.ops[e]), "marked", max([o.seq for o in self.ops[e]] + [0]), "dmas", self.dma_count[e])
            block = st.enter_context(nc.Block())

            def run(ename, eng):
                waited = {}
                for o in self.ops[ename]:
                    for d in o.deps:
                        if d.kind == "c":
                            key = ("c", d.eng)
                            val = d.seq
                            sem = esem[d.eng]
                        else:
                            key = ("d", d.eng, d.dsem)
                            val = d.dval
                            sem = dsem[d.eng][d.dsem]
                        if waited.get(key, 0) >= val:
                            continue
                        waited[key] = val
                        eng.wait_ge(sem, val)
                    ins = o.emit(eng)
                    if o.kind == "c":
                        if o.marked:
                            ins.then_inc(esem[ename], 1)
                    else:
                        ins.then_inc(dsem[ename][o.dsem], 16)
                fin = {}
                for o in self.out_dmas:
                    if o.eng == ename:
                        fin[o.dsem] = max(fin.get(o.dsem, 0), o.dval)
                for s, v in fin.items():
                    eng.wait_ge(dsem[ename][s], v)

            @block.sync
            def _(e):
                run("sp", e)

            @block.tensor
            def _(e):
                run("pe", e)

            @block.scalar
            def _(e):
                run("act", e)

            @block.vector
            def _(e):
                run("dve", e)

            @block.gpsimd
            def _(e):
                run("pool", e)


def build_nc():
    nc = bass.Bass("TRN2", target_bir_lowering=False)
    P = Prog(nc)

    def din(name, shape):
        return nc.dram_tensor(name, list(shape), F32, kind="ExternalInput").ap()

    def dout(name, shape):
        return nc.dram_tensor(name, list(shape), F32, kind="ExternalOutput").ap()

    xp = din("xp", [2048, D]); xs = din("xs", [64, D])
    sre = din("sre", [16, 32, 64]); sim = din("sim", [16, 32, 64]); scv = din("scv", [32, 2 * DFF])
    g_mix = din("g_mix", [D]); w_in = din("w_in", [D, 1536]); g_v = din("g_v", [512])
    w_s = din("w_s", [4, 128, 128]); b_s = din("b_s", [4, 128])
    lam_re = din("lam_re", [32, 64]); lam_im = din("lam_im", [32, 64]); log_dt = din("log_dt", [32])
    b_re = din("b_re", [32, 64, 16]); b_im = din("b_im", [32, 64, 16])
    c_re = din("c_re", [512, 64]); c_im = din("c_im", [512, 64]); d_skip = din("d_skip", [512])
    w_glu = din("w_glu", [512, 512]); b_glu = din("b_glu", [512])
    g_out_a = din("g_out_a", [512]); g_out_b = din("g_out_b", [512])
    w_out = din("w_out", [D, D]); g_ffn = din("g_ffn", [D]); w_ffn_in = din("w_ffn_in", [D, 2 * DFF])
    conv_w = din("conv_w", [3, 2 * DFF]); conv_b = din("conv_b", [2 * DFF])
    w_ffn_out = din("w_ffn_out", [DFF, D]); g_final = din("g_final", [D])
    c_ident = din("c_ident", [128, 128]); c_mtoep = din("c_mtoep", [128, 128]); c_msgu = din("c_msgu", [128, 128])
    c_nvec = din("c_nvec", [128, 32]); c_sigma = din("c_sigma", [128, 1]); c_kvec = din("c_kvec", [128, 64])

    yp = dout("yp", [2048, D]); ys = dout("ys", [64, D]); vs = dout("vs", [64, 512])
    srep = dout("srep", [32, 64]); simp = dout("simp", [32, 64]); cvp = dout("cvp", [2, 2 * DFF])
    sres = dout("sres", [16, 32, 64]); sims = dout("sims", [16, 32, 64]); cvs = dout("cvs", [32, 2 * DFF])
    h1d = nc.dram_tensor("h1d", [2112, D], F32).ap()
    zscr = nc.dram_tensor("zscr", [2112, 512], BF16).ap()
    b_zscr = Buf()

    ES = contextlib.ExitStack()
    with ES:
        def sb(name, shape, dt=F32, stack=ES):
            return stack.enter_context(nc.sbuf_tensor(name, list(shape), dt))

        banks = [ES.enter_context(nc.psum_tensor("bank%d" % i, [128, 512], F32)) for i in range(8)]
        bbufs = [Buf("bank%d" % i) for i in range(8)]
        bctr = [0]

        def nb():
            i = bctr[0] % 8
            bctr[0] += 1
            return banks[i], bbufs[i]

        def mm(out, lhsT, rhs, start, stop, reads, writes):
            P.op("pe", lambda e: e.matmul(out, lhsT, rhs, start=start, stop=stop), reads, writes)

        def tr(out, in_, ident, reads, writes):
            P.op("pe", lambda e: e.transpose(out, in_, ident), reads, writes)

        def act(out, in_, func, reads, writes, **kw):
            P.op("act", lambda e: e.activation(out=out, in_=in_, func=func, **kw), reads, writes)

        def tt(eng, out, in0, in1, op, reads, writes):
            P.op(eng, lambda e: e.tensor_tensor(out=out, in0=in0, in1=in1, op=op), reads, writes)

        def ts(eng, out, in0, s1, s2, op0, op1, reads, writes):
            if s2 is None:
                P.op(eng, lambda e: e.tensor_scalar(out=out, in0=in0, scalar1=s1, scalar2=None, op0=op0), reads, writes)
            else:
                P.op(eng, lambda e: e.tensor_scalar(out=out, in0=in0, scalar1=s1, scalar2=s2, op0=op0, op1=op1), reads, writes)

        def stt(out, in0, scalar, in1, op0, op1, reads, writes):
            P.op("dve", lambda e: e.scalar_tensor_tensor(out=out, in0=in0, scalar=scalar, in1=in1, op0=op0, op1=op1), reads, writes)

        def cp(eng, out, in_, reads, writes):
            if eng == "act":
                P.op(eng, lambda e: e.activation(out=out, in_=in_, func=AF.Copy), reads, writes)
            else:
                P.op(eng, lambda e: e.tensor_copy(out=out, in_=in_), reads, writes)

        def dma(q, out, in_, reads, writes, is_output=False, **kw):
            P.dma(q, lambda e: e.dma_start(out=out, in_=in_, **kw), reads, writes, is_output=is_output)

        identf = sb("identf", [128, 128]); b_identf = Buf()
        identb = sb("identb", [128, 128], BF16); b_identb = Buf()
        onesb = sb("onesb", [128, 128], BF16); b_ones = Buf()
        epst = sb("epst", [128, 1]); b_eps = Buf()
        mhalf = sb("mhalf", [128, 512]); b_mhalf = Buf()
        sigma = sb("sigma", [128, 1]); b_sigma = Buf()
        ARR8 = sb("ARR8", [128, 2, 32]); AII8 = sb("AII8", [128, 2, 32]); b_A8 = Buf()
        AR4 = sb("AR4", [128, 32]); AI4 = sb("AI4", [128, 32]); b_A4 = Buf()
        ST = [sb("ST%d" % i, [128, 3, 32]) for i in range(2)]
        b_ST = [Buf(), Buf()]
        SM = contextlib.ExitStack()
        SM.__enter__()
        Toep = sb("Toep", [128, 32, 128], BF16, SM); b_Toep = Buf()
        MinX = sb("MinX", [128, 32, 192], BF16, SM); b_MinX = Buf()
        Mout = sb("Mout", [128, 32, 128], BF16, SM); b_Mout = Buf()
        w_in_sb = sb("w_in_sb", [128, 8, 1536], BF16, SM); b_win = Buf()
        w_glu_sb = sb("w_glu_sb", [128, 4, 512], BF16, SM); b_wglu = Buf()
        w_out_sb = sb("w_out_sb", [128, 8, 1024], BF16, SM); b_wout = Buf()
        TC = sb("TC", [128, 32, 64], F32, SM); TSs = sb("TSs", [128, 32, 64], F32, SM); b_tab = Buf()
        MG8 = sb("MG8", [128, 32], F32, SM); b_mg8 = Buf()
        C1 = sb("C1", [128, 32], F32, SM); C2 = sb("C2", [128, 32], F32, SM); b_C = Buf()
        th8c = sb("th8c", [128, 32], F32, SM); b_th8c = Buf()
        wiv = w_in.rearrange("(a p) n -> p a n", p=128)
        n_pref0 = len(P.all_ops)
        for a in range(8):
            dma("pool", w_in_sb[:, a, 0:1024], wiv[:, a, 0:1024], [], [b_win])
        dma("pool", w_glu_sb[:], w_glu.rearrange("(a p) n -> p a n", p=128), [], [b_wglu])
        wov = w_out.rearrange("(a p) n -> p a n", p=128)
        P.op("pool", lambda e: e.memset(C1[:], 0.0), [], [b_C])
        P.op("pool", lambda e: e.memset(C2[:], 0.0), [], [b_C])

        dma("sp", identf[:], c_ident, [], [b_identf])
        dma("sp", sigma[:], c_sigma, [], [b_sigma])
        cp("dve", identb[:], identf[:], [b_identf], [b_identb])
        P.op("pool", lambda e: e.memset(onesb[:], 1.0), [], [b_ones])
        P.op("pool", lambda e: e.memset(epst[:], EPS), [], [b_eps])
        P.op("pool", lambda e: e.memset(mhalf[:], -0.5), [], [b_mhalf])
        P.op("pool", lambda e: e.memset(ST[0][:], 0.0), [], [b_ST[0]])

        def rstd(out, ssum, n, pt, width, reads, writes, tmp, b_tmp):
            if F_POW:
                ts("dve", tmp, ssum, 1.0 / n, EPS, ALU.mult, ALU.add, reads, [b_tmp])
                tt("pool", out, tmp, mhalf[0:pt, 0:width], ALU.pow, [b_tmp, b_mhalf], writes)
                return
            act(tmp, ssum, AF.Sqrt, list(reads) + [b_eps], [b_tmp], scale=1.0 / n, bias=epst[0:pt, :])
            P.op("dve", lambda e: e.reciprocal(out, tmp), [b_tmp], writes)

        with contextlib.ExitStack() as S0:
            def sb0(name, shape, dt=F32):
                return sb(name, shape, dt, S0)

            wxs = sb0("wxs", [128, 8, 512], BF16); b_wxs = Buf()
            dma("pool", wxs[:], wiv[:, :, 1024:1536], [], [b_wxs])
            for a in range(8):
                dma("pool", w_out_sb[:, a, :], wov[:, a, :], [], [b_wout])
            n_pref1 = len(P.all_ops)
            cp("act", w_in_sb[:, :, 1024:1536].rearrange("p a (c g) -> p a c g", g=32),
               wxs[:].rearrange("p a (g c) -> p a c g", c=16), [b_wxs], [b_win])
            mtoep = sb0("mtoep", [128, 128]); b_mtoep = Buf()
            nvec = sb0("nvec", [128, 32]); b_nvec = Buf()
            dma("sp", mtoep[:], c_mtoep, [], [b_mtoep])
            dma("sp", nvec[:], c_nvec, [], [b_nvec])
            L2 = sb0("L2", [32, 256]); b_L2 = Buf()
            dma("sp", L2[:, 0:64], lam_re, [], [b_L2]); dma("sp", L2[:, 64:128], lam_re, [], [b_L2])
            dma("sp", L2[:, 128:192], lam_im, [], [b_L2]); dma("sp", L2[:, 192:256], lam_im, [], [b_L2])
            lr = sb0("lr", [128, 32]); li = sb0("li", [128, 32]); b_l = Buf()
            bk, bb = nb()
            tr(bk[:, 0:32], L2[:, 0:128], identf[0:32, 0:32], [b_L2, b_identf], [bb])
            tr(bk[:, 32:64], L2[:, 128:256], identf[0:32, 0:32], [b_L2, b_identf], [bb])
            cp("dve", lr[:], bk[:, 0:32], [bb], [b_l]); cp("dve", li[:], bk[:, 32:64], [bb], [b_l])
            dtb = sb0("dtb", [128, 32]); b_dt = Buf()
            dma("sp", dtb[:], log_dt.partition_broadcast(128), [], [b_dt])
            act(dtb[:], dtb[:], AF.Exp, [b_dt], [b_dt])
            rho = sb0("rho", [128, 32]); th = sb0("th", [128, 32]); b_rt = Buf()
            tt("dve", rho[:], lr[:], dtb[:], ALU.mult, [b_l, b_dt], [b_rt])
            tt("dve", th[:], li[:], dtb[:], ALU.mult, [b_l, b_dt], [b_rt])
            NQ = 32
            shp = [128, 32, NQ]
            ARG = sb0("ARG", shp); RHO = sb0("RHO", shp); b_arg = Buf()
            nv_b = nvec[:].unsqueeze(1).broadcast_to(shp)
            tt("dve", ARG[:], th[:].unsqueeze(2).broadcast_to(shp), nv_b, ALU.mult, [b_rt, b_nvec], [b_arg])
            tt("dve", RHO[:], rho[:].unsqueeze(2).broadcast_to(shp), nv_b, ALU.mult, [b_rt, b_nvec], [b_arg])
            mag = RHO; b_mag = Buf()
            act(mag[:], RHO[:], AF.Exp, [b_arg], [b_mag, b_arg])
            ki = sb0("ki", shp, I32); kf = sb0("kf", shp); rr = sb0("rr", shp); mk = sb0("mk", shp); b_red = Buf()
            sinv = sb0("sinv", shp); cosv = sb0("cosv", shp); b_sc = Buf()

            def sin_of(outt, shift):
                ts("dve", rr[:], ARG[:], shift, None, ALU.add, None, [b_arg], [b_red])
                ts("dve", kf[:], rr[:], 1.0 / TWO_PI, None, ALU.mult, None, [b_red], [b_red])
                cp("dve", ki[:], kf[:], [b_red], [b_red])
                cp("dve", kf[:], ki[:], [b_red], [b_red])
                stt(rr[:], kf[:], -TWO_PI, rr[:], ALU.mult, ALU.add, [b_red], [b_red])
                ts("dve", mk[:], rr[:], math.pi, None, ALU.is_gt, None, [b_red], [b_red])
                stt(rr[:], mk[:], -TWO_PI, rr[:], ALU.mult, ALU.add, [b_red], [b_red])
                ts("dve", mk[:], rr[:], -math.pi, None, ALU.is_lt, None, [b_red], [b_red])
                stt(rr[:], mk[:], TWO_PI, rr[:], ALU.mult, ALU.add, [b_red], [b_red])
                ts("dve", rr[:], rr[:], math.pi, -math.pi, ALU.min, ALU.max, [b_red], [b_red])
                act(outt[:], rr[:], AF.Sin, [b_red], [b_sc])

            sin_of(sinv, 0.0)
            sin_of(cosv, math.pi / 2)
            PA = cosv; PB = sinv; b_P = b_sc
            tt("dve", PA[:], mag[:], cosv[:], ALU.mult, [b_mag, b_sc], [b_P])
            tt("dve", PB[:], mag[:], sinv[:], ALU.mult, [b_mag, b_sc], [b_P])
            cp("dve", MG8[:], mag[:, :, 25], [b_mag], [b_mg8])
            den = sb0("den", [128, 32]); t0 = sb0("t0", [128, 32]); t1 = sb0("t1", [128, 32])
            fr = sb0("fr", [128, 32]); fi = sb0("fi", [128, 32]); am1 = sb0("am1", [128, 32]); b_f = Buf()
            tt("dve", den[:], lr[:], lr[:], ALU.mult, [b_l], [b_f])
            tt("dve", t0[:], li[:], li[:], ALU.mult, [b_l], [b_f])
            tt("dve", den[:], den[:], t0[:], ALU.add, [b_f], [b_f])
            P.op("dve", lambda e: e.reciprocal(den[:], den[:]), [b_f], [b_f])
            ts("dve", am1[:], PA[:, :, 27], -1.0, None, ALU.add, None, [b_P], [b_f])
            tt("dve", t0[:], am1[:], lr[:], ALU.mult, [b_f, b_l], [b_f])
            tt("dve", t1[:], PB[:, :, 27], li[:], ALU.mult, [b_P, b_l], [b_f])
            tt("dve", t0[:], t0[:], t1[:], ALU.add, [b_f], [b_f])
            tt("dve", fr[:], t0[:], den[:], ALU.mult, [b_f], [b_f])
            tt("dve", t0[:], PB[:, :, 27], lr[:], ALU.mult, [b_P, b_l], [b_f])
            tt("dve", t1[:], am1[:], li[:], ALU.mult, [b_f, b_l], [b_f])
            tt("dve", t0[:], t0[:], t1[:], ALU.subtract, [b_f], [b_f])
            tt("dve", fi[:], t0[:], den[:], ALU.mult, [b_f], [b_f])
            shp16 = [128, 32, 16]
            FRn = sb0("FRn", shp16); FIs = sb0("FIs", shp16); tq = sb0("tq", shp16); b_FR = Buf()
            frb = fr[:].unsqueeze(2).broadcast_to(shp16); fib = fi[:].unsqueeze(2).broadcast_to(shp16)
            tt("dve", FRn[:], PA[:, :, 0:16], frb, ALU.mult, [b_P, b_f], [b_FR])
            tt("dve", tq[:], PB[:, :, 0:16], fib, ALU.mult, [b_P, b_f], [b_FR])
            tt("dve", FRn[:], FRn[:], tq[:], ALU.subtract, [b_FR], [b_FR])
            tt("dve", FIs[:], PA[:, :, 0:16], fib, ALU.mult, [b_P, b_f], [b_FR])
            tt("dve", tq[:], PB[:, :, 0:16], frb, ALU.mult, [b_P, b_f], [b_FR])
            tt("dve", FIs[:], FIs[:], tq[:], ALU.add, [b_FR], [b_FR])
            ts("dve", FIs[:], FIs[:], sigma[:, 0:1], None, ALU.mult, None, [b_FR, b_sigma], [b_FR])
            BT1 = sb0("BT1", [128, 32, 16]); BT2 = sb0("BT2", [128, 32, 16]); b_BT = Buf()
            brv = b_re.rearrange("g p c -> p g c"); biv = b_im.rearrange("g p c -> p g c")
            dma("sp", BT1[0:64], brv, [], [b_BT]); dma("sp", BT1[64:128], biv, [], [b_BT])
            dma("sp", BT2[0:64], biv, [], [b_BT]); dma("sp", BT2[64:128], brv, [], [b_BT])
            shp4 = [128, 32, 8, 16]
            BBneg = sb0("BBneg", shp4); MinT = BBneg; b_BB = Buf()
            bt1b = BT1[:].unsqueeze(2).broadcast_to(shp4); bt2b = BT2[:].unsqueeze(2).broadcast_to(shp4)

            def build_bb(dst, q0, tbuf, b_tbuf):
                tt("dve", dst[:], FRn[:, :, q0:q0 + 8].unsqueeze(3).broadcast_to(shp4), bt1b, ALU.mult, [b_FR, b_BT], [b_BB])
                tt("dve", tbuf, FIs[:, :, q0:q0 + 8].unsqueeze(3).broadcast_to(shp4), bt2b, ALU.mult, [b_FR, b_BT], [b_tbuf])
                tt("dve", dst[:], dst[:], tbuf, ALU.subtract, [b_BB, b_tbuf], [b_BB])
            CT1 = sb0("CT1", [128, 32, 16]); CT2 = sb0("CT2", [128, 32, 16]); b_CT = Buf()
            CL = sb0("CL", [128, 4, 256]); b_CL = Buf()
            crv = c_re.rearrange("(r p) k -> p r k", p=128); civ = c_im.rearrange("(r p) k -> p r k", p=128)
            dma("sp", CL[:, :, 0:64], crv, [], [b_CL]); dma("sp", CL[:, :, 64:128], civ, [], [b_CL])
            dma("sp", CL[:, :, 128:192], civ, [], [b_CL]); dma("sp", CL[:, :, 192:256], crv, [], [b_CL])
            for half, dst in ((0, CT1), (1, CT2)):
                bk, bb = nb()
                for r in range(4):
                    tr(bk[:, r * 128:(r + 1) * 128], CL[:, r, half * 128:(half + 1) * 128], identf[:], [b_CL, b_identf], [bb])
                cp("dve", dst[:].rearrange("p g c -> p (g c)"), bk[:, 0:512], [bb], [b_CT])
            shp9 = [128, 32, 9, 16]
            EE = sb0("EE", shp9); te = sb0("te", shp9); PAs = sb0("PAs", [128, 32, 9]); b_EE = Buf(); b_te = Buf()
            ts("dve", PAs[:], PA[:, :, 16:25], sigma[:, 0:1], None, ALU.mult, None, [b_P, b_sigma], [b_EE])
            tt("dve", EE[:], PAs[:].unsqueeze(3).broadcast_to(shp9), CT1[:].unsqueeze(2).broadcast_to(shp9), ALU.mult, [b_EE, b_CT], [b_EE])
            tt("dve", te[:], PB[:, :, 16:25].unsqueeze(3).broadcast_to(shp9), CT2[:].unsqueeze(2).broadcast_to(shp9), ALU.mult, [b_P, b_CT], [b_te])
            tt("dve", EE[:], EE[:], te[:], ALU.subtract, [b_EE, b_te], [b_EE])
            build_bb(BBneg, 0, te[:, :, 0:8, :], b_te)
            cp("dve", Mout[:].rearrange("p g (j c) -> p g j c", c=16), EE[:, :, 1:9, :], [b_EE], [b_Mout])
            Drep = sb0("Drep", [128, 32]); b_Drep = Buf()
            dsv = d_skip.rearrange("(g c) -> c g", c=16)
            for i in range(8):
                dma("sp", Drep[16 * i:16 * i + 16, :], dsv, [], [b_Drep], allow_slow_non_contiguous=True)
            tmpT = te[:, 0:4, 0:8, :].rearrange("p g n c -> p g (n c)"); b_tmpT = b_te
            for gq in range(8):
                bk, bb = nb()
                for gl in range(4):
                    g = gq * 4 + gl
                    mm(bk[:, gl * 128:(gl + 1) * 128], BBneg[:, g].rearrange("p i c -> p (i c)"),
                       EE[:, g, 0:8, :].rearrange("p n c -> p (n c)"), True, True, [b_BB, b_EE], [bb])
                tt("dve", tmpT, bk[:, 0:512].rearrange("p (g c) -> p g c", g=4),
                   mtoep[:].unsqueeze(1).broadcast_to([128, 4, 128]), ALU.mult, [bb, b_mtoep], [b_tmpT])
                for gl in range(4):
                    g = gq * 4 + gl
                    stt(Toep[:, g, :], identf[:], Drep[:, g:g + 1], tmpT[:, gl, :], ALU.mult, ALU.add,
                        [b_identf, b_Drep, b_tmpT], [b_Toep])
            build_bb(MinT, 8, te[:, :, 0:8, :], b_te)
            for gq in range(8):
                bk, bb = nb()
                for gl in range(4):
                    g = gq * 4 + gl
                    tr(bk[:, gl * 128:(gl + 1) * 128], MinT[:, g].rearrange("p i c -> p (i c)"), identf[:], [b_BB, b_identf], [bb])
                bv = bk[:, 0:512].rearrange("p (g c) -> p g c", g=4)
                cp("dve", MinX[:, gq * 4:gq * 4 + 4, 0:128], bv, [bb], [b_MinX])
                cp("act", MinX[:, gq * 4:gq * 4 + 4, 128:192], bv[:, :, 0:64], [bb], [b_MinX])
            cp("dve", ARR8[:, 0, :], PA[:, :, 25], [b_P], [b_A8]); cp("dve", ARR8[:, 1, :], PA[:, :, 25], [b_P], [b_A8])
            ts("dve", AII8[:, 1, :], PB[:, :, 25], sigma[:, 0:1], None, ALU.mult, None, [b_P, b_sigma], [b_A8])
            ts("dve", AII8[:, 0, :], AII8[:, 1, :], -1.0, None, ALU.mult, None, [b_A8], [b_A8])
            cp("dve", AR4[:], PA[:, :, 26], [b_P], [b_A4])
            ts("dve", AI4[:], PB[:, :, 26], sigma[:, 0:1], -1.0, ALU.mult, ALU.mult, [b_P, b_sigma], [b_A4])
            ts("dve", th8c[:], th[:], 8.0, None, ALU.mult, None, [b_rt], [b_th8c])
            P.barrier(exclude=set(id(o) for o in P.all_ops[n_pref0:n_pref1]))

        with contextlib.ExitStack() as S1:
            shpk = [128, 32, 64]
            kvec = sb("kvec", [128, 64], F32, S1); b_kvec = Buf()
            dma("sp", kvec[:], c_kvec, [], [b_kvec])
            th8r = sb("th8r", [128, 32], F32, S1); b_th8r = Buf()
            kA = sb("kA", shpk, F32, S1); kR = sb("kR", shpk, F32, S1); kF = sb("kF", shpk, F32, S1)
            kI = sb("kI", shpk, I32, S1); kM = sb("kM", shpk, F32, S1); b_k = Buf()

            def reduce_pi(rr_, src, shift, kf_, ki_, mk_, rd, wr):
                ts("dve", rr_, src, shift, None, ALU.add, None, rd, wr)
                ts("dve", kf_, rr_, 1.0 / TWO_PI, None, ALU.mult, None, wr, wr)
                cp("dve", ki_, kf_, wr, wr)
                cp("dve", kf_, ki_, wr, wr)
                stt(rr_, kf_, -TWO_PI, rr_, ALU.mult, ALU.add, wr, wr)
                ts("dve", mk_, rr_, math.pi, None, ALU.is_gt, None, wr, wr)
                stt(rr_, mk_, -TWO_PI, rr_, ALU.mult, ALU.add, wr, wr)
                ts("dve", mk_, rr_, -math.pi, None, ALU.is_lt, None, wr, wr)
                stt(rr_, mk_, TWO_PI, rr_, ALU.mult, ALU.add, wr, wr)
                ts("dve", rr_, rr_, math.pi, -math.pi, ALU.min, ALU.max, wr, wr)

            reduce_pi(th8r[:], th8c[:], 0.0, kF[:, :, 0], kI[:, :, 0], kM[:, :, 0], [b_th8c], [b_th8r, b_k])
            tt("dve", kA[:], th8r[:].unsqueeze(2).broadcast_to(shpk), kvec[:].unsqueeze(1).broadcast_to(shpk), ALU.mult,
               [b_th8r, b_kvec], [b_k])
            reduce_pi(kR[:], kA[:], 0.0, kF[:], kI[:], kM[:], [b_k], [b_k])
            act(TSs[:], kR[:], AF.Sin, [b_k], [b_tab])
            ts("dve", TSs[:], TSs[:], sigma[:, 0:1], None, ALU.mult, None, [b_tab, b_sigma], [b_tab])
            reduce_pi(kR[:], kA[:], math.pi / 2, kF[:], kI[:], kM[:], [b_k], [b_k])
            act(TC[:], kR[:], AF.Sin, [b_k], [b_tab])
            P.barrier(exclude=set(id(o) for o in P.all_ops[n_pref0:n_pref1]))

        with contextlib.ExitStack() as SA:
            def sba(name, shape, dt=F32):
                return sb(name, shape, dt, SA)

            gmix_bc = sba("gmix_bc", [128, 1024]); b_gmix = Buf()
            dma("sp", gmix_bc[:], g_mix.partition_broadcast(128), [], [b_gmix])
            gv_bc = sba("gv_bc", [128, 512]); b_gv = Buf()
            dma("sp", gv_bc[:], g_v.partition_broadcast(128), [], [b_gv])
            goa_bc = sba("goa_bc", [128, 512]); b_goa = Buf()
            dma("sp", goa_bc[:], g_out_a.partition_broadcast(128), [], [b_goa])
            gob = sba("gob", [128, 4]); bglu = sba("bglu", [128, 4]); b_col = Buf()
            bsT = sba("bsT", [128, 4]); bsT_s = sba("bsT_s", [64, 4])
            def sa_late():
                dma("sp", gob[:], g_out_b.rearrange("(m p) -> p m", p=128), [], [b_col], allow_slow_non_contiguous=True)
                dma("sp", bglu[:], b_glu.rearrange("(m p) -> p m", p=128), [], [b_col], allow_slow_non_contiguous=True)
                dma("sp", bsT[:], b_s.rearrange("h t -> t h"), [], [b_col], allow_slow_non_contiguous=True)
                for b in range(16):
                    dma("sp", bsT_s[4 * b:4 * b + 4, :], b_s[:, 0:4].rearrange("h t -> t h"), [], [b_col], allow_slow_non_contiguous=True)
                for m_ in range(4):
                    ts("dve", w_out_sb[:, 4 + m_, :], w_out_sb[:, 4 + m_, :], gob[:, m_:m_ + 1], None, ALU.mult, None, [b_wout, b_col], [b_wout])

            msgu = sba("msgu", [128, 128]); b_msgu = Buf()
            dma("sp", msgu[:], c_msgu, [], [b_msgu])
            Msg = sba("Msg", [128, 4, 128], BF16); Msg_s = sba("Msg_s", [64, 4, 64], BF16); b_Msg = Buf()
            xa = [sba("xa%d" % i, [128, 1024]) for i in range(2)]; b_xa = [Buf(), Buf()]
            xh = [sba("xh%d" % i, [128, 1024]) for i in range(2)]; b_xh = [Buf(), Buf()]
            b_ss_s = [Buf() for _ in range(4)]; b_rs_s = [Buf() for _ in range(4)]; b_tm_s = [Buf() for _ in range(4)]
            b_xs_s = [Buf() for _ in range(4)]
            cnt_a = {"xh": 0}
            ss4 = sba("ss4", [128, 4]); rs4 = sba("rs4", [128, 4]); tm4 = sba("tm4", [128, 4]); b_ss = Buf(); b_rs = Buf(); b_tm4 = Buf()
            xn = [sba("xn%d" % i, [128, 1024], BF16) for i in range(2)]; b_xn = [Buf(), Buf()]
            n1T = sba("n1T", [128, 8, 512], BF16); b_n1T = Buf(); b_n1T_list = [Buf() for _ in range(4)]
            u_tm = [sba("u_tm%d" % i, [128, 512], BF16) for i in range(2)]; b_u = [Buf(), Buf()]
            vsq = sba("vsq", [128, 512], BF16); b_vsq = Buf()
            ssv = sba("ssv", [128, 4]); rv = sba("rv", [128, 4]); tmv = sba("tmv", [128, 4]); b_ssv = Buf(); b_rv = Buf(); b_tmv = Buf()
            vtmp = sba("vtmp", [128, 512]); b_vtmp = Buf()
            vnf_alias = True
            vn = [sba("vn%d" % i, [128, 512], BF16) for i in range(2)]; b_vn = [Buf(), Buf()]
            xs_tm = sba("xs_tm", [128, 4, 512], BF16); b_xstm = Buf()
            a_tms = [sba("a_tm%d" % i, [128, 512]) for i in range(2)]; b_atms = [Buf(), Buf()]
            vnf = a_tms[1]; b_vnf = b_atms[1]
            ssa = sba("ssa", [128, 1]); ra = sba("ra", [128, 1]); tma = sba("tma", [128, 1]); b_ssa = Buf(); b_ra = Buf(); b_tma = Buf()
            mixas = [sba("mixa%d" % i, [128, 512], BF16) for i in range(2)]; b_mixas = [Buf(), Buf()]
            mixTa = [sba("mixTa%d" % i, [128, 4, 512], BF16) for i in range(2)]; b_mixTa = [Buf(), Buf()]
            mixTb = sba("mixTb", [128, 4, 512], BF16); b_mixTb = Buf()
            Z = sba("Z", [64, 8, 512], BF16); b_Zs = [Buf() for _ in range(4)]
            X2 = sba("X2", [128, 32, 64], BF16); b_X2 = Buf()
            R1 = sba("R1", [128, 32, 64]); R2 = sba("R2", [128, 32, 64])
            b_R1 = [Buf() for _ in range(32)]; b_R2 = [Buf() for _ in range(32)]
            mA = sba("mA", [128, 8, 64]); mB = sba("mB", [128, 8, 64]); b_mA = Buf(); b_mB = Buf()
            cA = sba("cA", [128, 32]); cB = sba("cB", [128, 32]); b_cA = Buf(); b_cB = Buf()
            HPb = sba("HPb", [128, 32, 64], BF16); b_HPb = Buf()
            gTb = sba("gTb", [128, 4, 512], BF16); b_gTb = Buf()
            junk = gTb[:].rearrange("p a t -> p (a t)")[:, 0:1024]; b_junk = b_gTb
            sg = sba("sg", [128, 512]); b_sg = Buf()
            R1f = R1[:].rearrange("p g k -> p (g k)"); R2f = R2[:].rearrange("p g k -> p (g k)")
            allR1 = b_R1; allR2 = b_R2
            sqb = R1f.bitcast(BF16)[:, 0:2048].rearrange("p (a t) -> p a t", a=4); b_sqb = b_R1[0]
            rb_bc = R2f[:, 0:512]; tmb = R2f[:, 512:1024]; b_rb = b_R2[0]; b_tmb = b_R2[8]
            stS = sba("stS", [32, 128]); b_stS = Buf()
            rb4 = sba("rb4", [128, 4]); tb4 = sba("tb4", [128, 4]); b_rb4 = Buf(); b_tb4 = Buf()
            Hins = [R2f[0:16, 1024:2048].rearrange("p (g c) -> p g c", g=4), R1f[0:16, 1024:2048].rearrange("p (g c) -> p g c", g=4)]
            b_Hins = [b_R2[16], b_R1[16]]
            r3 = lambda t_: t_.rearrange("p (g b) -> p g b", b=16)
            V10 = r3(rb_bc); b_V10 = b_rb
            V20 = r3(R1f[:, 512:1024]); b_V20 = b_R1[8]
            Vend = r3(tmb); b_Vend = b_tmb
            Vt = r3(sg[:]); b_Vt = b_sg
            Houts = [vtmp[0:16, :].rearrange("p (g c) -> p g c", g=4), mA[0:16].rearrange("p a k -> p (a k)").rearrange("p (g c) -> p g c", g=4)]
            b_Houts = [b_vtmp, b_mA]

            wsl = R1f[:, 0:512].rearrange("p (h s) -> p h s", h=4)
            wblk = R1f[0:64, 512:768].rearrange("p (h s) -> p h s", h=4)
            dma("sp", wsl, w_s.rearrange("h t s -> t h s"), [], b_R1[0:8])
            P.op("dve", lambda e: e.memset(wblk, 0.0), [], b_R1[8:12])
            for b in range(16):
                dma("act", wblk[4 * b:4 * b + 4, :, 4 * b:4 * b + 4], w_s[:, 0:4, 0:4].rearrange("h t s -> t h s"), b_R1[8:12], b_R1[8:12],
                    allow_slow_non_contiguous=True)
            bk, bb = nb()
            for h in range(4):
                tr(bk[:, h * 128:(h + 1) * 128], wsl[:, h, :], identf[:], b_R1[0:8] + [b_identf], [bb])
            tt("dve", Msg[:], bk[:, 0:512].rearrange("p (h t) -> p h t", h=4), msgu[:].unsqueeze(1).broadcast_to([128, 4, 128]),
               ALU.mult, [bb, b_msgu], [b_Msg])
            bk, bb = nb()
            for h in range(4):
                tr(bk[0:64, h * 64:(h + 1) * 64], wblk[:, h, :], identf[0:64, 0:64], b_R1[8:12] + [b_identf], [bb])
            tt("dve", Msg_s[:], bk[0:64, 0:256].rearrange("p (h t) -> p h t", h=4),
               msgu[0:64, 0:64].unsqueeze(1).broadcast_to([64, 4, 64]), ALU.mult, [bb, b_msgu], [b_Msg])

            st_cur = [0]

            def mixer_tile(tok0, src, ntok, sample, h1row, par):
                NS = 1 if sample else 4
                PT = 64 if sample else 128
                NC = 16 if sample else 64
                mixTa_t = mixTa[par]; b_mixTa_t = b_mixTa[par]
                shared = {}

                def front(part):
                    b_n1T_s = b_n1T_list

                    def a2(s):
                        xs_ = xa[s % 2]; bxs_ = b_xa[s % 2]
                        if sample:
                            dma("sp", xs_[0:64, :], src, [], [bxs_])
                        else:
                            dma("sp", xs_[:, :], src[tok0 + 128 * s:tok0 + 128 * (s + 1), :], [], [bxs_])
                        act(junk[0:PT, :], xs_[0:PT, :], AF.Square, [bxs_], [b_junk, b_ss_s[s]], accum_out=ss4[0:PT, s:s + 1])
                        rstd(rs4[0:PT, s:s + 1], ss4[0:PT, s:s + 1], 1024.0, PT, 1, [b_ss_s[s]], [b_rs_s[s]], tm4[0:PT, s:s + 1], b_tm_s[s])

                    def a4(s):
                        k2 = s % 2
                        stt(xn[k2][0:PT, :], xa[k2][0:PT, :], rs4[0:PT, s:s + 1], gmix_bc[0:PT, :], ALU.mult, ALU.mult,
                            [b_xa[k2], b_rs_s[s], b_gmix], [b_xn[k2]])
                        bk, bb = nb()
                        bkb = bk[:].bitcast(BF16)
                        for dti in range(8):
                            tr(bkb[:, dti * PT:(dti + 1) * PT], xn[k2][0:PT, dti * 128:(dti + 1) * 128], identb[0:PT, 0:PT],
                               [b_xn[k2], b_identb], [bb])
                        cp("act", n1T[:, :, s * 128:s * 128 + PT], bkb[:, 0:8 * PT].rearrange("p (a t) -> p a t", a=8), [bb], [b_n1T_s[s]])

                    if part == 0:
                        a2(0)
                        for s in range(NS):
                            if s + 1 < NS:
                                a2(s + 1)
                            a4(s)
                        return

                    def stage1(s):
                        k2 = s % 2
                        tsl = slice(s * 128, s * 128 + PT)
                        bu, bbu = nb(); bv, bbv = nb(); bx, bbx = nb()
                        for (bkx, bbx_, c0) in ((bv, bbv, 512), (bu, bbu, 0), (bx, bbx, 1024)):
                            for dti in range(8):
                                mm(bkx[0:PT, :], n1T[:, dti, tsl], w_in_sb[:, dti, c0:c0 + 512], dti == 0, dti == 7,
                                   [b_n1T_s[s], b_win], [bbx_])
                        return (bv, bbv, bu, bbu, bx, bbx)

                    def stage1e(s, bu, bbu, bx, bbx):
                        k2 = s % 2
                        cp("act", u_tm[k2][0:PT, :], bu[0:PT, :], [bbu], [b_u[k2]])
                        cp("act", xs_tm[0:PT, s, :], bx[0:PT, :], [bbx], [b_xs_s[s]])
                        if sample:
                            dma("sp", zscr[2048:2112, :], xs_tm[0:64, 0, :], [b_xs_s[0]], [b_zscr])
                        else:
                            r0 = h1row + 128 * s
                            dma("sp", zscr[r0:r0 + 128, :], xs_tm[:, s, :], [b_xs_s[s]], [b_zscr])

                    def stage2(s, bv, bbv):
                        k2 = s % 2
                        tsl = slice(s * 128, s * 128 + PT)
                        act(vsq[0:PT, :], bv[0:PT, :], AF.Square, [bbv], [b_vsq])
                        P.op("dve", lambda e: e.reduce_sum(out=ssv[0:PT, :], in_=vsq[0:PT, :].rearrange("p (h d) -> p h d", h=4), axis=AX.X),
                             [b_vsq], [b_ssv])
                        rstd(rv[0:PT, :], ssv[0:PT, :], 128.0, PT, 4, [b_ssv], [b_rv], tmv[0:PT, :], b_tmv)
                        tt("dve", vtmp[0:PT, :].rearrange("p (h d) -> p h d", h=4), bv[0:PT, :].rearrange("p (h d) -> p h d", h=4),
                           rv[0:PT, :].unsqueeze(2).broadcast_to([PT, 4, 128]), ALU.mult, [bbv, b_rv], [b_vtmp])
                        if sample:
                            tt("dve", vnf[0:PT, :], vtmp[0:PT, :], gv_bc[0:PT, :], ALU.mult, [b_vtmp, b_gv], [b_vnf])
                            dma("sp", vs, vnf[0:PT, :], [b_vnf], [], is_output=True)
                            cp("dve", vn[k2][0:PT, :], vnf[0:PT, :], [b_vnf], [b_vn[k2]])
                        else:
                            tt("dve", vn[k2][0:PT, :], vtmp[0:PT, :], gv_bc[0:PT, :], ALU.mult, [b_vtmp, b_gv], [b_vn[k2]])
                        bs_, bbs = nb()
                        for h in range(4):
                            lhs = Msg_s[:, h, :] if sample else Msg[:, h, :]
                            mm(bs_[0:PT, h * 128:(h + 1) * 128], lhs, vn[k2][0:PT, h * 128:(h + 1) * 128], True, True,
                               [b_Msg, b_vn[k2]], [bbs])
                        bsb = bsT_s if sample else bsT
                        a_tm = a_tms[k2]; b_atm = b_atms[k2]
                        for h in range(4):
                            stt(a_tm[0:PT, h * 128:(h + 1) * 128], bs_[0:PT, h * 128:(h + 1) * 128], bsb[0:PT, h:h + 1],
                                u_tm[k2][0:PT, h * 128:(h + 1) * 128], ALU.add, ALU.mult, [bbs, b_col, b_u[k2]], [b_atm])

                    def stage2b(s):
                        k2 = s % 2
                        tsl = slice(s * 128, s * 128 + PT)
                        a_tm = a_tms[k2]; b_atm = b_atms[k2]; mixa = mixas[k2]; b_mixa = b_mixas[k2]
                        act(junk[0:PT, 0:512], a_tm[0:PT, :], AF.Square, [b_atm], [b_junk, b_ssa], accum_out=ssa[0:PT, :])
                        rstd(ra[0:PT, :], ssa[0:PT, :], 512.0, PT, 1, [b_ssa], [b_ra], tma[0:PT, :], b_tma)
                        stt(mixa[0:PT, :], a_tm[0:PT, :], ra[0:PT, 0:1], goa_bc[0:PT, :], ALU.mult, ALU.mult, [b_atm, b_ra, b_goa], [b_mixa])
                        bk, bb = nb()
                        bkb = bk[:].bitcast(BF16)
                        for m in range(4):
                            tr(bkb[:, m * PT:(m + 1) * PT], mixa[0:PT, m * 128:(m + 1) * 128], identb[0:PT, 0:PT], [b_mixa, b_identb], [bb])
                        cp("act", mixTa_t[:, :, tsl], bkb[:, 0:4 * PT].rearrange("p (a t) -> p a t", a=4), [bb], [b_mixTa_t])

                    pend = stage1(0)
                    stage1e(0, *pend[2:])
                    lag = None
                    for s in range(NS):
                        nxt = stage1(s + 1) if s + 1 < NS else None
                        stage2(s, pend[0], pend[1])
                        if lag is not None:
                            stage2b(lag)
                        if nxt is not None:
                            stage1e(s + 1, *nxt[2:])
                        lag = s
                        pend = nxt
                    stage2b(lag)

                def z_readback():
                    if sample:
                        zv = zscr[2048:2112, :].rearrange("(b t) c -> b t c", t=4)
                        dma("pool", Z[0:16, 0:4, :], zv, [b_zscr], [b_Zs[0]])
                        dma("pool", Z[0:16, 4:8, :], zv, [b_zscr], [b_Zs[0]])
                    else:
                        for s_ in range(4):
                            r0 = h1row + 128 * s_
                            dma("pool", Z[16 * s_:16 * s_ + 16, :, :], zscr[r0:r0 + 128, :].rearrange("(k i) c -> k i c", i=8), [b_zscr], [b_Zs[s_]])

                def ssm_front():
                    if sample:
                        bk1, bbk1 = nb(); bk2, bbk2 = nb()
                        for gq in range(8):
                            gs = slice(gq * 4, gq * 4 + 4)
                            Hin = Hins[gq % 2]; b_Hin = b_Hins[gq % 2]
                            dma("sp", Hin[:, :, 0:64], sre[:, gs, :], [], [b_Hin]); dma("sp", Hin[:, :, 64:128], sim[:, gs, :], [], [b_Hin])
                            dma("sp", Hin[:, :, 128:192], sim[:, gs, :], [], [b_Hin]); dma("sp", Hin[:, :, 192:256], sre[:, gs, :], [], [b_Hin])
                            for gl in range(4):
                                g = gq * 4 + gl
                                tr(bk1[:, g * 16:(g + 1) * 16], Hin[:, gl, 0:128], identf[0:16, 0:16], [b_Hin, b_identf], [bbk1])
                                tr(bk2[:, g * 16:(g + 1) * 16], Hin[:, gl, 128:256], identf[0:16, 0:16], [b_Hin, b_identf], [bbk2])
                        f2 = lambda v_: v_.rearrange("p g b -> p (g b)")
                        cp("act", f2(V10), bk1[:, 0:512], [bbk1], [b_V10])
                        cp("dve", f2(V20), bk2[:, 0:512], [bbk2], [b_V20])
                        cp("dve", HPb[:, :, 0:16], V10, [b_V10], [b_HPb])
                        shb = [128, 32, 16]
                        tt("dve", Vend, V10, AR4[:].unsqueeze(2).broadcast_to(shb), ALU.mult, [b_V10, b_A4], [b_Vend])
                        tt("dve", Vt, V20, AI4[:].unsqueeze(2).broadcast_to(shb), ALU.mult, [b_V20, b_A4], [b_Vt])
                        tt("dve", Vend, Vend, Vt, ALU.add, [b_Vend, b_Vt], [b_Vend])

                    for gh in range(2):
                        bk, bb = nb()
                        bkb = bk[:].bitcast(BF16)
                        for gl in range(16):
                            g = gh * 16 + gl
                            tr(bkb[:, gl * NC:(gl + 1) * NC], Z[:].rearrange("k i c -> k (i c)")[0:NC, g:4096:32], identb[0:NC, 0:NC],
                               b_Zs + [b_identb], [bb])
                        cp("act" if gh == 0 else "dve", X2[:, gh * 16:gh * 16 + 16, 0:NC],
                           bkb[:, 0:16 * NC].rearrange("p (g k) -> p g k", g=16), [bb], [b_X2])
                    if sample:
                        b1, bb1 = nb()
                        for g in range(32):
                            mm(b1[:, g * 16:(g + 1) * 16], MinX[64:128, g, 0:128], X2[64:128, g, 0:16], True, True, [b_MinX, b_X2], [bb1])
                    else:
                        GPB = 8
                        for gq in range(4):
                            gs = slice(gq * GPB, (gq + 1) * GPB)
                            b1, bb1 = nb(); b2, bb2 = nb()
                            for gl in range(GPB):
                                g = gq * GPB + gl
                                mm(b1[:, gl * 64:(gl + 1) * 64], MinX[:, g, 0:128], X2[:, g, 0:64], True, True, [b_MinX, b_X2], [bb1])
                                mm(b2[:, gl * 64:(gl + 1) * 64], MinX[:, g, 64:192], X2[:, g, 0:64], True, True, [b_MinX, b_X2], [bb2])
                            s1v = b1[:, 0:512].rearrange("p (g k) -> p g k", g=GPB)
                            s2v = b2[:, 0:512].rearrange("p (g k) -> p g k", g=GPB)
                            wr1 = b_R1[gq * GPB:(gq + 1) * GPB]; wr2 = b_R2[gq * GPB:(gq + 1) * GPB]
                            tt("dve", R1[:, gs, :], s1v, TC[:, gs, :], ALU.mult, [bb1, b_tab], wr1)
                            tt("dve", mA[:], s2v, TSs[:, gs, :], ALU.mult, [bb2, b_tab], [b_mA])
                            tt("pool", R1[:, gs, :], R1[:, gs, :], mA[:], ALU.add, wr1 + [b_mA], wr1)
                            tt("dve", R2[:, gs, :], s2v, TC[:, gs, :], ALU.mult, [bb2, b_tab], wr2)
                            tt("dve", mB[:], s1v, TSs[:, gs, :], ALU.mult, [bb1, b_tab], [b_mB])
                            tt("pool", R2[:, gs, :], R2[:, gs, :], mB[:], ALU.subtract, wr2 + [b_mB], wr2)
                    if sample:
                        shared["b1"] = (b1, bb1)

                def scan():
                    if sample:
                        f2 = lambda v_: v_.rearrange("p g b -> p (g b)")
                        b1, bb1 = shared["b1"]
                        tt("dve", f2(Vend), f2(Vend), b1[:, 0:512], ALU.add, [b_Vend, bb1], [b_Vend])
                        for gq in range(8):
                            bk, bb = nb()
                            for gl in range(4):
                                g = gq * 4 + gl
                                tr(bk[0:16, gl * 128:(gl + 1) * 128], Vend[:, g, :], identf[:], [b_Vend, b_identf], [bb])
                            Hout = Houts[gq % 2]; b_Hout = b_Houts[gq % 2]
                            cp("act", Hout, bk[0:16, 0:512].rearrange("p (g c) -> p g c", g=4), [bb], [b_Hout])
                            gs = slice(gq * 4, gq * 4 + 4)
                            dma("sp", sres[:, gs, :], Hout[:, :, 0:64], [b_Hout], [], is_output=True)
                            dma("sp", sims[:, gs, :], Hout[:, :, 64:128], [b_Hout], [], is_output=True)
                    else:
                        cp("act", HPb[:, :, 0], C1[:], [b_C], [b_HPb])
                        tt("dve", cA[:], MG8[:], C1[:], ALU.mult, [b_mg8, b_C], [b_cA])
                        tt("dve", R1[:, :, 0], R1[:, :, 0], cA[:], ALU.add, b_R1 + [b_cA], b_R1)
                        tt("dve", cB[:], MG8[:], C2[:], ALU.mult, [b_mg8, b_C], [b_cB])
                        tt("dve", R2[:, :, 0], R2[:, :, 0], cB[:], ALU.add, b_R2 + [b_cB], b_R2)
                        for g in range(32):
                            for (RR, bR) in ((R1, b_R1), (R2, b_R2)):
                                P.op("dve", lambda e, RR=RR, g=g: e.tensor_tensor_scan(
                                    out=RR[:, g, :], data0=MG8[:, g:g + 1].broadcast_to([128, 64]), data1=RR[:, g, :],
                                    initial=0.0, op0=ALU.mult, op1=ALU.add), [bR[g], b_mg8], [bR[g]])
                        for gq in range(4):
                            gs = slice(gq * 8, gq * 8 + 8)
                            wr1 = b_R1[gq * 8:gq * 8 + 8]; wr2 = b_R2[gq * 8:gq * 8 + 8]
                            tt("dve", mA[:], R1[:, gs, :], TC[:, gs, :], ALU.mult, wr1 + [b_tab], [b_mA])
                            tt("dve", mB[:], R2[:, gs, :], TSs[:, gs, :], ALU.mult, wr2 + [b_tab], [b_mB])
                            tt("dve", HPb[:, gs, 1:64], mA[:, :, 0:63], mB[:, :, 0:63], ALU.subtract, [b_mA, b_mB], [b_HPb])
                            tt("dve", C1[:, gs], mA[:, :, 63], mB[:, :, 63], ALU.subtract, [b_mA, b_mB], [b_C])
                            tt("dve", cA[:, gs], R2[:, gs, 63], TC[:, gs, 63], ALU.mult, wr2 + [b_tab], [b_cA])
                            tt("dve", cB[:, gs], R1[:, gs, 63], TSs[:, gs, 63], ALU.mult, wr1 + [b_tab], [b_cB])
                            tt("dve", C2[:, gs], cA[:, gs], cB[:, gs], ALU.add, [b_cA, b_cB], [b_C])
                def back(hook=None):
                    Zy = Z
                    for gq in range(8):
                        bk, bb = nb()
                        for gl in range(4):
                            g = gq * 4 + gl
                            if sample:
                                mm(bk[0:NC, gl * 128:(gl + 1) * 128], X2[0:64, g, 0:NC], Toep[0:64, g, :], True, False, [b_X2, b_Toep], [bb])
                            else:
                                mm(bk[0:NC, gl * 128:(gl + 1) * 128], X2[:, g, 0:NC], Toep[:, g, :], True, False, [b_X2, b_Toep], [bb])
                            mm(bk[0:NC, gl * 128:(gl + 1) * 128], HPb[:, g, 0:NC], Mout[:, g, :], False, True, [b_HPb, b_Mout], [bb])
                        cp("dve" if gq % 2 == 0 else "act", Zy[0:NC, :, 64 * gq:64 * gq + 64].rearrange("p j (g c) -> p g j c", g=4),
                           bk[0:NC, 0:512].rearrange("p (g j c) -> p g j c", g=4, j=8), [bb], b_Zs)
                    NJ = 4 if sample else 8
                    for cbp in range(2):
                        bk, bb = nb()
                        bkb = bk[:].bitcast(BF16)
                        for cl in range(2):
                            cb = cbp * 2 + cl
                            for j in range(NJ):
                                o0 = (cl * NJ + j) * NC
                                tr(bkb[:, o0:o0 + NC], Zy[0:NC, j, cb * 128:(cb + 1) * 128], identb[0:NC, 0:NC], b_Zs + [b_identb], [bb])
                        for cl in range(2):
                            cb = cbp * 2 + cl
                            o0 = cl * NJ * NC
                            act(gTb[:, cb, 0:ntok].rearrange("p (k j) -> p j k", j=NJ),
                                bkb[:, o0:o0 + NJ * NC].rearrange("p (j k) -> p j k", j=NJ), AF.Gelu_apprx_tanh, [bb], [b_gTb])
                    if hook is not None:
                        hook()
                    for m in range(4):
                        bk, bb = nb()
                        for kt in range(4):
                            mm(bk[:, 0:ntok], w_glu_sb[:, kt, m * 128:(m + 1) * 128], gTb[:, kt, 0:ntok], kt == 0, kt == 3, [b_wglu, b_gTb], [bb])
                        act(sg[:, 0:ntok], bk[:, 0:ntok], AF.Sigmoid, [bb, b_col], [b_sg], bias=bglu[:, m:m + 1])
                        tt("dve", mixTb[:, m, 0:ntok], gTb[:, m, 0:ntok], sg[:, 0:ntok], ALU.mult, [b_gTb, b_sg], [b_mixTb])
                    act(sqb[:, :, 0:ntok], mixTb[:, :, 0:ntok], AF.Square, [b_mixTb], [b_sqb, b_R1[8]])
                    bkR, bbR = nb()
                    for s in range(NS):
                        tsl = slice(s * 128, s * 128 + PT)
                        for m in range(4):
                            mm(bkR[0:PT, s:s + 1], sqb[:, m, tsl], onesb[:, 0:1], m == 0, m == 3, [b_ones, b_sqb, b_R1[8]], [bbR])
                    rstd(rb4[0:PT, 0:NS], bkR[0:PT, 0:NS], 512.0, PT, NS, [bbR], [b_rb4], tb4[0:PT, 0:NS], b_tb4)
                    for s in range(NS):
                        tsl = slice(s * 128, s * 128 + PT)
                        hi_ = cnt_a["xh"] % 2; cnt_a["xh"] += 1
                        xh_ = xh[hi_]; bxh_ = b_xh[hi_]
                        if sample:
                            dma("sp", xh_[0:64, :], src, [], [bxh_])
                        else:
                            dma("sp", xh_[:, :], src[tok0 + 128 * s:tok0 + 128 * (s + 1), :], [], [bxh_])
                        for nh in range(2):
                            bka, bba = nb(); bkb_, bbb = nb()
                            for kt in range(4):
                                mm(bka[0:PT, :], mixTa_t[:, kt, tsl], w_out_sb[:, kt, nh * 512:(nh + 1) * 512], kt == 0, kt == 3, [b_mixTa_t, b_wout], [bba])
                            for kt in range(4):
                                mm(bkb_[0:PT, :], mixTb[:, kt, tsl], w_out_sb[:, 4 + kt, nh * 512:(nh + 1) * 512], kt == 0, kt == 3, [b_mixTb, b_wout], [bbb])
                            xsl = xh_[0:PT, nh * 512:(nh + 1) * 512]
                            tt("dve", xsl, xsl, bka[0:PT, :], ALU.add, [bxh_, bba], [bxh_])
                            stt(xsl, bkb_[0:PT, :], rb4[0:PT, s:s + 1], xsl, ALU.mult, ALU.add, [bxh_, bbb, b_rb4], [bxh_])
                        if sample:
                            dma("sp", h1d[h1row:h1row + 64, :], xh_[0:64, :], [bxh_], [])
                        else:
                            dma("sp", h1d[h1row + 128 * s:h1row + 128 * (s + 1), :], xh_[:, :], [bxh_], [])

                return front, ssm_front, scan, back, z_readback

            tiles = [mixer_tile(512 * t, xp, 512, False, 512 * t, t % 2) for t in range(4)]
            tiles.append(mixer_tile(0, xs, 64, True, 2048, 0))
            tiles[0][0](0)
            sa_late()
            tiles[0][0](1)
            tiles[0][4]()
            tiles[0][1]()
            for t in range(5):
                if t + 1 < 5:
                    tiles[t + 1][0](0)
                tiles[t][2]()
                if t + 1 < 5:
                    tiles[t + 1][0](1)
                tiles[t][3](tiles[t + 1][4] if t + 1 < 5 else None)
                if t == 3:
                    bk, bb = nb()
                    tr(bk[0:32, 0:128], C1[:], identf[:], [b_C, b_identf], [bb])
                    cp("act", stS[:], bk[0:32, 0:128], [bb], [b_stS])
                    dma("sp", srep, stS[:, 0:64], [b_stS], [], is_output=True)
                    dma("sp", simp, stS[:, 64:128], [b_stS], [], is_output=True)
                if t + 1 < 5:
                    tiles[t + 1][1]()
            P.barrier()
        SM.close()

        with contextlib.ExitStack() as SB:
            def sbb(name, shape, dt=F32):
                return sb(name, shape, dt, SB)

            wfo = sbb("wfo", [128, 22, 1024], BF16); b_wfo = Buf()
            wfov = w_ffn_out.rearrange("(a p) n -> p a n", p=128)
            wfo_pending = list(range(22))
            gffn_bc = sbb("gffn_bc", [128, 1024]); gfin_bc = sbb("gfin_bc", [128, 1024]); b_gbc = Buf()
            dma("sp", gffn_bc[:], g_ffn.partition_broadcast(128), [], [b_gbc])
            dma("sp", gfin_bc[:], g_final.partition_broadcast(128), [], [b_gbc])
            cwl = sbb("cwl", [88, 2, 128]); b_cwl = Buf()
            cwv = conv_w.rearrange("k (c p) -> (k c) p", p=128)
            dma("sp", cwl[:, 0, :], cwv[0:88, :], [], [b_cwl])
            dma("sp", cwl[0:44, 1, :], cwv[88:132, :], [], [b_cwl])
            dma("sp", cwl[44:88, 1, :], conv_b.rearrange("(c p) -> c p", p=128), [], [b_cwl])
            cw = sbb("cw", [128, 4, 44]); b_cw = Buf()
            bk, bb = nb()
            tr(bk[:, 0:88], cwl[:, 0, :], identf[0:88, 0:88], [b_cwl, b_identf], [bb])
            tr(bk[:, 88:176], cwl[:, 1, :], identf[0:88, 0:88], [b_cwl, b_identf], [bb])
            cp("dve", cw[:].rearrange("p k c -> p (k c)"), bk[:, 0:176], [bb], [b_cw])
            hist = sbb("hist", [128, 2, 44]); b_hist0 = Buf(); b_hist = [Buf() for _ in range(44)]
            P.op("pool", lambda e: e.memset(hist[:], 0.0), [], [b_hist0] + b_hist)
            hist_s = sbb("hist_s", [128, 44, 32]); b_hists = Buf()
            scl = sbb("scl", [32, 1408]); b_scl = Buf()
            for q4 in range(4):
                dma("sp", scl[:], scv[:, q4 * 1408:(q4 + 1) * 1408], [], [b_scl])
                bk, bb = nb()
                for cl in range(11):
                    tr(bk[:, cl * 32:(cl + 1) * 32], scl[:, cl * 128:(cl + 1) * 128], identf[0:32, 0:32], [b_scl, b_identf], [bb])
                cp("act", hist_s[:, q4 * 11:(q4 + 1) * 11, :], bk[:, 0:352].rearrange("p (c r) -> p c r", c=11), [bb], [b_hists])
            cst_s = sbb("cst_s", [128, 44, 32]); b_csts = Buf()

            NTB = 1088
            hb = [sbb("hb%d" % i, [128, 1024]) for i in range(3)]; b_hb = [Buf(), Buf(), Buf()]
            xn = [sbb("xnB%d" % i, [128, 1024], BF16) for i in range(2)]; b_xn = [Buf(), Buf()]
            junk = sbb("junkB", [128, 1024], BF16); b_junk = Buf()
            ssB = [sbb("ssB%d" % i, [128, 1]) for i in range(2)]; rsB = [sbb("rsB%d" % i, [128, 1]) for i in range(2)]
            tmB = [sbb("tmB%d" % i, [128, 1]) for i in range(2)]
            b_ssB = [Buf(), Buf()]; b_rsB = [Buf(), Buf()]; b_tmB = [Buf(), Buf()]
            n2T = sbb("n2T", [128, 8, NTB], BF16); b_n2T = Buf()
            zt = sbb("zt", [128, 22, NTB], BF16); b_z = Buf()
            wst = [sbb("wst%d" % i, [128, 8, 256], BF16) for i in range(3)]; b_wst = [Buf(), Buf(), Buf()]
            upS = [[sbb("upS%d_%d" % (i, j), [128, 514]) for j in range(2)] for i in range(2)]
            b_upS = [[Buf(), Buf()], [Buf(), Buf()]]
            b_upH = [[Buf(), Buf()], [Buf(), Buf()]]
            hsave = [sbb("hsv%d" % i, [128, 2]) for i in range(3)]; b_hsave = [Buf() for _ in range(3)]
            acc = [[sbb("acc%d_%d" % (i, j), [128, 512]) for j in range(2)] for i in range(3)]
            b_acc = [[Buf(), Buf()] for _ in range(3)]
            wfiv = w_ffn_in.rearrange("(a p) n -> p a n", p=128)
            cnt = {"hb": 0, "w": 0, "u": 0, "yo": 0}

            def ffn_tile(subtiles, cgs):
                def b1a(si):
                    hrow, PT, oap, col0 = subtiles[si]
                    i3 = cnt["hb"] % 3; cnt["hb"] += 1
                    k2 = si % 2
                    dma("sp", hb[i3][0:PT, :], h1d[hrow:hrow + PT, :], [], [b_hb[i3]])
                    act(junk[0:PT, :], hb[i3][0:PT, :], AF.Square, [b_hb[i3]], [b_junk, b_ssB[k2]], accum_out=ssB[k2][0:PT, :])
                    rstd(rsB[k2][0:PT, :], ssB[k2][0:PT, :], 1024.0, PT, 1, [b_ssB[k2]], [b_rsB[k2]], tmB[k2][0:PT, :], b_tmB[k2])
                    stt(xn[k2][0:PT, :], hb[i3][0:PT, :], rsB[k2][0:PT, 0:1], gffn_bc[0:PT, :], ALU.mult, ALU.mult,
                        [b_hb[i3], b_rsB[k2], b_gbc], [b_xn[k2]])

                def b1b(si):
                    hrow, PT, oap, col0 = subtiles[si]
                    k2 = si % 2
                    bk, bb = nb()
                    bkb = bk[:].bitcast(BF16)
                    for dti in range(8):
                        tr(bkb[:, dti * PT:(dti + 1) * PT], xn[k2][0:PT, dti * 128:(dti + 1) * 128], identb[0:PT, 0:PT],
                           [b_xn[k2], b_identb], [bb])
                    cp("act", n2T[:, :, col0:col0 + PT], bkb[:, 0:8 * PT].rearrange("p (a t) -> p a t", a=8), [bb], [b_n2T])

                b1a(0)
                for si in range(len(subtiles)):
                    if si + 1 < len(subtiles):
                        b1a(si + 1)
                    b1b(si)
                def wload(cc):
                    wj = (cnt["w"] + cc) % 3
                    dma("pool", wst[wj][:, :, 0:128], wfiv[:, :, cc * 128:(cc + 1) * 128], [], [b_wst[wj]])
                    dma("pool", wst[wj][:, :, 128:256], wfiv[:, :, DFF + cc * 128:DFF + (cc + 1) * 128], [], [b_wst[wj]])
                wload(0); wload(1)
                pend_fin = [None]
                for c in range(22):
                    wi = (cnt["w"] + c) % 3
                    if c + 2 < 22:
                        wload(c + 2)
                    if wfo_pending:
                        a_ = wfo_pending.pop(0)
                        dma("pool", wfo[:, a_, :], wfov[:, a_, :], [], [b_wfo])
                    for (c0, ncol, smp) in cgs:
                        ui = cnt["u"] % 3; uu = cnt["u"] % 2; cnt["u"] += 1
                        bks = [nb(), nb()]
                        for gv in range(2):
                            for dti in range(8):
                                mm(bks[gv][0][:, 0:ncol], wst[wi][:, dti, gv * 128:(gv + 1) * 128], n2T[:, dti, c0:c0 + ncol],
                                   dti == 0, dti == 7, [b_wst[wi], b_n2T], [bks[gv][1]])
                        for gv in range(2):
                            ci = gv * 22 + c
                            bk, bb = bks[gv]
                            U = upS[uu][gv]; bU = b_upS[uu][gv]; A = acc[ui][gv]; bA = b_acc[ui][gv]
                            w0 = cw[:, 0, ci:ci + 1]; w1 = cw[:, 1, ci:ci + 1]; w2 = cw[:, 2, ci:ci + 1]; bs = cw[:, 3, ci:ci + 1]
                            if F_CONVOLD:
                                b_h1 = b_hist[0]
                                if not smp:
                                    cp("pool", U[:, 0:2], hist[:, :, ci], [b_h1], [bU])
                                    cp("act", U[:, 2:2 + ncol], bk[:, 0:ncol], [bb], [bU])
                                    cp("pool", hist[:, :, ci], U[:, ncol:ncol + 2], [bU], [b_h1])
                                    act(A[:, 0:ncol], bk[:, 0:ncol], AF.Identity, [bb, b_cw], [bA], scale=w2, bias=bs)
                                    stt(A[:, 0:ncol], U[:, 1:1 + ncol], w1, A[:, 0:ncol], ALU.mult, ALU.add, [bU, b_cw, bA], [bA])
                                    stt(A[:, 0:ncol], U[:, 0:ncol], w0, A[:, 0:ncol], ALU.mult, ALU.add, [bU, b_cw, bA], [bA])
                                else:
                                    U3 = U[:, 0:96].rearrange("p (b t) -> p b t", t=6)
                                    A3 = A[:, 0:64].rearrange("p (b t) -> p b t", t=4)
                                    cp("pool", U3[:, :, 0:2], hist_s[:, ci, :].rearrange("p (b k) -> p b k", k=2), [b_hists], [bU])
                                    cp("act", U3[:, :, 2:6], bk[:, 0:64].rearrange("p (b t) -> p b t", t=4), [bb], [bU])
                                    cp("pool", cst_s[:, ci, :].rearrange("p (b k) -> p b k", k=2), U3[:, :, 4:6], [bU], [b_csts])
                                    act(A3, bk[:, 0:64].rearrange("p (b t) -> p b t", t=4), AF.Identity, [bb, b_cw], [bA], scale=w2, bias=bs)
                                    stt(A3, U3[:, :, 1:5], w1, A3, ALU.mult, ALU.add, [bU, b_cw, bA], [bA])
                                    stt(A3, U3[:, :, 0:4], w0, A3, ALU.mult, ALU.add, [bU, b_cw, bA], [bA])
                                continue
                            bH = b_upH[uu][gv]
                            if (not smp) and gv == 1:
                                hsv = hsave[ui]; bhs = b_hsave[ui]
                                cp("act", hsv[:, 0:2], hist[:, :, ci], [b_hist[ci]], [bhs])
                                cp("act", hist[:, :, ci], bk[:, ncol - 2:ncol], [bb], [b_hist[ci]])
                                act(A[:, 0:ncol], bk[:, 0:ncol], AF.Identity, [bb, b_cw], [bA], scale=w2, bias=bs)
                                stt(A[:, 1:ncol], bk[:, 0:ncol - 1], w1, A[:, 1:ncol], ALU.mult, ALU.add, [bb, b_cw, bA], [bA])
                                stt(A[:, 2:ncol], bk[:, 0:ncol - 2], w0, A[:, 2:ncol], ALU.mult, ALU.add, [bb, b_cw, bA], [bA])
                                stt(A[:, 0:1], hsv[:, 1:2], w1, A[:, 0:1], ALU.mult, ALU.add, [bhs, b_cw, bA], [bA])
                                stt(A[:, 0:2], hsv[:, 0:2], w0, A[:, 0:2], ALU.mult, ALU.add, [bhs, b_cw, bA], [bA])
                            elif not smp:
                                cp("act", U[:, 0:2], hist[:, :, ci], [b_hist[ci]], [bH])
                                cp("act", U[:, 2:2 + ncol], bk[:, 0:ncol], [bb], [bU])
                                cp("act", hist[:, :, ci], bk[:, ncol - 2:ncol], [bb], [b_hist[ci]])
                                act(A[:, 0:ncol], bk[:, 0:ncol], AF.Identity, [bb, b_cw], [bA], scale=w2, bias=bs)
                                stt(A[:, 0:ncol], U[:, 1:1 + ncol], w1, A[:, 0:ncol], ALU.mult, ALU.add, [bU, bH, b_cw, bA], [bA])
                                stt(A[:, 0:ncol], U[:, 0:ncol], w0, A[:, 0:ncol], ALU.mult, ALU.add, [bU, bH, b_cw, bA], [bA])
                            else:
                                U3 = U[:, 0:96].rearrange("p (b t) -> p b t", t=6)
                                A3 = A[:, 0:64].rearrange("p (b t) -> p b t", t=4)
                                bk3 = bk[:, 0:64].rearrange("p (b t) -> p b t", t=4)
                                cp("act", U3[:, :, 0:2], hist_s[:, ci, :].rearrange("p (b k) -> p b k", k=2), [b_hists], [bH])
                                cp("act", U3[:, :, 2:6], bk3, [bb], [bU])
                                cp("act", cst_s[:, ci, :].rearrange("p (b k) -> p b k", k=2), bk3[:, :, 2:4], [bb], [b_csts])
                                act(A3, bk3, AF.Identity, [bb, b_cw], [bA], scale=w2, bias=bs)
                                stt(A3, U3[:, :, 1:5], w1, A3, ALU.mult, ALU.add, [bU, bH, b_cw, bA], [bA])
                                stt(A3, U3[:, :, 0:4], w0, A3, ALU.mult, ALU.add, [bU, bH, b_cw, bA], [bA])
                        def fin(ui=ui, c=c, c0=c0, ncol=ncol):
                            act(acc[ui][0][:, 0:ncol], acc[ui][0][:, 0:ncol], AF.Gelu_apprx_tanh, [b_acc[ui][0]], [b_acc[ui][0]])
                            tt("pool", zt[:, c, c0:c0 + ncol], acc[ui][0][:, 0:ncol], acc[ui][1][:, 0:ncol], ALU.mult,
                               [b_acc[ui][0], b_acc[ui][1]], [b_z])
                        if pend_fin[0] is not None:
                            pend_fin[0]()
                        pend_fin[0] = fin
                if pend_fin[0] is not None:
                    pend_fin[0]()
                    pend_fin[0] = None
                cnt["w"] += 22
                for si, (hrow, PT, oap, col0) in enumerate(subtiles):
                    i3 = cnt["hb"] % 3; cnt["hb"] += 1
                    k2 = si % 2
                    dma("sp", hb[i3][0:PT, :], h1d[hrow:hrow + PT, :], [], [b_hb[i3]])
                    bks = [nb(), nb()]
                    for nh in range(2):
                        for c in range(22):
                            mm(bks[nh][0][0:PT, :], zt[:, c, col0:col0 + PT], wfo[:, c, nh * 512:(nh + 1) * 512], c == 0, c == 21,
                               [b_z, b_wfo], [bks[nh][1]])
                    for nh in range(2):
                        tt("dve", hb[i3][0:PT, nh * 512:(nh + 1) * 512], hb[i3][0:PT, nh * 512:(nh + 1) * 512], bks[nh][0][0:PT, :], ALU.add,
                           [b_hb[i3], bks[nh][1]], [b_hb[i3]])
                    act(junk[0:PT, :], hb[i3][0:PT, :], AF.Square, [b_hb[i3]], [b_junk, b_ssB[k2]], accum_out=ssB[k2][0:PT, :])
                    rstd(rsB[k2][0:PT, :], ssB[k2][0:PT, :], 1024.0, PT, 1, [b_ssB[k2]], [b_rsB[k2]], tmB[k2][0:PT, :], b_tmB[k2])
                    stt(hb[i3][0:PT, :], hb[i3][0:PT, :], rsB[k2][0:PT, 0:1], gfin_bc[0:PT, :], ALU.mult, ALU.mult,
                        [b_hb[i3], b_rsB[k2], b_gbc], [b_hb[i3]])
                    dma("sp", oap, hb[i3][0:PT, :], [b_hb[i3]], [], is_output=True)

            subA = [(128 * s, 128, yp[128 * s:128 * (s + 1), :], 128 * s) for s in range(8)]
            ffn_tile(subA, [(0, 512, False), (512, 512, False)])
            subB = [(1024 + 128 * s, 128, yp[1024 + 128 * s:1024 + 128 * (s + 1), :], 128 * s) for s in range(8)]
            subB.append((2048, 64, ys, 1024))
            ffn_tile(subB, [(0, 512, False), (512, 512, False), (1024, 64, True)])
            cpo = sbb("cpo", [88, 128]); b_cpo = Buf()
            bk, bb = nb()
            tr(bk[0:88, 0:128], hist[:].rearrange("p k c -> p (k c)"), identf[:], b_hist + [b_identf], [bb])
            cp("act", cpo[:], bk[0:88, 0:128], [bb], [b_cpo])
            dma("sp", cvp[0].rearrange("(c p) -> c p", p=128), cpo[0:44, :], [b_cpo], [], is_output=True)
            dma("sp", cvp[1].rearrange("(c p) -> c p", p=128), cpo[44:88, :], [b_cpo], [], is_output=True)
            cso = sbb("cso", [128, 11, 128]); b_cso = Buf()
            for q3 in range(3):
                bk, bb = nb()
                nblk = 4 if q3 < 2 else 3
                for bl in range(nblk):
                    cbk = q3 * 4 + bl
                    tr(bk[:, bl * 128:(bl + 1) * 128], cst_s[:, cbk * 4:(cbk + 1) * 4, :].rearrange("p c r -> p (c r)"), identf[:],
                       [b_csts, b_identf], [bb])
                cp("act", cso[:, q3 * 4:q3 * 4 + nblk, :], bk[:, 0:nblk * 128].rearrange("p (a c) -> p a c", a=nblk), [bb], [b_cso])
            cvsv = cvs.rearrange("r (cb cl p) -> cl r cb p", cl=4, p=128)
            for cl in range(4):
                dma("sp", cvsv[cl], cso[32 * cl:32 * cl + 32, :, :], [b_cso], [], is_output=True)
        P.build()
    return nc


_CACHE = {}


def _consts():
    ident = np.eye(128, dtype=np.float32)
    ii = np.arange(128) // 16
    mtoep = (ii[None, :] >= ii[:, None]).astype(np.float32)
    ar = np.arange(128)
    msgu = (ar[:, None] <= ar[None, :]).astype(np.float32)
    nv = np.zeros(32, np.float32)
    nv[0:8] = -np.arange(8)
    nv[8:16] = 7 - np.arange(8)
    nv[16:25] = np.arange(9)
    nv[25] = 8; nv[26] = 4; nv[27] = 1
    nvec = np.tile(nv[None, :], (128, 1)).astype(np.float32)
    sigma = np.ones((128, 1), np.float32)
    sigma[64:] = -1.0
    kvec = np.tile(np.arange(1, 65, dtype=np.float32)[None, :], (128, 1))
    return dict(c_ident=ident, c_mtoep=mtoep, c_msgu=msgu, c_nvec=nvec, c_sigma=sigma, c_kvec=kvec)


def kernel(x_prompt, x_sample, state_ssm_re, state_ssm_im, state_conv,
           g_mix, w_in, g_v, w_s, b_s, lam_re, lam_im, log_dt, b_re, b_im, c_re, c_im,
           d_skip, w_glu, b_glu, g_out_a, g_out_b, w_out, g_ffn, w_ffn_in, conv_w, conv_b,
           w_ffn_out, g_final):
    f = lambda a: np.ascontiguousarray(np.asarray(a, dtype=np.float32))
    if "nc" not in _CACHE:
        _CACHE["nc"] = build_nc()
    nc = _CACHE["nc"]
    shared = dict(
        g_mix=f(g_mix), w_in=f(w_in), g_v=f(g_v).reshape(512), w_s=f(w_s), b_s=f(b_s),
        lam_re=f(lam_re), lam_im=f(lam_im), log_dt=f(log_dt), b_re=f(b_re), b_im=f(b_im),
        c_re=f(c_re).reshape(512, 64), c_im=f(c_im).reshape(512, 64), d_skip=f(d_skip),
        w_glu=f(w_glu), b_glu=f(b_glu), g_out_a=f(g_out_a), g_out_b=f(g_out_b), w_out=f(w_out),
        g_ffn=f(g_ffn), w_ffn_in=f(w_ffn_in), conv_w=f(conv_w), conv_b=f(conv_b),
        w_ffn_out=f(w_ffn_out), g_final=f(g_final))
    shared.update(_consts())
    xpf = f(x_prompt); xsf = f(x_sample); sr = f(state_ssm_re); si = f(state_ssm_im); sc = f(state_conv)
    in_maps = []
    for c in range(NCORES):
        m = dict(shared)
        m["xp"] = xpf[c]
        m["xs"] = xsf[16 * c:16 * c + 16].reshape(64, D)
        m["sre"] = sr[16 * c:16 * c + 16]
        m["sim"] = si[16 * c:16 * c + 16]
        m["scv"] = sc[16 * c:16 * c + 16].reshape(32, 2 * DFF)
        in_maps.append(m)
    res = run_bass_kernel_spmd(nc, in_maps, core_ids=list(range(NCORES)))
    R = res.results
    y_prompt = np.stack([R[c]["yp"] for c in range(NCORES)], 0)
    y_sample = np.concatenate([R[c]["ys"].reshape(16, 4, D) for c in range(NCORES)], 0)
    v_sample = np.concatenate([R[c]["vs"].reshape(16, 4, 4, 128) for c in range(NCORES)], 0)
    srp = np.stack([R[c]["srep"] for c in range(NCORES)], 0)
    sip = np.stack([R[c]["simp"] for c in range(NCORES)], 0)
    cvp = np.stack([R[c]["cvp"] for c in range(NCORES)], 0)
    srs = np.concatenate([R[c]["sres"] for c in range(NCORES)], 0)
    sis = np.concatenate([R[c]["sims"] for c in range(NCORES)], 0)
    cvs = np.concatenate([R[c]["cvs"].reshape(16, 2, 2 * DFF) for c in range(NCORES)], 0)
    out = (y_prompt, y_sample, v_sample, srp, sip, cvp, srs, sis, cvs)
    return tuple(np.ascontiguousarray(o, dtype=np.float32) for o in out)
```
